# Optimizing a Trainium2 kernel written in Bass

```python
import math
import jax, jax.numpy as jnp
from jax import lax
import numpy as np

D_MODEL = 1024
BATCH = 8
SEQ = 4096
DEPTH = 4

CHUNK = 64
Q_BLOCK = 128
NORM_EPS = 1e-6
N_MIXERS = 2
N_GDN_LAYERS = (DEPTH + 1) // 2
N_MLA_LAYERS = DEPTH // 2

GDN_K_HEADS = D_MODEL // 128
GDN_V_HEADS = 2 * GDN_K_HEADS
GDN_HEAD_K = 128
GDN_HEAD_V = 128
GDN_CONV = 4
GDN_KEY_DIM = GDN_K_HEADS * GDN_HEAD_K
GDN_VAL_DIM = GDN_V_HEADS * GDN_HEAD_V
GDN_CONV_DIM = 2 * GDN_KEY_DIM + GDN_VAL_DIM
GDN_IN_DIM = GDN_CONV_DIM + GDN_VAL_DIM + 2 * GDN_V_HEADS

MLA_HEADS = D_MODEL // 128
MLA_Q_RANK = 3 * D_MODEL // 8
MLA_KV_RANK = D_MODEL // 4
MLA_NOPE = 128
MLA_ROPE = 64
MLA_V = 128
MLA_IN_DIM = MLA_Q_RANK + MLA_KV_RANK + MLA_ROPE
ROPE_BASE = 10000.0

D_FF = -(-8 * D_MODEL // (3 * 256)) * 256

kernel_name = "hybrid_gdn_mla_chunk_causal_trunk"


def rms_norm(x, gain):
    xf = x.astype(jnp.float32)
    y = xf * lax.rsqrt(jnp.mean(xf * xf, axis=-1, keepdims=True) + NORM_EPS)
    return (y * gain.astype(jnp.float32)).astype(x.dtype)


def l2_norm(x):
    return x * lax.rsqrt(jnp.sum(x * x, axis=-1, keepdims=True) + 1e-6)


def causal_dwconv(x, w):
    k = w.shape[0]
    return lax.conv_general_dilated(
        x, w[:, None, :].astype(x.dtype), window_strides=(1,), padding=[(k - 1, 0)],
        dimension_numbers=("NWC", "WIO", "NWC"), feature_group_count=x.shape[-1])


def gated_delta_rule(q, k, v, g, beta):
    bsz, nh, seq, dk = k.shape
    n_chunks = seq // CHUNK
    q = q * (dk ** -0.5)
    rs = lambda t: t.reshape(bsz, nh, n_chunks, CHUNK, *t.shape[3:])
    q, k, v, g, beta = rs(q), rs(k), rs(v), rs(g), rs(beta)
    g = jnp.cumsum(g, axis=-1)
    tril = jnp.tril(jnp.ones((CHUNK, CHUNK), bool))
    strict = jnp.tril(jnp.ones((CHUNK, CHUNK), bool), k=-1)
    diff = g[..., :, None] - g[..., None, :]
    decay = jnp.where(tril, jnp.exp(jnp.where(tril, diff, 0.0)), 0.0)
    k_beta = k * beta[..., None]
    v_beta = v * beta[..., None]
    a = jnp.where(strict, jnp.einsum("bhncd,bhnjd->bhncj", k_beta, k) * decay, 0.0)
    eye = jnp.eye(CHUNK, dtype=jnp.float32)
    t_inv = lax.linalg.triangular_solve(eye + a, jnp.broadcast_to(eye, a.shape),
                                        left_side=True, lower=True)
    u = jnp.einsum("bhncj,bhnjv->bhncv", t_inv, v_beta)
    w = jnp.einsum("bhncj,bhnjk->bhnck", t_inv, k_beta * jnp.exp(g)[..., None])
    qk = jnp.einsum("bhncd,bhnjd->bhncj", q, k) * decay
    g_last = g[..., -1]
    q_dec = q * jnp.exp(g)[..., None]
    k_dec = k * jnp.exp(g_last[..., None] - g)[..., None]

    def step(state, inp):
        qk_c, u_c, w_c, qd_c, kd_c, gl_c = inp
        v_new = u_c - jnp.einsum("bhck,bhkv->bhcv", w_c, state)
        out = (jnp.einsum("bhck,bhkv->bhcv", qd_c, state)
               + jnp.einsum("bhcj,bhjv->bhcv", qk_c, v_new))
        state = (state * jnp.exp(gl_c)[..., None, None]
                 + jnp.einsum("bhck,bhcv->bhkv", kd_c, v_new))
        return state, out

    mv = lambda t: jnp.moveaxis(t, 2, 0)
    state0 = jnp.zeros((bsz, nh, dk, v.shape[-1]), jnp.float32)
    _, out = lax.scan(step, state0, (mv(qk), mv(u), mv(w), mv(q_dec), mv(k_dec), mv(g_last)))
    return jnp.moveaxis(out, 0, 2).reshape(bsz, nh, seq, v.shape[-1])


def gdn_mixer(h, w_in, conv_w, a_log, dt_bias, out_gain, w_out):
    bsz, seq, _ = h.shape
    proj = h @ w_in
    qkv = proj[..., :GDN_CONV_DIM]
    z = proj[..., GDN_CONV_DIM:GDN_CONV_DIM + GDN_VAL_DIM]
    b = proj[..., GDN_CONV_DIM + GDN_VAL_DIM:GDN_CONV_DIM + GDN_VAL_DIM + GDN_V_HEADS]
    a = proj[..., GDN_CONV_DIM + GDN_VAL_DIM + GDN_V_HEADS:]
    qkv = jax.nn.silu(causal_dwconv(qkv, conv_w))
    q = qkv[..., :GDN_KEY_DIM].reshape(bsz, seq, GDN_K_HEADS, GDN_HEAD_K).astype(jnp.float32)
    k = qkv[..., GDN_KEY_DIM:2 * GDN_KEY_DIM].reshape(bsz, seq, GDN_K_HEADS, GDN_HEAD_K).astype(jnp.float32)
    v = qkv[..., 2 * GDN_KEY_DIM:].reshape(bsz, seq, GDN_V_HEADS, GDN_HEAD_V).astype(jnp.float32)
    rep = GDN_V_HEADS // GDN_K_HEADS
    q = jnp.repeat(l2_norm(q), rep, axis=2)
    k = jnp.repeat(l2_norm(k), rep, axis=2)
    beta = jax.nn.sigmoid(b.astype(jnp.float32))
    g = -jnp.exp(a_log.astype(jnp.float32)) * jax.nn.softplus(
        a.astype(jnp.float32) + dt_bias.astype(jnp.float32))
    tr = lambda t: jnp.swapaxes(t, 1, 2)
    o = gated_delta_rule(tr(q), tr(k), tr(v), tr(g), tr(beta))
    o = jnp.swapaxes(o, 1, 2)
    zf = z.reshape(bsz, seq, GDN_V_HEADS, GDN_HEAD_V).astype(jnp.float32)
    o = (o * lax.rsqrt(jnp.mean(o * o, axis=-1, keepdims=True) + NORM_EPS)
         * out_gain.astype(jnp.float32) * jax.nn.silu(zf))
    return o.reshape(bsz, seq, GDN_VAL_DIM).astype(h.dtype) @ w_out


def rope_cos_sin(positions):
    inv_freq = ROPE_BASE ** (-jnp.arange(0, MLA_ROPE, 2, dtype=jnp.float32) / MLA_ROPE)
    ang = positions.astype(jnp.float32)[..., None] * inv_freq
    return jnp.cos(ang), jnp.sin(ang)


def apply_rope(x, cos, sin):
    xf = x.astype(jnp.float32).reshape(*x.shape[:-1], -1, 2)
    x1, x2 = xf[..., 0], xf[..., 1]
    out = jnp.stack([x1 * cos - x2 * sin, x1 * sin + x2 * cos], axis=-1)
    return out.reshape(x.shape).astype(x.dtype)


def mla_mixer(h, positions, w_in, q_norm_g, w_q_up, kv_norm_g, w_kv_up, w_out):
    bsz, seq, _ = h.shape
    proj = h @ w_in
    c_q = proj[..., :MLA_Q_RANK]
    c_kv = proj[..., MLA_Q_RANK:MLA_Q_RANK + MLA_KV_RANK]
    k_rope = proj[..., MLA_Q_RANK + MLA_KV_RANK:]
    q = (rms_norm(c_q, q_norm_g) @ w_q_up).reshape(bsz, seq, MLA_HEADS, MLA_NOPE + MLA_ROPE)
    kv = (rms_norm(c_kv, kv_norm_g) @ w_kv_up).reshape(bsz, seq, MLA_HEADS, MLA_NOPE + MLA_V)
    q_nope, q_rope = q[..., :MLA_NOPE], q[..., MLA_NOPE:]
    k_nope, v = kv[..., :MLA_NOPE], kv[..., MLA_NOPE:]
    cos, sin = rope_cos_sin(positions)
    q_rope = apply_rope(q_rope, cos[:, :, None, :], sin[:, :, None, :])
    k_rope = apply_rope(k_rope, cos, sin)
    scale = (MLA_NOPE + MLA_ROPE) ** -0.5
    outs = []
    for blk in range(seq // Q_BLOCK):
        q0 = blk * Q_BLOCK
        q1 = q0 + Q_BLOCK
        s = (jnp.einsum("bqhd,bkhd->bhqk", q_nope[:, q0:q1], k_nope[:, :q1])
             + jnp.einsum("bqhr,bkr->bhqk", q_rope[:, q0:q1], k_rope[:, :q1]))
        s = s.astype(jnp.float32) * scale
        q_chunk = jnp.arange(q0, q1) // CHUNK
        k_chunk = jnp.arange(q1) // CHUNK
        mask = k_chunk[None, :] <= q_chunk[:, None]
        p = jax.nn.softmax(jnp.where(mask, s, -jnp.inf), axis=-1).astype(v.dtype)
        outs.append(jnp.einsum("bhqk,bkhd->bqhd", p, v[:, :q1]))
    o = jnp.concatenate(outs, axis=1).reshape(bsz, seq, MLA_HEADS * MLA_V)
    return o @ w_out


def swiglu(h, w_gate_up, w_down):
    gu = h @ w_gate_up
    return (jax.nn.silu(gu[..., :D_FF]) * gu[..., D_FF:]) @ w_down


def setup_inputs(seed: int = 0) -> dict:
    key = jax.random.key(seed)
    ks = jax.random.split(key, 24)
    f32 = jnp.float32

    def dense(k, shape, fan_in):
        return jax.random.normal(k, shape, f32) * fan_in ** -0.5

    def gain(k, shape):
        return 1.0 + 0.02 * jax.random.normal(k, shape, f32)

    x = jax.random.normal(ks[0], (BATCH, SEQ, D_MODEL), f32)
    offsets = jax.random.randint(ks[1], (BATCH, 1), 0, 128) * CHUNK
    positions = (offsets + jnp.arange(SEQ, dtype=jnp.int32)[None, :]).astype(jnp.int32)
    dt = jnp.exp(jax.random.uniform(ks[7], (N_GDN_LAYERS, GDN_V_HEADS), f32,
                                    math.log(1e-3), math.log(1e-1)))
    return {
        "x": x,
        "positions": positions,
        "norm_mix": gain(ks[2], (DEPTH, D_MODEL)),
        "norm_ffn": gain(ks[3], (DEPTH, D_MODEL)),
        "gdn_w_in": dense(ks[4], (N_GDN_LAYERS, D_MODEL, GDN_IN_DIM), D_MODEL),
        "gdn_conv_w": dense(ks[5], (N_GDN_LAYERS, GDN_CONV, GDN_CONV_DIM), GDN_CONV),
        "gdn_a_log": jnp.log(jax.random.uniform(ks[6], (N_GDN_LAYERS, GDN_V_HEADS), f32, 1.0, 16.0)),
        "gdn_dt_bias": dt + jnp.log(-jnp.expm1(-dt)),
        "gdn_out_norm": gain(ks[8], (N_GDN_LAYERS, GDN_HEAD_V)),
        "gdn_w_out": dense(ks[9], (N_GDN_LAYERS, GDN_VAL_DIM, D_MODEL), GDN_VAL_DIM),
        "mla_w_in": dense(ks[10], (N_MLA_LAYERS, D_MODEL, MLA_IN_DIM), D_MODEL),
        "mla_q_norm": gain(ks[11], (N_MLA_LAYERS, MLA_Q_RANK)),
        "mla_w_q_up": dense(ks[12], (N_MLA_LAYERS, MLA_Q_RANK, MLA_HEADS * (MLA_NOPE + MLA_ROPE)), MLA_Q_RANK),
        "mla_kv_norm": gain(ks[13], (N_MLA_LAYERS, MLA_KV_RANK)),
        "mla_w_kv_up": dense(ks[14], (N_MLA_LAYERS, MLA_KV_RANK, MLA_HEADS * (MLA_NOPE + MLA_V)), MLA_KV_RANK),
        "mla_w_out": dense(ks[15], (N_MLA_LAYERS, MLA_HEADS * MLA_V, D_MODEL), MLA_HEADS * MLA_V),
        "ffn_w_gate_up": dense(ks[16], (DEPTH, D_MODEL, 2 * D_FF), D_MODEL),
        "ffn_w_down": dense(ks[17], (DEPTH, D_FF, D_MODEL), D_FF),
        "final_norm": gain(ks[18], (D_MODEL,)),
    }


def reference(x, positions, norm_mix, norm_ffn, gdn_w_in, gdn_conv_w, gdn_a_log, gdn_dt_bias,
              gdn_out_norm, gdn_w_out, mla_w_in, mla_q_norm, mla_w_q_up, mla_kv_norm,
              mla_w_kv_up, mla_w_out, ffn_w_gate_up, ffn_w_down, final_norm):
    for i in range(DEPTH):
        h = rms_norm(x, norm_mix[i])
        j = i // N_MIXERS
        if i % N_MIXERS == 0:
            mix = gdn_mixer(h, gdn_w_in[j], gdn_conv_w[j], gdn_a_log[j], gdn_dt_bias[j],
                            gdn_out_norm[j], gdn_w_out[j])
        else:
            mix = mla_mixer(h, positions, mla_w_in[j], mla_q_norm[j], mla_w_q_up[j],
                            mla_kv_norm[j], mla_w_kv_up[j], mla_w_out[j])
        x = x + mix.astype(x.dtype)
        x = x + swiglu(rms_norm(x, norm_ffn[i]), ffn_w_gate_up[i], ffn_w_down[i]).astype(x.dtype)
    return rms_norm(x, final_norm)
```

```python
import contextlib
import math
import numpy as np
import concourse.bass as bass
import concourse.mybir as mybir
from concourse.alu_op_type import AluOpType as ALU
from concourse.bass_utils import run_bass_kernel_spmd

AF = mybir.ActivationFunctionType
F32 = mybir.dt.float32
BF16 = mybir.dt.bfloat16
I32 = mybir.dt.int32

SEM_LIM = 30000
N_DMA_SEMS = 24

S = 4096
D = 1024
NT = S // 128
KC = D // 128
DEPTH = 4
EPS = 1e-6
D_FF = 2816
FC = D_FF // 128
GDN_IN = 6176
MLA_IN = 704


class Res:
    __slots__ = ("last_w", "readers")

    def __init__(self):
        self.last_w = None
        self.readers = []


class Op:
    __slots__ = ("eng", "fn", "deps", "is_dma", "signal", "sem", "val", "dsem", "dprev")

    def __init__(self, eng, fn, is_dma):
        self.eng = eng
        self.fn = fn
        self.is_dma = is_dma
        self.deps = ()
        self.signal = False
        self.sem = None
        self.val = 0
        self.dsem = None
        self.dprev = 0


class Prog:
    ENGS = ("pe", "act", "dve", "pool", "sp")

    def __init__(self, nc):
        self.nc = nc
        self.ops = {e: [] for e in self.ENGS}
        self.stack = contextlib.ExitStack()
        self.dma_cnt = [0] * N_DMA_SEMS
        self.dma_last = [None] * N_DMA_SEMS
        self.dma_rr = 0
        self.uid = 0

    def sbuf(self, name, shape, dtype, stack=None):
        self.uid += 1
        return (stack or self.stack).enter_context(
            self.nc.sbuf_tensor(f"{name}_{self.uid}", list(shape), dtype))

    def psum(self, name, shape, dtype):
        return self.stack.enter_context(self.nc.psum_tensor(name, list(shape), dtype))

    def op(self, eng, fn, reads=(), writes=(), dma=False, extra_deps=()):
        o = Op(eng, fn, dma)
        deps = set(extra_deps)
        for r in reads:
            if r.last_w is not None:
                deps.add(r.last_w)
        for r in writes:
            if r.last_w is not None:
                deps.add(r.last_w)
            deps.update(r.readers)
        for r in reads:
            r.readers.append(o)
        for r in writes:
            r.last_w = o
            r.readers = []
        deps.discard(o)
        if eng == "pe":
            deps = {d for d in deps if d.is_dma or d.eng != "pe"}
        o.deps = deps
        for d in deps:
            d.signal = True
        if dma:
            k = self.dma_rr
            self.dma_rr = (self.dma_rr + 1) % N_DMA_SEMS
            o.dprev = self.dma_cnt[k] * 16
            self.dma_cnt[k] += 1
            o.dsem = k
            o.val = self.dma_cnt[k] * 16
            self.dma_last[k] = o
        self.ops[eng].append(o)
        return o

    def dma(self, out, in_, reads=(), writes=(), eng="sp", **kw):
        return self.op(eng, lambda e: e.dma_start(out=out, in_=in_, **kw), reads, writes, dma=True)

    def barrier(self):
        lasts = []
        for e in self.ENGS:
            for o in reversed(self.ops[e]):
                if not o.is_dma and o.fn is not None:
                    lasts.append(o)
                    break
        lasts += [o for o in self.dma_last if o is not None]
        for e in self.ENGS:
            self.op(e, None, extra_deps=lasts)

    def emit(self, final_wait_ops=()):
        nc = self.nc
        st = self.stack
        nsem = {}
        for e in self.ENGS:
            cnt = 0
            for o in self.ops[e]:
                if o.is_dma or o.fn is None:
                    continue
                if o.signal:
                    o.sem = (e, cnt // SEM_LIM)
                    o.val = cnt % SEM_LIM + 1
                    cnt += 1
            nsem[e] = (cnt + SEM_LIM - 1) // SEM_LIM
        sems = {}
        for e in self.ENGS:
            for k in range(nsem[e]):
                sems[(e, k)] = st.enter_context(nc.semaphore(f"s_{e}{k}"))
        dsems = [st.enter_context(nc.semaphore(f"s_dma{k}")) for k in range(N_DMA_SEMS)]
        block = st.enter_context(nc.Block())
        battr = {"pe": "tensor", "act": "scalar", "dve": "vector", "pool": "gpsimd", "sp": "sync"}

        def run_engine(ename, eng):
            waited = {}
            for o in self.ops[ename]:
                need = {}
                for d in o.deps:
                    if d.is_dma:
                        key = ("d", d.dsem)
                        s = dsems[d.dsem]
                    elif d.fn is None:
                        continue
                    else:
                        key = d.sem
                        s = sems[d.sem]
                    if need.get(key, (None, 0))[1] < d.val:
                        need[key] = (s, d.val)
                if o.is_dma and o.dprev > 0:
                    key = ("d", o.dsem)
                    if need.get(key, (None, 0))[1] < o.dprev:
                        need[key] = (dsems[o.dsem], o.dprev)
                for key, (s, v) in need.items():
                    if waited.get(key, 0) < v:
                        eng.wait_ge(s, v)
                        waited[key] = v
                if o.fn is None:
                    continue
                ins = o.fn(eng)
                if o.is_dma:
                    ins.then_inc(dsems[o.dsem], 16)
                elif o.signal:
                    ins.then_inc(sems[o.sem], 1)
            if ename == "sp":
                for o in final_wait_ops:
                    eng.wait_ge(dsems[o.dsem], o.val)

        for ename in self.ENGS:
            def mk(ename=ename):
                def body(eng):
                    run_engine(ename, eng)
                return body
            getattr(block, battr[ename])(mk())


class Ring:
    def __init__(self, P, name, shape, dtype, n, stack=None):
        self.t = [P.sbuf(f"{name}{i}", shape, dtype, stack) for i in range(n)]
        self.r = [Res() for _ in range(n)]
        self.i = 0
        self.n = n

    def next(self):
        k = self.i
        self.i = (self.i + 1) % self.n
        return self.t[k], self.r[k]


class K:
    pass


def dump(k, name, ap, reads, dt=BF16):
    if not getattr(k, "dbg", False):
        return
    shape = list(ap.shape)
    t = k.nc.dram_tensor("dbg_" + name, shape, F32, kind="ExternalOutput").ap()
    k.dbg_ops.append(k.P.dma(t, ap, reads=reads, eng="pool"))


def build(n_layers=DEPTH, do_mix=True, do_ffn=True, dbg=False):
    nc = bass.Bass("TRN2", target_bir_lowering=False)
    P = Prog(nc)
    k = K()
    k.nc, k.P = nc, P
    k.dbg = dbg
    k.dbg_ops = []

    def din(name, shape, dt=F32):
        return nc.dram_tensor(name, list(shape), dt, kind="ExternalInput").ap()

    k.x_in = din("x", [S, D])
    k.pos = din("positions", [1, S], I32)
    k.norm_mix = din("norm_mix", [DEPTH, D])
    k.norm_ffn = din("norm_ffn", [DEPTH, D])
    k.gdn_w_in = din("gdn_w_in", [2, D, GDN_IN])
    k.gdn_conv_w = din("gdn_conv_w", [2, 4, 4096])
    k.gdn_a_log = din("gdn_a_log", [2, 16])
    k.gdn_dt_bias = din("gdn_dt_bias", [2, 16])
    k.gdn_out_norm = din("gdn_out_norm", [2, 128])
    k.gdn_w_out = din("gdn_w_out", [2, 2048, D])
    k.mla_w_in = din("mla_w_in", [2, D, MLA_IN])
    k.mla_q_norm = din("mla_q_norm", [2, 384])
    k.mla_w_q_up = din("mla_w_q_up", [2, 384, 1536])
    k.mla_kv_norm = din("mla_kv_norm", [2, 256])
    k.mla_w_kv_up = din("mla_w_kv_up", [2, 256, 2048])
    k.mla_w_out = din("mla_w_out", [2, D, D])
    k.ffn_w_gate_up = din("ffn_w_gate_up", [DEPTH, D, 2 * D_FF])
    k.ffn_w_down = din("ffn_w_down", [DEPTH, D_FF, D])
    k.final_norm = din("final_norm", [1, D])
    k.invf = din("invf", [128, 2])
    k.gmask = din("gmask", [128, 8, 128])
    k.out = nc.dram_tensor("out", [S, D], F32, kind="ExternalOutput").ap()
    k.xres = nc.dram_tensor("xres", [S, D], F32, kind="Internal").ap()
    k.oT_d = nc.dram_tensor("oT_d", [2048, S], BF16, kind="Internal").ap()
    k.r_xres = [Res() for _ in range(NT)]
    k.xr = lambda src, i: [k.r_xres[i]] if src is k.xres else []

    with P.stack:
        setup_consts(k)
        x_src = k.x_in
        k.final_ops = []
        fuse = (do_mix is True) and do_ffn
        hT_ready = False
        for L in range(n_layers):
            if do_mix is True or (do_mix is not False and do_mix == L % 2):
                if not hT_ready:
                    rms_to_hT(k, x_src, k.gainT[:, L * KC:(L + 1) * KC])
                    P.barrier()
                hT_ready = False
                if L % 2 == 0:
                    nn = ("hT", k.gainT[:, (4 + L) * KC:(5 + L) * KC]) if fuse else None
                    gdn_layer(k, L // 2, x_src, nn)
                    hT_ready = nn is not None
                else:
                    mla_layer(k, L // 2, x_src)
                P.barrier()
                x_src = k.xres
            if do_ffn:
                if not hT_ready:
                    rms_to_hT(k, x_src, k.gainT[:, (4 + L) * KC:(5 + L) * KC])
                    P.barrier()
                hT_ready = False
                nn = None
                if fuse:
                    nn = ("final",) if L == n_layers - 1 else ("hT", k.gainT[:, (L + 1) * KC:(L + 2) * KC])
                ffn_layer(k, L, x_src, nn)
                hT_ready = nn is not None and nn[0] == "hT"
                P.barrier()
                x_src = k.xres
        if fuse:
            outs = k.final_ops
        else:
            outs = final_norm(k, x_src)
        P.emit(final_wait_ops=outs + k.dbg_ops)
    return nc


def setup_consts(k):
    P, nc = k.P, k.nc
    k.ps = [P.psum(f"ps{i}", [128, 512], F32) for i in range(8)]
    k.rps = [Res() for _ in range(8)]
    k.bank_rr = {}
    k.identf = P.sbuf("identf", [128, 128], F32)
    k.ident = P.sbuf("ident", [128, 128], BF16)
    k.onesf = P.sbuf("onesf", [128, 128], F32)
    k.negonesf = P.sbuf("negonesf", [128, 128], F32)
    k.onesb = P.sbuf("onesb", [128, 128], BF16)
    k.r_const = Res()
    rc = k.r_const
    P.op("pool", lambda e: e.memset(k.identf[:], 0.0), writes=[rc])
    P.op("pool", lambda e: e.affine_select(out=k.identf[:], in_=k.identf[:], pattern=[[1, 128]],
                                           compare_op=ALU.not_equal, fill=1.0, base=0,
                                           channel_multiplier=-1), writes=[rc])
    P.op("dve", lambda e: e.tensor_copy(out=k.ident[:], in_=k.identf[:]), reads=[rc], writes=[rc])
    P.op("pool", lambda e: e.memset(k.onesf[:], 1.0), writes=[rc])
    P.op("pool", lambda e: e.memset(k.negonesf[:], -1.0), writes=[rc])
    P.op("pool", lambda e: e.memset(k.onesb[:], 1.0), writes=[rc])
    k.gainT = P.sbuf("gainT", [128, 74], F32)
    g_raw = P.sbuf("g_raw", [74, 128], F32)
    r_g = Res()
    P.dma(g_raw[0:32, :], k.norm_mix.rearrange("r (c p) -> (r c) p", p=128), writes=[r_g])
    P.dma(g_raw[32:64, :], k.norm_ffn.rearrange("r (c p) -> (r c) p", p=128), writes=[r_g])
    P.dma(g_raw[64:70, :], k.mla_q_norm.rearrange("r (c p) -> (r c) p", p=128), writes=[r_g])
    P.dma(g_raw[70:74, :], k.mla_kv_norm.rearrange("r (c p) -> (r c) p", p=128), writes=[r_g])
    P.op("pe", lambda e: e.transpose(out=k.ps[0][:, 0:74], in_=g_raw[:, :], identity=k.identf[0:74, 0:74]),
         reads=[r_g, rc], writes=[k.rps[0]])
    P.op("dve", lambda e: e.tensor_copy(out=k.gainT[:], in_=k.ps[0][:, 0:74]), writes=[k.rps[0], rc])
    k.bigA = P.sbuf("bigA", [128, KC, S], BF16)
    k.r_big = [Res() for _ in range(S // 512)]
    k.m05 = P.sbuf("m05", [128, 1], F32)
    P.op("pool", lambda e: e.memset(k.m05[:], -0.5), writes=[rc])
    k.epsc = P.sbuf("epsc", [128, 1], F32)
    P.op("pool", lambda e: e.memset(k.epsc[:], EPS), writes=[rc])
    k.fold = P.sbuf("fold", [128, 128], BF16)
    for (a, b) in ((0, 0), (64, 64), (0, 64), (64, 0)):
        P.op("dve", lambda e, a=a, b=b: e.tensor_copy(out=k.fold[a:a + 64, b:b + 64], in_=k.identf[a:a + 64, a:a + 64]),
             reads=[rc], writes=[rc])


def rope_table(k, st_out):
    P = k.P
    rc = k.r_const
    k.cs = P.sbuf("cs", [128, S], F32, st_out)
    with contextlib.ExitStack() as st:
        invf = P.sbuf("invf", [128, 2], F32, st)
        posi = P.sbuf("posi", [128, S], I32, st)
        t = P.sbuf("rt", [128, S], F32, st)
        ti = P.sbuf("rti", [128, S], I32, st)
        m = P.sbuf("rm", [128, S], F32, st)
        tf = m
        r = Res()
        P.dma(invf[:], k.invf, writes=[r])
        P.dma(posi[:], k.pos.partition_broadcast(128), writes=[r])
        P.op("dve", lambda e: e.tensor_copy(out=t[:], in_=posi[:]), writes=[r])
        P.op("dve", lambda e: e.tensor_scalar(out=t[:], in0=t[:], scalar1=invf[:, 0:1], scalar2=invf[:, 1:2],
                                              op0=ALU.mult, op1=ALU.add), writes=[r])
        P.op("dve", lambda e: e.tensor_copy(out=ti[:], in_=t[:]), writes=[r])
        P.op("dve", lambda e: e.tensor_copy(out=tf[:], in_=ti[:]), writes=[r])
        P.op("dve", lambda e: e.tensor_tensor(out=t[:], in0=t[:], in1=tf[:], op=ALU.subtract), writes=[r])
        P.op("dve", lambda e: e.tensor_scalar(out=m[:], in0=t[:], scalar1=0.5, scalar2=None, op0=ALU.is_gt), writes=[r])
        P.op("dve", lambda e: e.tensor_tensor(out=t[:], in0=t[:], in1=m[:], op=ALU.subtract), writes=[r])
        P.op("dve", lambda e: e.tensor_scalar(out=m[:], in0=t[:], scalar1=-0.5, scalar2=None, op0=ALU.is_lt), writes=[r])
        P.op("dve", lambda e: e.tensor_tensor(out=t[:], in0=t[:], in1=m[:], op=ALU.add), writes=[r])
        P.op("act", lambda e: e.activation(out=k.cs[:], in_=t[:], func=AF.Sin, scale=6.283185), reads=[r], writes=[rc])
    P.barrier()


def bank(k, cls, banks):
    i = k.bank_rr.get(cls, 0)
    k.bank_rr[cls] = i + 1
    b = banks[i % len(banks)]
    return k.ps[b], k.rps[b]


def rms_to_hT(k, x_src, gT):
    P = k.P
    rc = k.r_const
    with contextlib.ExitStack() as st:
        xr = Ring(P, "nx", [128, D], F32, 3, st)
        sqr = Ring(P, "nsq", [128, D], F32, 2, st)
        ybr = Ring(P, "nyb", [128, D], BF16, 2, st)
        ssr = Ring(P, "nss", [128, 4], F32, 4, st)
        for i in range(NT):
            xt, rx = xr.next()
            sq, rsq = sqr.next()
            yb, ryb = ybr.next()
            ss, rss = ssr.next()
            P.dma(xt[:], x_src[i * 128:(i + 1) * 128, :], reads=k.xr(x_src, i), writes=[rx])
            P.op("act", lambda e, sq=sq, xt=xt, ss=ss: e.activation(
                out=sq[:], in_=xt[:], func=AF.Square, accum_out=ss[:, 0:1]), reads=[rx], writes=[rsq, rss])
            P.op("dve", lambda e, ss=ss: e.tensor_scalar(out=ss[:, 1:2], in0=ss[:, 0:1], scalar1=1.0 / D,
                                                         scalar2=EPS, op0=ALU.mult, op1=ALU.add),
                 reads=[rss], writes=[rss])
            P.op("pool", lambda e, ss=ss: e.tensor_tensor(out=ss[:, 2:3], in0=ss[:, 1:2], in1=k.m05[:],
                                                          op=ALU.pow), reads=[rss, rc], writes=[rss])
            P.op("act", lambda e, yb=yb, xt=xt, ss=ss: e.activation(out=yb[:], in_=xt[:], func=AF.Copy,
                                                                    scale=ss[:, 2:3]),
                 reads=[rx, rss], writes=[ryb])
            pt, rp = bank(k, "n", [0, 1])
            psb = pt[:].bitcast(BF16)
            for c in range(KC):
                P.op("pe", lambda e, c=c, psb=psb, yb=yb: e.transpose(
                    out=psb[:, c * 128:(c + 1) * 128], in_=yb[:, c * 128:(c + 1) * 128], identity=k.ident[:]),
                    reads=[ryb, rc], writes=[rp])
            P.op("dve", lambda e, i=i, psb=psb: e.tensor_tensor(
                out=k.bigA[:, :, i * 128:(i + 1) * 128],
                in0=psb[:, 0:D].rearrange("p (c n) -> p c n", c=KC),
                in1=gT.unsqueeze(2).to_broadcast([128, KC, 128]), op=ALU.mult),
                reads=[rc], writes=[rp, k.r_big[i // 4]])


def final_norm(k, x_src):
    P = k.P
    rc = k.r_const
    outs = []
    with contextlib.ExitStack() as st:
        k.gfin = P.sbuf("gfin", [128, D], F32, st)
        P.dma(k.gfin[:], k.final_norm.partition_broadcast(128), writes=[rc])
        xr = Ring(P, "fx", [128, D], F32, 3, st)
        sqr = Ring(P, "fsq", [128, D], F32, 2, st)
        yr = Ring(P, "fy", [128, D], F32, 3, st)
        ssr = Ring(P, "fss", [128, 4], F32, 4, st)
        for i in range(NT):
            xt, rx = xr.next()
            sq, rsq = sqr.next()
            yt, ry = yr.next()
            ss, rss = ssr.next()
            P.dma(xt[:], x_src[i * 128:(i + 1) * 128, :], reads=k.xr(x_src, i), writes=[rx])
            P.op("act", lambda e, sq=sq, xt=xt, ss=ss: e.activation(
                out=sq[:], in_=xt[:], func=AF.Square, accum_out=ss[:, 0:1]), reads=[rx], writes=[rsq, rss])
            P.op("dve", lambda e, ss=ss: e.tensor_scalar(out=ss[:, 1:2], in0=ss[:, 0:1], scalar1=1.0 / D,
                                                         scalar2=EPS, op0=ALU.mult, op1=ALU.add),
                 reads=[rss], writes=[rss])
            P.op("pool", lambda e, ss=ss: e.tensor_tensor(out=ss[:, 2:3], in0=ss[:, 1:2], in1=k.m05[:],
                                                          op=ALU.pow), reads=[rss, rc], writes=[rss])
            P.op("dve", lambda e, yt=yt, xt=xt, ss=ss: e.scalar_tensor_tensor(
                out=yt[:], in0=xt[:], scalar=ss[:, 2:3], in1=k.gfin[:], op0=ALU.mult, op1=ALU.mult),
                reads=[rx, rss, rc], writes=[ry])
            outs.append(P.dma(k.out[i * 128:(i + 1) * 128, :], yt[:], reads=[ry]))
    return outs


def norm_rings(k, st):
    P = k.P
    return dict(sq=Ring(P, "nf_sq", [128, D], BF16, 1, st), yb=Ring(P, "nf_yb", [128, D], BF16, 2, st),
                ss=Ring(P, "nf_ss", [128, 4], F32, 4, st))


def norm_tile(k, xt, rx, i, nn, NR, banks):
    P = k.P
    rc = k.r_const
    sq, rsq = NR["sq"].next()
    ss, rss = NR["ss"].next()
    P.op("act", lambda e: e.activation(out=sq[:], in_=xt[:], func=AF.Square, accum_out=ss[:, 0:1]),
         reads=[rx], writes=[rsq, rss])
    P.op("dve", lambda e: e.tensor_scalar(out=ss[:, 1:2], in0=ss[:, 0:1], scalar1=1.0 / D, scalar2=EPS,
                                          op0=ALU.mult, op1=ALU.add), writes=[rss])
    P.op("pool", lambda e: e.tensor_tensor(out=ss[:, 2:3], in0=ss[:, 1:2], in1=k.m05[:], op=ALU.pow),
         reads=[rc], writes=[rss])
    if nn[0] == "final":
        yt, ry = NR["y"].next()
        P.op("dve", lambda e: e.scalar_tensor_tensor(out=yt[:], in0=xt[:], scalar=ss[:, 2:3], in1=NR["gfin"][:],
                                                     op0=ALU.mult, op1=ALU.mult), reads=[rx, rss, rc], writes=[ry])
        k.final_ops.append(P.dma(k.out[i * 128:(i + 1) * 128, :], yt[:], reads=[ry]))
        return
    gT = nn[1]
    yb, ryb = NR["yb"].next()
    P.op("act", lambda e: e.activation(out=yb[:], in_=xt[:], func=AF.Copy, scale=ss[:, 2:3]),
         reads=[rx, rss], writes=[ryb])
    pt, rp = bank(k, "nf", banks)
    psb = pt[:].bitcast(BF16)
    for c in range(KC):
        P.op("pe", lambda e, c=c: e.transpose(out=psb[:, c * 128:(c + 1) * 128], in_=yb[:, c * 128:(c + 1) * 128],
                                              identity=k.ident[:]), reads=[ryb, rc], writes=[rp])
    P.op("dve", lambda e: e.tensor_tensor(
        out=k.bigA[:, :, i * 128:(i + 1) * 128], in0=psb[:, 0:D].rearrange("p (c n) -> p c n", c=KC),
        in1=gT.unsqueeze(2).to_broadcast([128, KC, 128]), op=ALU.mult),
        reads=[rc], writes=[rp, k.r_big[i // 4]])


def setup_next_norm(k, nn, st):
    if nn is None:
        return None
    NR = norm_rings(k, st)
    if nn[0] == "final":
        P = k.P
        NR["gfin"] = P.sbuf("nf_gfin", [128, D], F32, st)
        P.dma(NR["gfin"][:], k.final_norm.partition_broadcast(128), writes=[k.r_const])
        NR["y"] = Ring(P, "nf_y", [128, D], F32, 2, st)
    return NR


FFN_TB = 1024
GDN_LANES = 5
GDN_STAGGER = 9


def ffn_layer(k, L, x_src, nn=None):
    P = k.P
    wgu = k.ffn_w_gate_up[L].rearrange("(c p) n -> p c n", p=128)
    wdn = k.ffn_w_down[L].rearrange("(c p) n -> p c n", p=128)
    with contextlib.ExitStack() as st:
        wd = P.sbuf("wd", [128, FC, D], BF16, st)
        r_wd = Res()
        actT = P.sbuf("actT", [128, FC, FFN_TB], BF16, st)
        r_act = [Res() for _ in range(FFN_TB // 512)]
        wr = Ring(P, "wgu", [128, KC, 256], BF16, 4, st)
        pre_w = []
        for j in range(3):
            wt, rw = wr.next()
            P.dma(wt[:, :, 0:128], wgu[:, :, j * 128:(j + 1) * 128], writes=[rw], eng="pool")
            P.dma(wt[:, :, 128:256], wgu[:, :, D_FF + j * 128:D_FF + (j + 1) * 128], writes=[rw], eng="pool")
            pre_w.append((wt, rw))
        sgr = Ring(P, "sg", [128, 512], F32, 2, st)
        xr = Ring(P, "fx", [128, D], F32, 3, st)
        NR = setup_next_norm(k, nn, st)
        NSB = S // FFN_TB
        seq = [(sb, j) for sb in range(NSB) for j in range(FC)]
        loaded = {}
        for n_, (sb_, j_) in enumerate(seq[:3]):
            loaded[(sb_, j_)] = pre_w[n_]

        def prefetch(idx):
            if idx < len(seq):
                sb_, j_ = seq[idx]
                wt_, rw_ = wr.next()
                P.dma(wt_[:, :, 0:128], wgu[:, :, j_ * 128:(j_ + 1) * 128], writes=[rw_], eng="pool")
                P.dma(wt_[:, :, 128:256], wgu[:, :, D_FF + j_ * 128:D_FF + (j_ + 1) * 128], writes=[rw_], eng="pool")
                loaded[(sb_, j_)] = (wt_, rw_)

        for sb in range(NSB):
            for j in range(FC):
                prefetch(sb * FC + j + 3)
                if sb == 0 and j < FC // 2:
                    P.dma(wd[:, 2 * j:2 * j + 2, :], wdn[:, 2 * j:2 * j + 2, :], writes=[r_wd], eng="pool")
                wt, rw = loaded.pop((sb, j))
                for tb in range(FFN_TB // 512):
                    t0 = sb * FFN_TB + tb * 512
                    gb = (sb * FFN_TB) // 512 + tb
                    pg, rpg = bank(k, "fg", [0, 1, 2, 3])
                    pu, rpu = bank(k, "fg", [0, 1, 2, 3])
                    for c in range(KC):
                        P.op("pe", lambda e, c=c, pg=pg, wt=wt, t0=t0: e.matmul(
                            pg[:, :], lhsT=wt[:, c, 0:128], rhs=k.bigA[:, c, t0:t0 + 512],
                            start=(c == 0), stop=(c == KC - 1)), reads=[rw, k.r_big[gb]], writes=[rpg])
                    for c in range(KC):
                        P.op("pe", lambda e, c=c, pu=pu, wt=wt, t0=t0: e.matmul(
                            pu[:, :], lhsT=wt[:, c, 128:256], rhs=k.bigA[:, c, t0:t0 + 512],
                            start=(c == 0), stop=(c == KC - 1)), reads=[rw, k.r_big[gb]], writes=[rpu])
                    sg, rsg = sgr.next()
                    P.op("act", lambda e, sg=sg, pg=pg: e.activation(out=sg[:], in_=pg[:, :], func=AF.Silu),
                         writes=[rpg, rsg])
                    P.op("dve", lambda e, sg=sg, pu=pu, j=j, tb=tb: e.tensor_tensor(
                        out=actT[:, j, tb * 512:(tb + 1) * 512], in0=pu[:, :], in1=sg[:], op=ALU.mult),
                        reads=[rsg], writes=[rpu, r_act[tb]])
            for tt in range(FFN_TB // 128):
                tok0 = sb * FFN_TB + tt * 128
                xt, rx = xr.next()
                P.dma(xt[:], x_src[tok0:tok0 + 128, :], reads=k.xr(x_src, tok0 // 128), writes=[rx])
                for dh in range(2):
                    po, rpo = bank(k, "fd", [4, 5, 6, 7])
                    for j in range(FC):
                        P.op("pe", lambda e, j=j, po=po, tt=tt, dh=dh: e.matmul(
                            po[:, :], lhsT=actT[:, j, tt * 128:(tt + 1) * 128], rhs=wd[:, j, dh * 512:(dh + 1) * 512],
                            start=(j == 0), stop=(j == FC - 1)), reads=[r_act[tt // 4], r_wd], writes=[rpo])
                    P.op("dve", lambda e, po=po, xt=xt, dh=dh: e.tensor_tensor(
                        out=xt[:, dh * 512:(dh + 1) * 512], in0=po[:, :], in1=xt[:, dh * 512:(dh + 1) * 512],
                        op=ALU.add), reads=[], writes=[rpo, rx])
                P.dma(k.xres[tok0:tok0 + 128, :], xt[:], reads=[rx], writes=[k.r_xres[tok0 // 128]])
                if nn is not None:
                    norm_tile(k, xt, rx, tok0 // 128, nn, NR, [0, 1])


def gdn_layer(k, j, x_src, nn=None):
    P = k.P
    rc = k.r_const
    NB = S // 512
    w_in = k.gdn_w_in[j].rearrange("(c p) n -> p c n", p=128)
    oTd = k.oT_d.rearrange("(h p) t -> p h t", p=128)
    r_oTd = [Res() for _ in range(NB)]
    with contextlib.ExitStack() as st:
        cw_raw = P.sbuf("g_cwraw", [128, 128], F32, st)
        cwT = P.sbuf("g_cwT", [128, 4, 32], F32, st)
        dtb = P.sbuf("g_dtb", [128, 16], F32, st)
        nA = P.sbuf("g_nA", [128, 16], F32, st)
        gon = P.sbuf("g_gon", [128, 128], F32, st)
        triu = P.sbuf("g_triu", [128, 128], F32, st)
        neglt = P.sbuf("g_neglt", [128, 128], F32, st)
        posue = P.sbuf("g_posue", [128, 128], F32, st)
        r_lc = Res()
        gm = P.sbuf("g_gm", [128, 8, 128], BF16, st)
        P.dma(gm[:], k.gmask, writes=[r_lc], eng="pool")
        P.dma(cw_raw[:], k.gdn_conv_w[j].rearrange("t (c p) -> (t c) p", p=128), writes=[r_lc])
        P.dma(dtb[:], k.gdn_dt_bias[j:j + 1, :].partition_broadcast(128), writes=[r_lc])
        P.dma(nA[:], k.gdn_a_log[j:j + 1, :].partition_broadcast(128), writes=[r_lc])
        P.dma(gon[:], k.gdn_out_norm[j:j + 1, :].partition_broadcast(128), writes=[r_lc])
        P.op("act", lambda e: e.activation(out=nA[:], in_=nA[:], func=AF.Exp), writes=[r_lc])
        P.op("act", lambda e: e.mul(out=nA[:], in_=nA[:], mul=-1.0), writes=[r_lc])
        pc, rpc = bank(k, "gt", [2, 3, 4])
        P.op("pe", lambda e: e.transpose(out=pc[:, 0:128], in_=cw_raw[:, :], identity=k.identf[:]),
             reads=[r_lc, rc], writes=[rpc])
        P.op("dve", lambda e: e.tensor_copy(out=cwT[:].rearrange("p a b -> p (a b)"), in_=pc[:, 0:128]),
             writes=[rpc, r_lc])
        P.op("pool", lambda e: e.memset(triu[:], 1.0), writes=[r_lc])
        P.op("pool", lambda e: e.affine_select(out=triu[:], in_=triu[:], pattern=[[1, 128]], compare_op=ALU.is_ge,
                                               fill=0.0, base=0, channel_multiplier=-1), writes=[r_lc])
        P.op("pool", lambda e: e.memset(neglt[:], 0.0), writes=[r_lc])
        P.op("pool", lambda e: e.affine_select(out=neglt[:], in_=neglt[:], pattern=[[-1, 128]], compare_op=ALU.is_ge,
                                               fill=-30000.0, base=-1, channel_multiplier=1), writes=[r_lc])
        P.op("pool", lambda e: e.memset(posue[:], 0.0), writes=[r_lc])
        P.op("pool", lambda e: e.affine_select(out=posue[:], in_=posue[:], pattern=[[1, 128]], compare_op=ALU.is_ge,
                                               fill=30000.0, base=0, channel_multiplier=-1), writes=[r_lc])
        NH = 16
        gs = {}
        for nm in ("beta", "gc", "eg", "ekd", "egl", "kbs"):
            gs[nm] = P.sbuf("g_" + nm, [128, NT, NH], F32, st)
        r_gs = Res()
        f2 = lambda t: t[:].rearrange("p a b -> p (a b)")
        with contextlib.ExitStack() as stg:
            for nm in ("g", "glb"):
                gs[nm] = P.sbuf("g_" + nm, [128, NT, NH], F32, stg)
            wba = P.sbuf("g_wba", [128, KC, 32], BF16, stg)
            ba = P.sbuf("g_ba", [128, NT, 32], F32, stg)
            tmp = P.sbuf("g_tmp", [128, NT, NH], F32, stg)
            r_wba = Res()
            P.dma(wba[:], w_in[:, :, 6144:6176], writes=[r_wba], eng="pool")
            for half in range(2):
                pb_, rpb_ = bank(k, "gt", [2, 3, 4])
                for tl in range(16):
                    i = half * 16 + tl
                    for c in range(KC):
                        P.op("pe", lambda e, c=c, i=i, tl=tl, pb_=pb_: e.matmul(
                            pb_[:, tl * 32:(tl + 1) * 32], lhsT=k.bigA[:, c, i * 128:(i + 1) * 128], rhs=wba[:, c, :],
                            start=(c == 0), stop=(c == KC - 1)), reads=[r_wba, k.r_big[i // 4]], writes=[rpb_])
                P.op("act", lambda e, half=half, pb_=pb_: e.copy(
                    out=ba[:, half * 16:(half + 1) * 16, :].rearrange("p a b -> p (a b)"), in_=pb_[:, :]),
                    writes=[rpb_, r_gs])
            P.op("act", lambda e: e.activation(out=gs["beta"][:], in_=ba[:, :, 0:16], func=AF.Sigmoid), writes=[r_gs])
            P.op("dve", lambda e: e.tensor_tensor(out=tmp[:], in0=ba[:, :, 16:32],
                                                  in1=dtb[:, 0:16].unsqueeze(1).to_broadcast([128, NT, NH]), op=ALU.add),
                 reads=[r_lc], writes=[r_gs])
            P.op("act", lambda e: e.activation(out=tmp[:], in_=tmp[:], func=AF.Exp), writes=[r_gs])
            P.op("act", lambda e: e.activation(out=tmp[:], in_=tmp[:], func=AF.Ln, bias=1.0), writes=[r_gs])
            P.op("dve", lambda e: e.tensor_tensor(out=gs["g"][:], in0=tmp[:],
                                                  in1=nA[:, 0:16].unsqueeze(1).to_broadcast([128, NT, NH]), op=ALU.mult),
                 reads=[r_lc], writes=[r_gs])
            p1, rp1 = bank(k, "gt", [2, 3, 4])
            P.op("pe", lambda e: e.matmul(p1[:, :], lhsT=triu[:], rhs=f2(gs["g"]), start=True, stop=True),
                 reads=[r_gs, r_lc], writes=[rp1])
            P.op("dve", lambda e: e.tensor_copy(out=f2(gs["gc"]), in_=p1[:, :]), writes=[rp1, r_gs])
            p2, rp2 = bank(k, "gt", [2, 3, 4])
            P.op("pe", lambda e: e.matmul(p2[:, :], lhsT=k.onesf[:], rhs=f2(gs["g"]), start=True, stop=True),
                 reads=[r_gs, rc], writes=[rp2])
            P.op("dve", lambda e: e.tensor_copy(out=f2(gs["glb"]), in_=p2[:, :]), writes=[rp2, r_gs])
            P.op("act", lambda e: e.activation(out=gs["eg"][:], in_=gs["gc"][:], func=AF.Exp), writes=[r_gs])
            P.op("act", lambda e: e.activation(out=gs["egl"][:], in_=gs["glb"][:], func=AF.Exp), writes=[r_gs])
            P.op("dve", lambda e: e.tensor_tensor(out=tmp[:], in0=gs["glb"][:], in1=gs["gc"][:], op=ALU.subtract),
                 writes=[r_gs])
            P.op("act", lambda e: e.activation(out=gs["ekd"][:], in_=tmp[:], func=AF.Exp), writes=[r_gs])
            P.op("dve", lambda e: e.tensor_tensor(out=gs["kbs"][:], in0=gs["beta"][:], in1=gs["eg"][:], op=ALU.mult),
                 writes=[r_gs])
        for nm in ("beta", "gc", "eg", "ekd", "egl", "kbs"):
            dump(k, "gs_" + nm, gs[nm][:], [r_gs])
        P.barrier()
        wfr = Ring(P, "g_wf", [128, KC, 512], BF16, 2, st)
        wzr = Ring(P, "g_wz", [128, KC, 256], BF16, 1, st)
        qT = P.sbuf("g_qT", [128, S], BF16, st)
        kT = P.sbuf("g_kT", [128, S], BF16, st)
        vT = P.sbuf("g_vT", [128, 2, S], BF16, st)
        r_q = [Res() for _ in range(NB)]
        r_k = [Res() for _ in range(NB)]
        r_v = [Res() for _ in range(NB)]
        S32 = P.sbuf("g_S32", [128, 2, 128], F32, st)
        Sb = P.sbuf("g_Sb", [128, 2, 128], BF16, st)
        r_S32, r_Sb = Res(), Res()
        GT = [2, 3, 4]
        GR = [5, 6]
        GO = [7]
        GB = [0, 1]
        NL = GDN_LANES
        bc3 = lambda ap2: ap2.unsqueeze(1).to_broadcast([128, 2, 128])
        bcl = lambda ap2: ap2.unsqueeze(2).to_broadcast([128, 2, 128])
        v3 = lambda ap: ap.rearrange("p (a b) -> p a b", a=2)

        def chunk_gen(kh, tb, ch, wf, rwf, B):
            t0 = tb * 512
            rb = k.r_big[tb]
            chunk = (kh, 8 + kh, 16 + 2 * kh, 17 + 2 * kh)[ch]
            pp, rpp = k.ps[ch], k.rps[ch]
            for c in range(KC):
                P.op("pe", lambda e, c=c: e.matmul(
                    pp[:, :], lhsT=wf[:, c, ch * 128:(ch + 1) * 128], rhs=k.bigA[:, c, t0:t0 + 512],
                    start=(c == 0), stop=(c == KC - 1)), reads=[rwf, rb], writes=[rpp])
            yield
            pre, rpre = B["pre"][ch].next()
            halo, r_halo = B["halo"], B["r_halo"]
            P.op("act", lambda e: e.copy(out=pre[:, 3:515], in_=pp[:, :]), writes=[rpp, rpre])
            if tb == 0:
                P.op("pool", lambda e: e.memset(pre[:, 0:3], 0.0), writes=[rpre])
            else:
                P.op("pool", lambda e: e.tensor_copy(out=pre[:, 0:3], in_=halo[ch][:, 0:3]),
                     reads=[r_halo[ch]], writes=[rpre])
            yield
            P.op("pool", lambda e: e.tensor_copy(out=halo[ch][:, 0:3], in_=pre[:, 512:515]),
                 reads=[rpre], writes=[r_halo[ch]])
            cv, rcv = B["cv"][ch].next()
            P.op("dve", lambda e: e.tensor_scalar(
                out=cv[:], in0=pre[:, 3:515], scalar1=cwT[:, 3, chunk:chunk + 1], scalar2=None, op0=ALU.mult),
                reads=[rpre, r_lc], writes=[rcv])
            for tap in (2, 1, 0):
                P.op("dve", lambda e, tap=tap: e.scalar_tensor_tensor(
                    out=cv[:], in0=pre[:, tap:tap + 512], scalar=cwT[:, tap, chunk:chunk + 1], in1=cv[:],
                    op0=ALU.mult, op1=ALU.add), reads=[rpre, r_lc], writes=[rcv])
            yield
            if ch >= 2:
                P.op("act", lambda e: e.activation(out=vT[:, ch - 2, t0:t0 + 512], in_=cv[:], func=AF.Silu),
                     reads=[rcv], writes=[r_v[tb]])
                yield
                return
            P.op("act", lambda e: e.activation(out=cv[:], in_=cv[:], func=AF.Silu), writes=[rcv])
            yield
            sq, rsq = B["sq"][ch].next()
            P.op("pool", lambda e: e.tensor_tensor(out=sq[:], in0=cv[:], in1=cv[:], op=ALU.mult),
                 reads=[rcv], writes=[rsq])
            yield
            pss, rpss = k.ps[4 + ch], k.rps[4 + ch]
            P.op("pe", lambda e: e.matmul(pss[:, :], lhsT=k.onesb[:], rhs=sq[:], start=True, stop=True),
                 reads=[rsq, rc], writes=[rpss])
            yield
            ln, rln = B["ln"][ch].next()
            P.op("act", lambda e: e.activation(out=ln[:], in_=pss[:, :], func=AF.Ln, bias=k.epsc[:, 0:1]),
                 reads=[rc], writes=[rpss, rln])
            P.op("act", lambda e: e.activation(out=ln[:], in_=ln[:], func=AF.Exp, scale=-0.5), writes=[rln])
            yield
            dst, rdst, mul = ((qT, r_q, 128.0 ** -0.5), (kT, r_k, 1.0))[ch]
            P.op("dve", lambda e: e.scalar_tensor_tensor(
                out=dst[:, t0:t0 + 512], in0=cv[:], scalar=mul, in1=ln[:], op0=ALU.mult, op1=ALU.mult),
                reads=[rcv, rln], writes=[rdst[tb]])
            yield

        def pt_stage(kh, i, L, O):
            h0 = 2 * kh
            tb = i // 4
            tok = slice(i * 128, (i + 1) * 128)
            pair = lambda nm: gs[nm][:, i, h0:h0 + 2]
            col = lambda nm, hh: gs[nm][:, i, h0 + hh:h0 + hh + 1]
            kbg, rkbg = O["kbg"]
            kdec, rkdec = O["kdec"]
            bv, rbv = O["bv"]
            qkm, rqkm = O["qkm"]
            TT, rTT = O["TT"]
            nwT, rnwT = O["nwT"]
            hb = [0]

            def lbank():
                h = hb[0]
                hb[0] ^= 1
                return k.ps[L["bank"]][:, h * 256:(h + 1) * 256], k.rps[L["bank"]]
            pt, rpt = lbank()
            ptb = pt.bitcast(BF16)
            P.op("pe", lambda e: e.transpose(out=ptb[:, 0:128], in_=kT[:, tok], identity=k.ident[:]),
                 reads=[r_k[tb], rc], writes=[rpt])
            for hh in range(2):
                P.op("pe", lambda e, hh=hh: e.transpose(out=ptb[:, 128 + hh * 128:256 + hh * 128], in_=vT[:, hh, tok],
                                                        identity=k.ident[:]), reads=[r_v[tb], rc], writes=[rpt])
            dg, rdg = L["dg"]
            P.op("pool", lambda e: e.tensor_tensor(out=dg[:], in0=bc3(k.identf[:]), in1=bcl(pair("gc")), op=ALU.mult),
                 reads=[rc, r_gs], writes=[rdg])
            yield
            P.op("dve", lambda e: e.tensor_tensor(out=kbg[:], in0=bc3(ptb[:, 0:128]), in1=bcl(pair("kbs")), op=ALU.mult),
                 reads=[r_gs], writes=[rpt, rkbg])
            P.op("dve", lambda e: e.tensor_tensor(out=kdec[:], in0=bc3(ptb[:, 0:128]), in1=bcl(pair("ekd")), op=ALU.mult),
                 reads=[r_gs], writes=[rpt, rkdec])
            P.op("dve", lambda e: e.tensor_tensor(out=bv[:], in0=v3(ptb[:, 128:384]), in1=bcl(pair("beta")), op=ALU.mult),
                 reads=[r_gs], writes=[rpt, rbv])
            pa, rpa = lbank()
            for hh in range(2):
                P.op("pe", lambda e, hh=hh: e.matmul(pa[:, hh * 128:(hh + 1) * 128], lhsT=dg[:, hh, :], rhs=k.onesf[:],
                                                     start=True, stop=False), reads=[rdg, rc], writes=[rpa])
                P.op("pe", lambda e, hh=hh: e.matmul(pa[:, hh * 128:(hh + 1) * 128], lhsT=k.negonesf[:], rhs=dg[:, hh, :],
                                                     start=False, stop=True), reads=[rdg, rc], writes=[rpa])
            pk, rpk = lbank()
            P.op("pe", lambda e: e.matmul(pk[:, 0:128], lhsT=kT[:, tok], rhs=kT[:, tok], start=True, stop=True),
                 reads=[r_k[tb]], writes=[rpk])
            P.op("pe", lambda e: e.matmul(pk[:, 128:256], lhsT=kT[:, tok], rhs=qT[:, tok], start=True, stop=True),
                 reads=[r_k[tb], r_q[tb]], writes=[rpk])
            yield
            dm, rdm = L["dm"]
            dmt, rdmt = L["dmt"]
            P.op("dve", lambda e: e.scalar_tensor_tensor(out=dm[:], in0=v3(pa[:, 0:256]), scalar=0.0, in1=bc3(neglt[:]),
                                                         op0=ALU.min, op1=ALU.add), reads=[r_lc], writes=[rpa, rdm])
            P.op("dve", lambda e: e.scalar_tensor_tensor(out=dmt[:], in0=v3(pa[:, 0:256]), scalar=0.0, in1=bc3(posue[:]),
                                                         op0=ALU.max, op1=ALU.add), reads=[r_lc], writes=[rpa, rdmt])
            yield
            P.op("act", lambda e: e.activation(out=dm[:], in_=dm[:], func=AF.Exp), writes=[rdm])
            P.op("act", lambda e: e.activation(out=dmt[:], in_=dmt[:], func=AF.Exp, scale=-1.0), writes=[rdmt])
            yield
            A, rA = L["A"]
            for hh in range(2):
                P.op("dve", lambda e, hh=hh: e.scalar_tensor_tensor(
                    out=A[:, hh, :], in0=pk[:, 0:128], scalar=col("beta", hh), in1=dm[:, hh, :],
                    op0=ALU.mult, op1=ALU.mult), reads=[rdm, r_gs], writes=[rpk, rA])
            P.op("dve", lambda e: e.tensor_tensor(out=qkm[:], in0=bc3(pk[:, 128:256]), in1=dmt[:], op=ALU.mult),
                 reads=[rdmt], writes=[rpk, rqkm])
            yield
            pm, rpm = lbank()
            pmb = pm.bitcast(BF16)
            for hh in range(2):
                P.op("pe", lambda e, hh=hh: e.transpose(out=pmb[:, hh * 128:(hh + 1) * 128], in_=A[:, hh, :],
                                                        identity=k.ident[:]), reads=[rA, rc], writes=[rpm])
            Am, rAm = L["Mo"][0]
            P.op("pool", lambda e: e.tensor_tensor(out=Am[:], in0=A[:], in1=bc3(gm[:, 0, :]), op=ALU.mult),
                 reads=[rA, r_lc], writes=[rAm])
            yield
            M, rM = L["M"]
            P.op("act", lambda e: e.copy(out=M[:], in_=v3(pmb[:, 0:256])), writes=[rpm, rM])
            UV, rUV = L["UV"][0]
            P.op("dve", lambda e, UV=UV: e.tensor_tensor(out=UV[:, 1], in0=bc3(k.ident[:]), in1=Am[:], op=ALU.subtract),
                 reads=[rAm, rc], writes=[rUV])
            yield
            Mm, rMm = L["Mo"][1]
            P.op("pool", lambda e: e.tensor_tensor(out=Mm[:], in0=M[:], in1=bc3(gm[:, 1, :]), op=ALU.mult),
                 reads=[rM, r_lc], writes=[rMm])
            yield
            P.op("dve", lambda e, UV=UV: e.tensor_tensor(out=UV[:, 0], in0=bc3(k.ident[:]), in1=Mm[:], op=ALU.subtract),
                 reads=[rMm, rc], writes=[rUV])
            yield
            pbank, rpbank = k.ps[L["bank"]], k.rps[L["bank"]]
            for lv in range(6):
                pY, rpY = lbank()
                for hh in range(2):
                    P.op("pe", lambda e, hh=hh, pY=pY, UV=UV: e.matmul(
                        pY[:, hh * 128:(hh + 1) * 128], lhsT=M[:, hh, :], rhs=UV[:, 1, hh, :], start=True, stop=True),
                        reads=[rM, rUV], writes=[rpY])
                yield
                Y, rY = L["Y"]
                P.op("dve", lambda e, Y=Y, pY=pY, lv=lv: e.scalar_tensor_tensor(
                    out=Y[:], in0=v3(pY[:, 0:256]), scalar=-1.0, in1=bc3(gm[:, 2 + lv, :]), op0=ALU.mult, op1=ALU.mult),
                    reads=[r_lc], writes=[rpY, rY])
                yield
                for hh in range(2):
                    P.op("pe", lambda e, hh=hh, UV=UV: e.matmul(
                        pbank[:, hh * 128:(hh + 1) * 128], lhsT=k.ident[:], rhs=UV[:, 0, hh, :], start=True, stop=False),
                        reads=[rUV, rc], writes=[rpbank])
                    P.op("pe", lambda e, hh=hh, UV=UV, Y=Y: e.matmul(
                        pbank[:, hh * 128:(hh + 1) * 128], lhsT=Y[:, hh, :], rhs=UV[:, 0, hh, :], start=False, stop=True),
                        reads=[rUV, rY], writes=[rpbank])
                if lv < 5:
                    for hh in range(2):
                        P.op("pe", lambda e, hh=hh, UV=UV: e.matmul(
                            pbank[:, 256 + hh * 128:384 + hh * 128], lhsT=k.ident[:], rhs=UV[:, 1, hh, :], start=True, stop=False),
                            reads=[rUV, rc], writes=[rpbank])
                        P.op("pe", lambda e, hh=hh, UV=UV, Y=Y: e.matmul(
                            pbank[:, 256 + hh * 128:384 + hh * 128], lhsT=UV[:, 0, hh, :], rhs=Y[:, hh, :], start=False, stop=True),
                            reads=[rUV, rY], writes=[rpbank])
                yield
                if lv < 5:
                    UVn, rUVn = L["UV"][(lv + 1) % 2]
                    P.op("act", lambda e, UVn=UVn: e.copy(out=UVn[:].rearrange("p a b c -> p (a b c)"), in_=pbank[:, :]),
                         writes=[rpbank, rUVn])
                    UV, rUV = UVn, rUVn
                else:
                    P.op("act", lambda e: e.copy(out=TT[:], in_=v3(pbank[:, 0:256])), writes=[rpbank, rTT])
                hb[0] = 0
                yield
            pw, rpw = lbank()
            for hh in range(2):
                P.op("pe", lambda e, hh=hh: e.matmul(pw[:, hh * 128:(hh + 1) * 128], lhsT=kbg[:, hh, :], rhs=TT[:, hh, :],
                                                     start=True, stop=True), reads=[rkbg, rTT], writes=[rpw])
            yield
            P.op("act", lambda e: e.mul(out=nwT[:], in_=v3(pw[:, 0:256]), mul=-1.0), writes=[rpw, rnwT])
            yield

        def r_stage(kh, i, O, RB, done):
            h0 = 2 * kh
            tb = i // 4
            tok = slice(i * 128, (i + 1) * 128)
            pair = lambda nm: gs[nm][:, i, h0:h0 + 2]
            col = lambda nm, hh: gs[nm][:, i, h0 + hh:h0 + hh + 1]
            TT, rTT = O["TT"]
            nwT, rnwT = O["nwT"]
            bv, rbv = O["bv"]
            kdec, rkdec = O["kdec"]
            qkm, rqkm = O["qkm"]
            pv, rpv = k.ps[4][:, 0:256], k.rps[4]
            for hh in range(2):
                P.op("pe", lambda e, hh=hh: e.matmul(pv[:, hh * 128:(hh + 1) * 128], lhsT=TT[:, hh, :], rhs=bv[:, hh, :],
                                                     start=True, stop=False), reads=[rTT, rbv], writes=[rpv])
                P.op("pe", lambda e, hh=hh: e.matmul(pv[:, hh * 128:(hh + 1) * 128], lhsT=nwT[:, hh, :], rhs=Sb[:, hh, :],
                                                     start=False, stop=True), reads=[rnwT, r_Sb], writes=[rpv])
            pz, rpz = k.ps[5], k.rps[5]
            for hh in range(2):
                P.op("pe", lambda e, hh=hh: e.matmul(pz[:, hh * 128:(hh + 1) * 128], lhsT=qT[:, tok], rhs=Sb[:, hh, :],
                                                     start=True, stop=True), reads=[r_q[tb], r_Sb], writes=[rpz])
            yield
            vn, rvn = RB["vn"].next()
            P.op("act", lambda e: e.copy(out=vn[:], in_=v3(pv[:, 0:256])), writes=[rpv, rvn])
            yield
            pd, rpd = k.ps[4][:, 256:512], k.rps[4]
            for hh in range(2):
                P.op("pe", lambda e, hh=hh: e.matmul(pd[:, hh * 128:(hh + 1) * 128], lhsT=kdec[:, hh, :], rhs=vn[:, hh, :],
                                                     start=True, stop=True), reads=[rkdec, rvn], writes=[rpd])
            for hh in range(2):
                P.op("pe", lambda e, hh=hh: e.matmul(pz[:, 256 + hh * 128:384 + hh * 128], lhsT=qkm[:, hh, :], rhs=vn[:, hh, :],
                                                     start=True, stop=True), reads=[rqkm, rvn], writes=[rpz])
            yield
            for hh in range(2):
                P.op("dve", lambda e, hh=hh: e.scalar_tensor_tensor(
                    out=S32[:, hh, :], in0=S32[:, hh, :], scalar=col("egl", hh), in1=pd[:, hh * 128:(hh + 1) * 128],
                    op0=ALU.mult, op1=ALU.add), reads=[r_gs], writes=[rpd, r_S32])
            zs, rzs = RB["zs"].next()
            P.op("dve", lambda e: e.tensor_tensor(out=zs[:], in0=v3(pz[:, 0:256]), in1=bcl(pair("eg")), op=ALU.mult),
                 reads=[r_gs], writes=[rpz, rzs])
            yield
            P.op("pool", lambda e: e.tensor_copy(out=Sb[:], in_=S32[:]), reads=[r_S32], writes=[r_Sb])
            o32, ro32 = RB["o32"].next()
            P.op("dve", lambda e: e.tensor_tensor(out=o32[:], in0=v3(pz[:, 256:512]), in1=zs[:], op=ALU.add),
                 reads=[rzs], writes=[rpz, ro32])
            done[i] = (o32, ro32)
            yield

        def o_stage(kh, i, o32, ro32, wz, rwz, RB, oT4, roT4):
            h0 = 2 * kh
            tb = i // 4
            tok = slice(i * 128, (i + 1) * 128)
            pzz, rpzz = k.ps[6][:, 0:256], k.rps[6]
            for c in range(KC):
                P.op("pe", lambda e, c=c: e.matmul(pzz[:, 0:256], lhsT=k.bigA[:, c, tok], rhs=wz[:, c, :],
                                                   start=(c == 0), stop=(c == KC - 1)), reads=[rwz, k.r_big[tb]], writes=[rpzz])
            junk, rjunk = RB["junk"].next()
            ssq, rssq = RB["ssq"].next()
            for hh in range(2):
                P.op("act", lambda e, hh=hh: e.activation(out=junk[:, hh, :], in_=o32[:, hh, :], func=AF.Square,
                                                          accum_out=ssq[:, hh:hh + 1]), reads=[ro32], writes=[rjunk, rssq])
            yield
            zz, rzz = RB["zz"].next()
            P.op("act", lambda e: e.activation(out=zz[:], in_=pzz[:, 0:256], func=AF.Silu), writes=[rpzz, rzz])
            P.op("dve", lambda e: e.tensor_scalar(out=ssq[:, 2:4], in0=ssq[:, 0:2], scalar1=1.0 / 128, scalar2=EPS,
                                                  op0=ALU.mult, op1=ALU.add), writes=[rssq])
            yield
            P.op("pool", lambda e: e.tensor_tensor(out=ssq[:, 4:6], in0=ssq[:, 2:4], in1=k.m05[:, 0:1].to_broadcast([128, 2]),
                                                   op=ALU.pow), reads=[rc], writes=[rssq])
            yield
            t1, rt1 = RB["t1"].next()
            for hh in range(2):
                P.op("dve", lambda e, hh=hh: e.scalar_tensor_tensor(
                    out=t1[:, hh, :], in0=o32[:, hh, :], scalar=ssq[:, 4 + hh:5 + hh], in1=gon[:],
                    op0=ALU.mult, op1=ALU.mult), reads=[ro32, rssq, r_lc], writes=[rt1])
            yield
            ob, rob = RB["ob"].next()
            P.op("pool", lambda e: e.tensor_tensor(out=ob[:], in0=t1[:], in1=v3(zz[:]), op=ALU.mult),
                 reads=[rt1, rzz], writes=[rob])
            yield
            po, rpo = k.ps[6][:, 256:512], k.rps[6]
            pob = po.bitcast(BF16)
            for hh in range(2):
                P.op("pe", lambda e, hh=hh: e.transpose(out=pob[:, hh * 128:(hh + 1) * 128], in_=ob[:, hh, :],
                                                        identity=k.ident[:]), reads=[rob, rc], writes=[rpo])
            yield
            q4 = i % 4
            P.op("act", lambda e: e.copy(out=oT4[:, :, q4 * 128:(q4 + 1) * 128], in_=v3(pob[:, 0:256])),
                 writes=[rpo, roT4])
            if q4 == 3:
                P.dma(oTd[:, h0:h0 + 2, tb * 512:(tb + 1) * 512], oT4[:], reads=[roT4], writes=[r_oTd[tb]])
            yield

        def delayed(g, d):
            for _ in range(d):
                yield
            yield from g

        def run_lanes(gens):
            active = list(gens)
            while active:
                for g in list(active):
                    try:
                        next(g)
                    except StopIteration:
                        active.remove(g)

        def group(kh):
            wf, rwf = wfr.next()
            wz, rwz = wzr.next()
            for c in range(0, KC, 4):
                P.dma(wf[:, c:c + 4, 0:128], w_in[:, c:c + 4, kh * 128:(kh + 1) * 128], writes=[rwf], eng="pool")
                P.dma(wf[:, c:c + 4, 128:256], w_in[:, c:c + 4, 1024 + kh * 128:1024 + (kh + 1) * 128], writes=[rwf], eng="pool")
                P.dma(wf[:, c:c + 4, 256:512], w_in[:, c:c + 4, 2048 + kh * 256:2048 + (kh + 1) * 256], writes=[rwf], eng="pool")
                P.dma(wz[:, c:c + 4, :], w_in[:, c:c + 4, 4096 + kh * 256:4096 + (kh + 1) * 256], writes=[rwz], eng="pool")
            with contextlib.ExitStack() as stb:
                B = dict(pre=[Ring(P, "g_pre", [128, 515], F32, 2, stb) for _ in range(4)],
                         halo=[P.sbuf(f"g_halo{ch}", [128, 4], F32, stb) for ch in range(4)],
                         r_halo=[Res() for _ in range(4)],
                         cv=[Ring(P, "g_cv", [128, 512], F32, 2, stb) for _ in range(4)],
                         sq=[Ring(P, "g_sq", [128, 512], BF16, 2, stb) for _ in range(2)],
                         ln=[Ring(P, "g_ln", [128, 512], F32, 2, stb) for _ in range(2)])
                for tb in range(NB):
                    run_lanes([chunk_gen(kh, tb, ch, wf, rwf, B) for ch in range(4)])
            P.barrier()
            with contextlib.ExitStack() as stt:
                T3 = lambda nm, dt: (P.sbuf(nm, [128, 2, 128], dt, stt), Res())
                lanes = []
                NSLOT = NL + 2
                for ln_ in range(NL):
                    dgt = T3("g_dg", F32)
                    At = T3("g_A", BF16)
                    lanes.append(dict(bank=(0, 1, 2, 3, 7)[ln_], dg=dgt, dm=T3("g_dm", F32), dmt=dgt,
                                      A=At, M=T3("g_M", BF16),
                                      Mo=[T3("g_Mo", BF16), T3("g_Mo", BF16)],
                                      UV=[(P.sbuf("g_UV", [128, 2, 2, 128], BF16, stt), Res()) for _ in range(2)],
                                      Y=At))
                oslots = [dict((nm, T3("g_" + nm, BF16)) for nm in ("kbg", "kdec", "bv", "qkm", "TT", "nwT"))
                          for _ in range(NSLOT)]
                R3 = lambda nm, dt, n: Ring(P, nm, [128, 2, 128], dt, n, stt)
                RB = dict(vn=R3("g_vn", BF16, 2), zs=R3("g_zs", F32, 1), o32=R3("g_o32", F32, 4),
                          junk=R3("g_junk", F32, 1), ssq=Ring(P, "g_ssq", [128, 8], F32, 2, stt),
                          zz=Ring(P, "g_zz", [128, 256], F32, 2, stt), t1=R3("g_t1", F32, 1), ob=R3("g_ob", BF16, 2))
                oT4r = Ring(P, "g_oT4", [128, 2, 512], BF16, 2, stt)
                P.op("pool", lambda e: e.memset(S32[:], 0.0), writes=[r_S32])
                P.op("pool", lambda e: e.memset(Sb[:], 0.0), writes=[r_Sb])
                st4 = {}
                done = {}
                ptdone = set()
                rfin = set()
                ofin = set()

                def pt_worker(ln_):
                    for _ in range(ln_ * GDN_STAGGER):
                        yield
                    for i in range(ln_, NT, NL):
                        while i >= NSLOT and (i - NSLOT) not in rfin:
                            yield
                        yield from pt_stage(kh, i, lanes[ln_], oslots[i % NSLOT])
                        ptdone.add(i)

                def r_worker():
                    for i in range(NT):
                        while i not in ptdone or (i >= 3 and (i - 3) not in ofin):
                            yield
                        yield from r_stage(kh, i, oslots[i % NSLOT], RB, done)
                        rfin.add(i)

                def o_worker():
                    for i in range(NT):
                        while i not in done:
                            yield
                        if i % 4 == 0:
                            st4["o"] = oT4r.next()
                        oT4, roT4 = st4["o"]
                        o32, ro32 = done[i]
                        yield from o_stage(kh, i, o32, ro32, wz, rwz, RB, oT4, roT4)
                        ofin.add(i)

                run_lanes([r_worker(), o_worker()] + [pt_worker(ln_) for ln_ in range(NL)])
            P.barrier()

        for kh in range(8):
            group(kh)
    P.barrier()
    with contextlib.ExitStack() as st:
        otr = Ring(P, "g_oTt", [128, 16, 128], BF16, 3, st)
        cur = {}

        def pre_tile(i):
            t, r = otr.next()
            P.dma(t[:], oTd[:, :, i * 128:(i + 1) * 128], reads=[r_oTd[i // 4]], writes=[r])
            cur["t"], cur["r"] = t, r
            return t, r

        out_proj(k, k.gdn_w_out[j], 16, None, None, x_src, pre_tile=pre_tile, nn=nn)


def mla_layer(k, j, x_src):
    P = k.P
    rc = k.r_const
    NB = S // 512
    scale = 192.0 ** -0.5
    with contextlib.ExitStack() as st:
        rope_table(k, st)
        cqn = P.sbuf("cqn", [128, 3, S], BF16, st)
        ckvn = P.sbuf("ckvn", [128, 2, S], BF16, st)
        k2 = P.sbuf("k2", [128, S], BF16, st)
        r_cq = [Res() for _ in range(NB)]
        r_ckv = [Res() for _ in range(NB)]
        r_k2 = [Res() for _ in range(NB)]
        qg = k.gainT[:, 64 + 3 * j:64 + 3 * j + 3]
        kvg = k.gainT[:, 70 + 2 * j:70 + 2 * j + 2]
        with contextlib.ExitStack() as st1:
            win = P.sbuf("m_win", [128, KC, 768], BF16, st1)
            r_win = Res()
            w_in = k.mla_w_in[j].rearrange("(c p) n -> p c n", p=128)
            for c in range(0, KC, 2):
                P.dma(win[:, c:c + 2, 0:704], w_in[:, c:c + 2, :], writes=[r_win], eng="pool")
            vs = win[:, :, 640:704].rearrange("p c (i two) -> p c i two", two=2)
            vd = win[:, :, 704:768].rearrange("p c (i two) -> p c i two", two=2)
            P.op("act", lambda e: e.mul(out=vd[:, :, :, 0], in_=vs[:, :, :, 1], mul=-1.0), writes=[r_win])
            P.op("act", lambda e: e.copy(out=vd[:, :, :, 1], in_=vs[:, :, :, 0]), writes=[r_win])
            sqr = Ring(P, "m_sq", [128, 512], BF16, 4, st1)
            lnr = Ring(P, "m_ln", [128, 512], F32, 2, st1)
            rsr = Ring(P, "m_rs", [128, 512], F32, 2, st1)
            prr = Ring(P, "m_pr", [128, 512], BF16, 2, st1)
            allb = [0, 1, 2, 3, 4, 5, 6, 7]

            def latent(tb, nch, col0, dst, rdst, gcol, width):
                t0 = tb * 512
                rb = k.r_big[tb]
                pqs = []
                sqs = []
                for c3 in range(nch):
                    pq, rpq = bank(k, "m1", allb)
                    for c in range(KC):
                        P.op("pe", lambda e, pq=pq, c=c, c3=c3: e.matmul(
                            pq[:, :], lhsT=win[:, c, col0 + c3 * 128:col0 + (c3 + 1) * 128],
                            rhs=k.bigA[:, c, t0:t0 + 512], start=(c == 0), stop=(c == KC - 1)),
                            reads=[r_win, rb], writes=[rpq])
                    sq, rsq = sqr.next()
                    P.op("act", lambda e, sq=sq, pq=pq: e.activation(out=sq[:], in_=pq[:, :], func=AF.Square),
                         writes=[rpq, rsq])
                    pqs.append((pq, rpq))
                    sqs.append((sq, rsq))
                pss, rpss = bank(k, "m1", allb)
                for c3 in range(nch):
                    P.op("pe", lambda e, c3=c3, sq=sqs[c3][0]: e.matmul(
                        pss[:, :], lhsT=k.onesb[:], rhs=sq[:], start=(c3 == 0), stop=(c3 == nch - 1)),
                        reads=[sqs[c3][1], rc], writes=[rpss])
                ln, rln = lnr.next()
                rs, rrs = rsr.next()
                P.op("act", lambda e: e.activation(out=ln[:], in_=pss[:, :], func=AF.Ln, scale=1.0 / width,
                                                   bias=k.epsc[:, 0:1]), reads=[rc], writes=[rpss, rln])
                P.op("act", lambda e: e.activation(out=rs[:], in_=ln[:], func=AF.Exp, scale=-0.5),
                     reads=[rln], writes=[rrs])
                for c3 in range(nch):
                    pq, rpq = pqs[c3]
                    P.op("dve", lambda e, pq=pq, c3=c3: e.scalar_tensor_tensor(
                        out=dst[:, c3, t0:t0 + 512], in0=pq[:, :], scalar=gcol[:, c3:c3 + 1], in1=rs[:],
                        op0=ALU.mult, op1=ALU.mult), reads=[rrs, rc], writes=[rpq, rdst[tb]])

            def rope_key(tb):
                t0 = tb * 512
                rb = k.r_big[tb]
                pk, rpk = bank(k, "m1", allb)
                for c in range(KC):
                    P.op("pe", lambda e, c=c: e.matmul(
                        pk[:, :], lhsT=win[:, c, 640:768], rhs=k.bigA[:, c, t0:t0 + 512],
                        start=(c == 0), stop=(c == KC - 1)), reads=[r_win, rb], writes=[rpk])
                pr, rpr = prr.next()
                P.op("dve", lambda e: e.tensor_tensor(out=pr[:], in0=pk[:, :], in1=k.cs[:, t0:t0 + 512],
                                                      op=ALU.mult), reads=[rc], writes=[rpk, rpr])
                pf, rpf = bank(k, "m1", allb)
                P.op("pe", lambda e: e.matmul(pf[:, :], lhsT=k.fold[:], rhs=pr[:], start=True, stop=True),
                     reads=[rpr, rc], writes=[rpf])
                P.op("act", lambda e: e.copy(out=k2[:, t0:t0 + 512], in_=pf[:, :]), writes=[rpf, r_k2[tb]])

            for tb in range(NB):
                latent(tb, 3, 0, cqn, r_cq, qg, 384.0)
                latent(tb, 2, 384, ckvn, r_ckv, kvg, 256.0)
                rope_key(tb)
        dump(k, "cs", k.cs[:, 0:512], [rc], F32)
        dump(k, "cqn", cqn[:, :, 0:512], [r_cq[0]])
        dump(k, "ckvn", ckvn[:, :, 0:512], [r_ckv[0]])
        dump(k, "k2", k2[:, 0:512], [r_k2[0]])
        P.barrier()
        with contextlib.ExitStack() as st2:
            wqr = Ring(P, "m_wq", [128, 3, 256], BF16, 2, st2)
            wkvr = Ring(P, "m_wkv", [128, 2, 256], BF16, 2, st2)
            qn = P.sbuf("m_qn", [128, S], BF16, st2)
            q2 = P.sbuf("m_q2", [128, S], BF16, st2)
            kn = P.sbuf("m_kn", [128, S], BF16, st2)
            vv = P.sbuf("m_v", [128, NT, 128], BF16, st2)
            r_qn = [Res() for _ in range(NB)]
            r_q2 = [Res() for _ in range(NB)]
            r_kn = [Res() for _ in range(NB)]
            r_v = [Res() for _ in range(NB)]
            ptr = Ring(P, "m_pT", [128, 512], BF16, 5, st2)
            rir = Ring(P, "m_ri", [128, 512], F32, 2, st2)
            accr = Ring(P, "m_acc", [128, 512], F32, 4, st2)
            wqu = k.mla_w_q_up[j].rearrange("(c p) n -> p c n", p=128)
            wkvu = k.mla_w_kv_up[j].rearrange("(c p) n -> p c n", p=128)
            pb = [4, 5, 6, 7]

            def head_proj(h, tb, wq, rwq, wkv, rwkv):
                t0 = tb * 512
                p1, rp1 = bank(k, "mp", pb)
                for c3 in range(3):
                    P.op("pe", lambda e, c3=c3: e.matmul(
                        p1[:, :], lhsT=wq[:, c3, 0:128], rhs=cqn[:, c3, t0:t0 + 512], start=(c3 == 0), stop=(c3 == 2)),
                        reads=[rwq, r_cq[tb]], writes=[rp1])
                P.op("act", lambda e: e.copy(out=qn[:, t0:t0 + 512], in_=p1[:, :]), writes=[rp1, r_qn[tb]])
                p2, rp2 = bank(k, "mp", pb)
                for c3 in range(3):
                    P.op("pe", lambda e, c3=c3: e.matmul(
                        p2[:, :], lhsT=wq[:, c3, 128:256], rhs=cqn[:, c3, t0:t0 + 512], start=(c3 == 0), stop=(c3 == 2)),
                        reads=[rwq, r_cq[tb]], writes=[rp2])
                P.op("dve", lambda e: e.tensor_tensor(out=q2[:, t0:t0 + 512], in0=p2[:, :],
                                                      in1=k.cs[:, t0:t0 + 512], op=ALU.mult),
                     reads=[rc], writes=[rp2, r_q2[tb]])
                p3, rp3 = bank(k, "mp", pb)
                for c2 in range(2):
                    P.op("pe", lambda e, c2=c2: e.matmul(
                        p3[:, :], lhsT=wkv[:, c2, 0:128], rhs=ckvn[:, c2, t0:t0 + 512], start=(c2 == 0), stop=(c2 == 1)),
                        reads=[rwkv, r_ckv[tb]], writes=[rp3])
                P.op("act", lambda e: e.copy(out=kn[:, t0:t0 + 512], in_=p3[:, :]), writes=[rp3, r_kn[tb]])
                p4, rp4 = bank(k, "mp", pb)
                for tt in range(4):
                    for c2 in range(2):
                        P.op("pe", lambda e, c2=c2, tt=tt: e.matmul(
                            p4[:, tt * 128:(tt + 1) * 128], lhsT=ckvn[:, c2, t0 + tt * 128:t0 + (tt + 1) * 128],
                            rhs=wkv[:, c2, 128:256], start=(c2 == 0), stop=(c2 == 1)),
                            reads=[rwkv, r_ckv[tb]], writes=[rp4])
                P.op("dve", lambda e: e.tensor_copy(
                    out=vv[:, tb * 4:(tb + 1) * 4, :].rearrange("p a b -> p (a b)"), in_=p4[:, :]),
                    writes=[rp4, r_v[tb]])

            def attn_qk(h, qb, kt, nkt, accs):
                c0 = max(0, kt * 128 - qb * 512)
                w = 512 - c0
                q0 = qb * 512 + c0
                pS, rpS = bank(k, "mp", pb)
                kb = kt // 4
                P.op("pe", lambda e: e.matmul(
                    pS[:, c0:512], lhsT=kn[:, kt * 128:(kt + 1) * 128], rhs=qn[:, q0:q0 + w], start=True, stop=False),
                    reads=[r_kn[kb], r_qn[qb]], writes=[rpS])
                P.op("pe", lambda e: e.matmul(
                    pS[:, c0:512], lhsT=k2[:, kt * 128:(kt + 1) * 128], rhs=q2[:, q0:q0 + w], start=False, stop=True),
                    reads=[r_k2[kb], r_q2[qb]], writes=[rpS])
                pT, rpT = ptr.next()
                P.op("act", lambda e: e.activation(out=pT[:, c0:512], in_=pS[:, c0:512], func=AF.Exp, scale=scale),
                     writes=[rpS, rpT])
                if kt >= 4 * qb:
                    P.op("pool", lambda e: e.memset(pT[64:128, c0:c0 + 64], 0.0), writes=[rpT])
                acc, racc = accs[kt % 2]
                eng = ("pool", "dve")[kt % 2]
                if kt < 2:
                    if c0 > 0:
                        P.op(eng, lambda e: e.memset(acc[:, 0:c0], 0.0), writes=[racc])
                    P.op(eng, lambda e: e.tensor_copy(out=acc[:, c0:512], in_=pT[:, c0:512]), reads=[rpT], writes=[racc])
                else:
                    P.op(eng, lambda e: e.tensor_tensor(out=acc[:, c0:512], in0=acc[:, c0:512], in1=pT[:, c0:512], op=ALU.add),
                         reads=[rpT], writes=[racc])
                return pT, rpT, c0

            def attn_pv(kt, nkt, po, rpo, pT, rpT, c0):
                kb = kt // 4
                P.op("pe", lambda e: e.matmul(
                    po[:, c0:512], lhsT=vv[:, kt, :], rhs=pT[:, c0:512], start=(kt == 0), stop=(kt == nkt - 1)),
                    reads=[r_v[kb], rpT], writes=[rpo])

            def attn_block(h, qb):
                po, rpo = bank(k, "ao", [0, 1])
                prs, rprs = bank(k, "ar", [2, 3])
                nkt = 4 * qb + 4
                accs = [accr.next(), accr.next()]
                LA = 2
                pend = {}
                for idx in range(nkt + LA):
                    if idx < nkt:
                        pend[idx] = attn_qk(h, qb, idx, nkt, accs)
                    if idx >= LA:
                        attn_pv(idx - LA, nkt, po, rpo, *pend.pop(idx - LA))
                P.op("dve", lambda e: e.tensor_tensor(out=accs[0][0][:], in0=accs[0][0][:], in1=accs[1][0][:], op=ALU.add),
                     reads=[accs[1][1]], writes=[accs[0][1]])
                P.op("pe", lambda e: e.matmul(prs[:, :], lhsT=k.onesf[:], rhs=accs[0][0][:], start=True, stop=True),
                     reads=[rc, accs[0][1]], writes=[rprs])
                ri, rri = rir.next()
                P.op("dve", lambda e: e.reciprocal(out=ri[:], in_=prs[:, :]), writes=[rprs, rri])
                P.op("dve", lambda e: e.tensor_tensor(
                    out=k.bigA[:, h, qb * 512:(qb + 1) * 512], in0=po[:, :], in1=ri[:], op=ALU.mult),
                    reads=[rri], writes=[rpo, k.r_big[qb]])

            def head(h):
                wq, rwq = wqr.next()
                wkv, rwkv = wkvr.next()
                P.dma(wq[:, :, 0:192], wqu[:, :, h * 192:(h + 1) * 192], writes=[rwq], eng="pool")
                P.dma(wkv[:, :, :], wkvu[:, :, h * 256:(h + 1) * 256], writes=[rwkv], eng="pool")
                vs = wq[:, :, 128:192].rearrange("p c (i two) -> p c i two", two=2)
                vd = wq[:, :, 192:256].rearrange("p c (i two) -> p c i two", two=2)
                P.op("act", lambda e: e.mul(out=vd[:, :, :, 0], in_=vs[:, :, :, 1], mul=-1.0), writes=[rwq])
                P.op("act", lambda e: e.copy(out=vd[:, :, :, 1], in_=vs[:, :, :, 0]), writes=[rwq])
                for tb in range(NB):
                    head_proj(h, tb, wq, rwq, wkv, rwkv)
                if h == 0:
                    dump(k, "qn", qn[:, 0:512], [r_qn[0]])
                    dump(k, "q2", q2[:, 0:512], [r_q2[0]])
                    dump(k, "kn", kn[:, 0:512], [r_kn[0]])
                    dump(k, "vv", vv[:, 0:4, :], [r_v[0]])
                for qb in range(NB):
                    attn_block(h, qb)

            for h in range(8):
                head(h)
        dump(k, "oT", k.bigA[:, :, 0:1024], [k.r_big[0], k.r_big[1]])
        P.barrier()
        out_proj(k, k.mla_w_out[j], 8, lambda h, i: k.bigA[:, h, i * 128:(i + 1) * 128],
                 lambda i: [k.r_big[i // 4]], x_src)


def out_proj(k, w_dram, nch, lhs_fn, lhs_res_fn, x_src, pre_tile=None, nn=None):
    P = k.P
    with contextlib.ExitStack() as st:
        wo = P.sbuf("wo", [128, nch, D], BF16, st)
        r_wo = Res()
        wv = w_dram.rearrange("(c p) n -> p c n", p=128)
        for c in range(0, nch, 2):
            P.dma(wo[:, c:c + 2, :], wv[:, c:c + 2, :], writes=[r_wo], eng="pool")
        xr = Ring(P, "ox", [128, D], F32, 3, st)
        NR = setup_next_norm(k, nn, st)

        def tile(i):
            if pre_tile is not None:
                lt, lr = pre_tile(i)
                lhs = lambda c: lt[:, c, :]
                lres = [lr]
            else:
                lhs = lambda c: lhs_fn(c, i)
                lres = lhs_res_fn(i)
            xt, rx = xr.next()
            P.dma(xt[:], x_src[i * 128:(i + 1) * 128, :], reads=k.xr(x_src, i), writes=[rx])
            for dh in range(2):
                po, rpo = bank(k, "op", [0, 1, 2, 3])
                for c in range(nch):
                    P.op("pe", lambda e, po=po, c=c, dh=dh: e.matmul(
                        po[:, :], lhsT=lhs(c), rhs=wo[:, c, dh * 512:(dh + 1) * 512],
                        start=(c == 0), stop=(c == nch - 1)), reads=[r_wo] + lres, writes=[rpo])
                P.op("dve", lambda e, po=po, dh=dh: e.tensor_tensor(
                    out=xt[:, dh * 512:(dh + 1) * 512], in0=po[:, :], in1=xt[:, dh * 512:(dh + 1) * 512], op=ALU.add),
                    writes=[rpo, rx])
            P.dma(k.xres[i * 128:(i + 1) * 128, :], xt[:], reads=[rx], writes=[k.r_xres[i]])
            if nn is not None:
                norm_tile(k, xt, rx, i, nn, NR, [4, 5])

        for i in range(NT):
            tile(i)


def _invf_table():
    inv_freq = (10000.0 ** (-np.arange(0, 64, 2, dtype=np.float32) / 64.0)).astype(np.float32)
    t = np.zeros((128, 2), np.float32)
    for p in range(128):
        t[p, 0] = inv_freq[(p % 64) // 2] / (2.0 * math.pi)
        t[p, 1] = 0.25 if p < 64 else 0.0
    return t


def _gdn_masks():
    i = np.arange(128)[:, None]
    j = np.arange(128)[None, :]
    m = np.zeros((128, 8, 128), np.float32)
    m0 = ((i // 2 == j // 2) & (i > j)).astype(np.float32)
    m[:, 0, :] = m0
    m[:, 1, :] = m0.T
    for lv, b in enumerate((2, 4, 8, 16, 32, 64)):
        ma = ((i // (2 * b) == j // (2 * b)) & ((i // b) % 2 == 1) & ((j // b) % 2 == 0)).astype(np.float32)
        m[:, 2 + lv, :] = ma
    return m


_NC_CACHE = {}


def kernel(**inputs):
    key = "full"
    if key not in _NC_CACHE:
        _NC_CACHE[key] = build()
    nc = _NC_CACHE[key]
    n = 8
    shared = {}
    for name, v in inputs.items():
        if name in ("x", "positions"):
            continue
        a = np.ascontiguousarray(np.asarray(v))
        if name == "final_norm":
            a = a.reshape(1, D)
        shared[name] = a
    shared["invf"] = _invf_table()
    shared["gmask"] = _gdn_masks()
    x = np.asarray(inputs["x"])
    pos = np.asarray(inputs["positions"])
    in_maps = []
    for i in range(n):
        m = dict(shared)
        m["x"] = np.ascontiguousarray(x[i])
        m["positions"] = np.ascontiguousarray(pos[i].reshape(1, S).astype(np.int32))
        in_maps.append(m)
    res = run_bass_kernel_spmd(nc, in_maps, core_ids=list(range(n)))
    _NC_CACHE["last"] = res
    return np.stack([np.asarray(res.results[i]["out"]) for i in range(n)], axis=0).astype(np.float32)
```

```python
import contextlib
import math
import numpy as np
import concourse.bass as bass
import concourse.mybir as mybir
from concourse.alu_op_type import AluOpType as ALU
from concourse.bass_utils import run_bass_kernel_spmd

AF = mybir.ActivationFunctionType
F32 = mybir.dt.float32
BF16 = mybir.dt.bfloat16
I32 = mybir.dt.int32

SEM_LIM = 30000
N_DMA_SEMS = 24

S = 4096
D = 1024
NT = S // 128
KC = D // 128
DEPTH = 4
EPS = 1e-6
D_FF = 2816
FC = D_FF // 128
GDN_IN = 6176
MLA_IN = 704


class Res:
    __slots__ = ("last_w", "readers")

    def __init__(self):
        self.last_w = None
        self.readers = []


class Op:
    __slots__ = ("eng", "fn", "deps", "is_dma", "signal", "sem", "val", "dsem", "dprev")

    def __init__(self, eng, fn, is_dma):
        self.eng = eng
        self.fn = fn
        self.is_dma = is_dma
        self.deps = ()
        self.signal = False
        self.sem = None
        self.val = 0
        self.dsem = None
        self.dprev = 0


class Prog:
    ENGS = ("pe", "act", "dve", "pool", "sp")

    def __init__(self, nc):
        self.nc = nc
        self.ops = {e: [] for e in self.ENGS}
        self.stack = contextlib.ExitStack()
        self.dma_cnt = [0] * N_DMA_SEMS
        self.dma_last = [None] * N_DMA_SEMS
        self.dma_rr = 0
        self.uid = 0

    def sbuf(self, name, shape, dtype, stack=None):
        self.uid += 1
        return (stack or self.stack).enter_context(
            self.nc.sbuf_tensor(f"{name}_{self.uid}", list(shape), dtype))

    def psum(self, name, shape, dtype):
        return self.stack.enter_context(self.nc.psum_tensor(name, list(shape), dtype))

    def op(self, eng, fn, reads=(), writes=(), dma=False, extra_deps=()):
        o = Op(eng, fn, dma)
        deps = set(extra_deps)
        for r in reads:
            if r.last_w is not None:
                deps.add(r.last_w)
        for r in writes:
            if r.last_w is not None:
                deps.add(r.last_w)
            deps.update(r.readers)
        for r in reads:
            r.readers.append(o)
        for r in writes:
            r.last_w = o
            r.readers = []
        deps.discard(o)
        if eng == "pe":
            deps = {d for d in deps if d.is_dma or d.eng != "pe"}
        o.deps = deps
        for d in deps:
            d.signal = True
        if dma:
            k = self.dma_rr
            self.dma_rr = (self.dma_rr + 1) % N_DMA_SEMS
            o.dprev = self.dma_cnt[k] * 16
            self.dma_cnt[k] += 1
            o.dsem = k
            o.val = self.dma_cnt[k] * 16
            self.dma_last[k] = o
        self.ops[eng].append(o)
        return o

    def dma(self, out, in_, reads=(), writes=(), eng="sp", **kw):
        return self.op(eng, lambda e: e.dma_start(out=out, in_=in_, **kw), reads, writes, dma=True)

    def barrier(self):
        lasts = []
        for e in self.ENGS:
            for o in reversed(self.ops[e]):
                if not o.is_dma and o.fn is not None:
                    lasts.append(o)
                    break
        lasts += [o for o in self.dma_last if o is not None]
        for e in self.ENGS:
            self.op(e, None, extra_deps=lasts)

    def emit(self, final_wait_ops=()):
        nc = self.nc
        st = self.stack
        nsem = {}
        for e in self.ENGS:
            cnt = 0
            for o in self.ops[e]:
                if o.is_dma or o.fn is None:
                    continue
                if o.signal:
                    o.sem = (e, cnt // SEM_LIM)
                    o.val = cnt % SEM_LIM + 1
                    cnt += 1
            nsem[e] = (cnt + SEM_LIM - 1) // SEM_LIM
        sems = {}
        for e in self.ENGS:
            for k in range(nsem[e]):
                sems[(e, k)] = st.enter_context(nc.semaphore(f"s_{e}{k}"))
        dsems = [st.enter_context(nc.semaphore(f"s_dma{k}")) for k in range(N_DMA_SEMS)]
        block = st.enter_context(nc.Block())
        battr = {"pe": "tensor", "act": "scalar", "dve": "vector", "pool": "gpsimd", "sp": "sync"}

        def run_engine(ename, eng):
            waited = {}
            for o in self.ops[ename]:
                need = {}
                for d in o.deps:
                    if d.is_dma:
                        key = ("d", d.dsem)
                        s = dsems[d.dsem]
                    elif d.fn is None:
                        continue
                    else:
                        key = d.sem
                        s = sems[d.sem]
                    if need.get(key, (None, 0))[1] < d.val:
                        need[key] = (s, d.val)
                if o.is_dma and o.dprev > 0:
                    key = ("d", o.dsem)
                    if need.get(key, (None, 0))[1] < o.dprev:
                        need[key] = (dsems[o.dsem], o.dprev)
                for key, (s, v) in need.items():
                    if waited.get(key, 0) < v:
                        eng.wait_ge(s, v)
                        waited[key] = v
                if o.fn is None:
                    continue
                ins = o.fn(eng)
                if o.is_dma:
                    ins.then_inc(dsems[o.dsem], 16)
                elif o.signal:
                    ins.then_inc(sems[o.sem], 1)
            if ename == "sp":
                for o in final_wait_ops:
                    eng.wait_ge(dsems[o.dsem], o.val)

        for ename in self.ENGS:
            def mk(ename=ename):
                def body(eng):
                    run_engine(ename, eng)
                return body
            getattr(block, battr[ename])(mk())


class Ring:
    def __init__(self, P, name, shape, dtype, n, stack=None):
        self.t = [P.sbuf(f"{name}{i}", shape, dtype, stack) for i in range(n)]
        self.r = [Res() for _ in range(n)]
        self.i = 0
        self.n = n

    def next(self):
        k = self.i
        self.i = (self.i + 1) % self.n
        return self.t[k], self.r[k]


class K:
    pass


def dump(k, name, ap, reads, dt=BF16):
    if not getattr(k, "dbg", False):
        return
    shape = list(ap.shape)
    t = k.nc.dram_tensor("dbg_" + name, shape, F32, kind="ExternalOutput").ap()
    k.dbg_ops.append(k.P.dma(t, ap, reads=reads, eng="pool"))


def build(n_layers=DEPTH, do_mix=True, do_ffn=True, dbg=False):
    nc = bass.Bass("TRN2", target_bir_lowering=False)
    P = Prog(nc)
    k = K()
    k.nc, k.P = nc, P
    k.dbg = dbg
    k.dbg_ops = []

    def din(name, shape, dt=F32):
        return nc.dram_tensor(name, list(shape), dt, kind="ExternalInput").ap()

    k.x_in = din("x", [S, D])
    k.pos = din("positions", [1, S], I32)
    k.norm_mix = din("norm_mix", [DEPTH, D])
    k.norm_ffn = din("norm_ffn", [DEPTH, D])
    k.gdn_w_in = din("gdn_w_in", [2, D, GDN_IN])
    k.gdn_conv_w = din("gdn_conv_w", [2, 4, 4096])
    k.gdn_a_log = din("gdn_a_log", [2, 16])
    k.gdn_dt_bias = din("gdn_dt_bias", [2, 16])
    k.gdn_out_norm = din("gdn_out_norm", [2, 128])
    k.gdn_w_out = din("gdn_w_out", [2, 2048, D])
    k.mla_w_in = din("mla_w_in", [2, D, MLA_IN])
    k.mla_q_norm = din("mla_q_norm", [2, 384])
    k.mla_w_q_up = din("mla_w_q_up", [2, 384, 1536])
    k.mla_kv_norm = din("mla_kv_norm", [2, 256])
    k.mla_w_kv_up = din("mla_w_kv_up", [2, 256, 2048])
    k.mla_w_out = din("mla_w_out", [2, D, D])
    k.ffn_w_gate_up = din("ffn_w_gate_up", [DEPTH, D, 2 * D_FF])
    k.ffn_w_down = din("ffn_w_down", [DEPTH, D_FF, D])
    k.final_norm = din("final_norm", [1, D])
    k.invf = din("invf", [128, 2])
    k.gmask = din("gmask", [128, 8, 128])
    k.out = nc.dram_tensor("out", [S, D], F32, kind="ExternalOutput").ap()
    k.xres = nc.dram_tensor("xres", [S, D], F32, kind="Internal").ap()
    k.oT_d = nc.dram_tensor("oT_d", [2048, S], BF16, kind="Internal").ap()
    k.r_xres = [Res() for _ in range(NT)]
    k.xr = lambda src, i: [k.r_xres[i]] if src is k.xres else []

    with P.stack:
        setup_consts(k)
        x_src = k.x_in
        k.final_ops = []
        fuse = (do_mix is True) and do_ffn
        hT_ready = False
        for L in range(n_layers):
            if do_mix is True or (do_mix is not False and do_mix == L % 2):
                if not hT_ready:
                    rms_to_hT(k, x_src, k.gainT[:, L * KC:(L + 1) * KC])
                    P.barrier()
                hT_ready = False
                if L % 2 == 0:
                    nn = ("hT", k.gainT[:, (4 + L) * KC:(5 + L) * KC]) if fuse else None
                    gdn_layer(k, L // 2, x_src, nn)
                    hT_ready = nn is not None
                else:
                    mla_layer(k, L // 2, x_src)
                P.barrier()
                x_src = k.xres
            if do_ffn:
                if not hT_ready:
                    rms_to_hT(k, x_src, k.gainT[:, (4 + L) * KC:(5 + L) * KC])
                    P.barrier()
                hT_ready = False
                nn = None
                if fuse:
                    nn = ("final",) if L == n_layers - 1 else ("hT", k.gainT[:, (L + 1) * KC:(L + 2) * KC])
                ffn_layer(k, L, x_src, nn)
                hT_ready = nn is not None and nn[0] == "hT"
                P.barrier()
                x_src = k.xres
        if fuse:
            outs = k.final_ops
        else:
            outs = final_norm(k, x_src)
        P.emit(final_wait_ops=outs + k.dbg_ops)
    return nc


def setup_consts(k):
    P, nc = k.P, k.nc
    k.ps = [P.psum(f"ps{i}", [128, 512], F32) for i in range(8)]
    k.rps = [Res() for _ in range(8)]
    k.bank_rr = {}
    k.identf = P.sbuf("identf", [128, 128], F32)
    k.ident = P.sbuf("ident", [128, 128], BF16)
    k.onesf = P.sbuf("onesf", [128, 128], F32)
    k.negonesf = P.sbuf("negonesf", [128, 128], F32)
    k.onesb = P.sbuf("onesb", [128, 128], BF16)
    k.r_const = Res()
    rc = k.r_const
    P.op("pool", lambda e: e.memset(k.identf[:], 0.0), writes=[rc])
    P.op("pool", lambda e: e.affine_select(out=k.identf[:], in_=k.identf[:], pattern=[[1, 128]],
                                           compare_op=ALU.not_equal, fill=1.0, base=0,
                                           channel_multiplier=-1), writes=[rc])
    P.op("dve", lambda e: e.tensor_copy(out=k.ident[:], in_=k.identf[:]), reads=[rc], writes=[rc])
    P.op("pool", lambda e: e.memset(k.onesf[:], 1.0), writes=[rc])
    P.op("pool", lambda e: e.memset(k.negonesf[:], -1.0), writes=[rc])
    P.op("pool", lambda e: e.memset(k.onesb[:], 1.0), writes=[rc])
    k.gainT = P.sbuf("gainT", [128, 74], F32)
    g_raw = P.sbuf("g_raw", [74, 128], F32)
    r_g = Res()
    P.dma(g_raw[0:32, :], k.norm_mix.rearrange("r (c p) -> (r c) p", p=128), writes=[r_g])
    P.dma(g_raw[32:64, :], k.norm_ffn.rearrange("r (c p) -> (r c) p", p=128), writes=[r_g])
    P.dma(g_raw[64:70, :], k.mla_q_norm.rearrange("r (c p) -> (r c) p", p=128), writes=[r_g])
    P.dma(g_raw[70:74, :], k.mla_kv_norm.rearrange("r (c p) -> (r c) p", p=128), writes=[r_g])
    P.op("pe", lambda e: e.transpose(out=k.ps[0][:, 0:74], in_=g_raw[:, :], identity=k.identf[0:74, 0:74]),
         reads=[r_g, rc], writes=[k.rps[0]])
    P.op("dve", lambda e: e.tensor_copy(out=k.gainT[:], in_=k.ps[0][:, 0:74]), writes=[k.rps[0], rc])
    k.bigA = P.sbuf("bigA", [128, KC, S], BF16)
    k.r_big = [Res() for _ in range(S // 512)]
    k.m05 = P.sbuf("m05", [128, 1], F32)
    P.op("pool", lambda e: e.memset(k.m05[:], -0.5), writes=[rc])
    k.epsc = P.sbuf("epsc", [128, 1], F32)
    P.op("pool", lambda e: e.memset(k.epsc[:], EPS), writes=[rc])
    k.fold = P.sbuf("fold", [128, 128], BF16)
    for (a, b) in ((0, 0), (64, 64), (0, 64), (64, 0)):
        P.op("dve", lambda e, a=a, b=b: e.tensor_copy(out=k.fold[a:a + 64, b:b + 64], in_=k.identf[a:a + 64, a:a + 64]),
             reads=[rc], writes=[rc])


def rope_table(k, st_out):
    P = k.P
    rc = k.r_const
    k.cs = P.sbuf("cs", [128, S], F32, st_out)
    with contextlib.ExitStack() as st:
        invf = P.sbuf("invf", [128, 2], F32, st)
        posi = P.sbuf("posi", [128, S], I32, st)
        t = P.sbuf("rt", [128, S], F32, st)
        ti = P.sbuf("rti", [128, S], I32, st)
        m = P.sbuf("rm", [128, S], F32, st)
        tf = m
        r = Res()
        P.dma(invf[:], k.invf, writes=[r])
        P.dma(posi[:], k.pos.partition_broadcast(128), writes=[r])
        P.op("dve", lambda e: e.tensor_copy(out=t[:], in_=posi[:]), writes=[r])
        P.op("dve", lambda e: e.tensor_scalar(out=t[:], in0=t[:], scalar1=invf[:, 0:1], scalar2=invf[:, 1:2],
                                              op0=ALU.mult, op1=ALU.add), writes=[r])
        P.op("dve", lambda e: e.tensor_copy(out=ti[:], in_=t[:]), writes=[r])
        P.op("dve", lambda e: e.tensor_copy(out=tf[:], in_=ti[:]), writes=[r])
        P.op("dve", lambda e: e.tensor_tensor(out=t[:], in0=t[:], in1=tf[:], op=ALU.subtract), writes=[r])
        P.op("dve", lambda e: e.tensor_scalar(out=m[:], in0=t[:], scalar1=0.5, scalar2=None, op0=ALU.is_gt), writes=[r])
        P.op("dve", lambda e: e.tensor_tensor(out=t[:], in0=t[:], in1=m[:], op=ALU.subtract), writes=[r])
        P.op("dve", lambda e: e.tensor_scalar(out=m[:], in0=t[:], scalar1=-0.5, scalar2=None, op0=ALU.is_lt), writes=[r])
        P.op("dve", lambda e: e.tensor_tensor(out=t[:], in0=t[:], in1=m[:], op=ALU.add), writes=[r])
        P.op("act", lambda e: e.activation(out=k.cs[:], in_=t[:], func=AF.Sin, scale=6.283185), reads=[r], writes=[rc])
    P.barrier()


def bank(k, cls, banks):
    i = k.bank_rr.get(cls, 0)
    k.bank_rr[cls] = i + 1
    b = banks[i % len(banks)]
    return k.ps[b], k.rps[b]


def rms_to_hT(k, x_src, gT):
    P = k.P
    rc = k.r_const
    with contextlib.ExitStack() as st:
        xr = Ring(P, "nx", [128, D], F32, 3, st)
        sqr = Ring(P, "nsq", [128, D], F32, 2, st)
        ybr = Ring(P, "nyb", [128, D], BF16, 2, st)
        ssr = Ring(P, "nss", [128, 4], F32, 4, st)
        for i in range(NT):
            xt, rx = xr.next()
            sq, rsq = sqr.next()
            yb, ryb = ybr.next()
            ss, rss = ssr.next()
            P.dma(xt[:], x_src[i * 128:(i + 1) * 128, :], reads=k.xr(x_src, i), writes=[rx])
            P.op("act", lambda e, sq=sq, xt=xt, ss=ss: e.activation(
                out=sq[:], in_=xt[:], func=AF.Square, accum_out=ss[:, 0:1]), reads=[rx], writes=[rsq, rss])
            P.op("dve", lambda e, ss=ss: e.tensor_scalar(out=ss[:, 1:2], in0=ss[:, 0:1], scalar1=1.0 / D,
                                                         scalar2=EPS, op0=ALU.mult, op1=ALU.add),
                 reads=[rss], writes=[rss])
            P.op("pool", lambda e, ss=ss: e.tensor_tensor(out=ss[:, 2:3], in0=ss[:, 1:2], in1=k.m05[:],
                                                          op=ALU.pow), reads=[rss, rc], writes=[rss])
            P.op("act", lambda e, yb=yb, xt=xt, ss=ss: e.activation(out=yb[:], in_=xt[:], func=AF.Copy,
                                                                    scale=ss[:, 2:3]),
                 reads=[rx, rss], writes=[ryb])
            pt, rp = bank(k, "n", [0, 1])
            psb = pt[:].bitcast(BF16)
            for c in range(KC):
                P.op("pe", lambda e, c=c, psb=psb, yb=yb: e.transpose(
                    out=psb[:, c * 128:(c + 1) * 128], in_=yb[:, c * 128:(c + 1) * 128], identity=k.ident[:]),
                    reads=[ryb, rc], writes=[rp])
            P.op("dve", lambda e, i=i, psb=psb: e.tensor_tensor(
                out=k.bigA[:, :, i * 128:(i + 1) * 128],
                in0=psb[:, 0:D].rearrange("p (c n) -> p c n", c=KC),
                in1=gT.unsqueeze(2).to_broadcast([128, KC, 128]), op=ALU.mult),
                reads=[rc], writes=[rp, k.r_big[i // 4]])


def final_norm(k, x_src):
    P = k.P
    rc = k.r_const
    outs = []
    with contextlib.ExitStack() as st:
        k.gfin = P.sbuf("gfin", [128, D], F32, st)
        P.dma(k.gfin[:], k.final_norm.partition_broadcast(128), writes=[rc])
        xr = Ring(P, "fx", [128, D], F32, 3, st)
        sqr = Ring(P, "fsq", [128, D], F32, 2, st)
        yr = Ring(P, "fy", [128, D], F32, 3, st)
        ssr = Ring(P, "fss", [128, 4], F32, 4, st)
        for i in range(NT):
            xt, rx = xr.next()
            sq, rsq = sqr.next()
            yt, ry = yr.next()
            ss, rss = ssr.next()
            P.dma(xt[:], x_src[i * 128:(i + 1) * 128, :], reads=k.xr(x_src, i), writes=[rx])
            P.op("act", lambda e, sq=sq, xt=xt, ss=ss: e.activation(
                out=sq[:], in_=xt[:], func=AF.Square, accum_out=ss[:, 0:1]), reads=[rx], writes=[rsq, rss])
            P.op("dve", lambda e, ss=ss: e.tensor_scalar(out=ss[:, 1:2], in0=ss[:, 0:1], scalar1=1.0 / D,
                                                         scalar2=EPS, op0=ALU.mult, op1=ALU.add),
                 reads=[rss], writes=[rss])
            P.op("pool", lambda e, ss=ss: e.tensor_tensor(out=ss[:, 2:3], in0=ss[:, 1:2], in1=k.m05[:],
                                                          op=ALU.pow), reads=[rss, rc], writes=[rss])
            P.op("dve", lambda e, yt=yt, xt=xt, ss=ss: e.scalar_tensor_tensor(
                out=yt[:], in0=xt[:], scalar=ss[:, 2:3], in1=k.gfin[:], op0=ALU.mult, op1=ALU.mult),
                reads=[rx, rss, rc], writes=[ry])
            outs.append(P.dma(k.out[i * 128:(i + 1) * 128, :], yt[:], reads=[ry]))
    return outs


def norm_rings(k, st):
    P = k.P
    return dict(sq=Ring(P, "nf_sq", [128, D], BF16, 1, st), yb=Ring(P, "nf_yb", [128, D], BF16, 2, st),
                ss=Ring(P, "nf_ss", [128, 4], F32, 4, st))


def norm_tile(k, xt, rx, i, nn, NR, banks):
    P = k.P
    rc = k.r_const
    sq, rsq = NR["sq"].next()
    ss, rss = NR["ss"].next()
    P.op("act", lambda e: e.activation(out=sq[:], in_=xt[:], func=AF.Square, accum_out=ss[:, 0:1]),
         reads=[rx], writes=[rsq, rss])
    P.op("dve", lambda e: e.tensor_scalar(out=ss[:, 1:2], in0=ss[:, 0:1], scalar1=1.0 / D, scalar2=EPS,
                                          op0=ALU.mult, op1=ALU.add), writes=[rss])
    P.op("pool", lambda e: e.tensor_tensor(out=ss[:, 2:3], in0=ss[:, 1:2], in1=k.m05[:], op=ALU.pow),
         reads=[rc], writes=[rss])
    if nn[0] == "final":
        yt, ry = NR["y"].next()
        P.op("dve", lambda e: e.scalar_tensor_tensor(out=yt[:], in0=xt[:], scalar=ss[:, 2:3], in1=NR["gfin"][:],
                                                     op0=ALU.mult, op1=ALU.mult), reads=[rx, rss, rc], writes=[ry])
        k.final_ops.append(P.dma(k.out[i * 128:(i + 1) * 128, :], yt[:], reads=[ry]))
        return
    gT = nn[1]
    yb, ryb = NR["yb"].next()
    P.op("act", lambda e: e.activation(out=yb[:], in_=xt[:], func=AF.Copy, scale=ss[:, 2:3]),
         reads=[rx, rss], writes=[ryb])
    pt, rp = bank(k, "nf", banks)
    psb = pt[:].bitcast(BF16)
    for c in range(KC):
        P.op("pe", lambda e, c=c: e.transpose(out=psb[:, c * 128:(c + 1) * 128], in_=yb[:, c * 128:(c + 1) * 128],
                                              identity=k.ident[:]), reads=[ryb, rc], writes=[rp])
    P.op("dve", lambda e: e.tensor_tensor(
        out=k.bigA[:, :, i * 128:(i + 1) * 128], in0=psb[:, 0:D].rearrange("p (c n) -> p c n", c=KC),
        in1=gT.unsqueeze(2).to_broadcast([128, KC, 128]), op=ALU.mult),
        reads=[rc], writes=[rp, k.r_big[i // 4]])


def setup_next_norm(k, nn, st):
    if nn is None:
        return None
    NR = norm_rings(k, st)
    if nn[0] == "final":
        P = k.P
        NR["gfin"] = P.sbuf("nf_gfin", [128, D], F32, st)
        P.dma(NR["gfin"][:], k.final_norm.partition_broadcast(128), writes=[k.r_const])
        NR["y"] = Ring(P, "nf_y", [128, D], F32, 2, st)
    return NR


FFN_TB = 1024
GDN_LANES = 6
GDN_STAGGER = 9


def ffn_layer(k, L, x_src, nn=None):
    P = k.P
    wgu = k.ffn_w_gate_up[L].rearrange("(c p) n -> p c n", p=128)
    wdn = k.ffn_w_down[L].rearrange("(c p) n -> p c n", p=128)
    with contextlib.ExitStack() as st:
        wd = P.sbuf("wd", [128, FC, D], BF16, st)
        r_wd = Res()
        actT = P.sbuf("actT", [128, FC, FFN_TB], BF16, st)
        r_act = [Res() for _ in range(FFN_TB // 512)]
        wr = Ring(P, "wgu", [128, KC, 256], BF16, 4, st)
        pre_w = []
        for j in range(3):
            wt, rw = wr.next()
            P.dma(wt[:, :, 0:128], wgu[:, :, j * 128:(j + 1) * 128], writes=[rw], eng="pool")
            P.dma(wt[:, :, 128:256], wgu[:, :, D_FF + j * 128:D_FF + (j + 1) * 128], writes=[rw], eng="pool")
            pre_w.append((wt, rw))
        sgr = Ring(P, "sg", [128, 512], F32, 2, st)
        xr = Ring(P, "fx", [128, D], F32, 3, st)
        NR = setup_next_norm(k, nn, st)
        NSB = S // FFN_TB
        seq = [(sb, j) for sb in range(NSB) for j in range(FC)]
        loaded = {}
        for n_, (sb_, j_) in enumerate(seq[:3]):
            loaded[(sb_, j_)] = pre_w[n_]

        def prefetch(idx):
            if idx < len(seq):
                sb_, j_ = seq[idx]
                wt_, rw_ = wr.next()
                P.dma(wt_[:, :, 0:128], wgu[:, :, j_ * 128:(j_ + 1) * 128], writes=[rw_], eng="pool")
                P.dma(wt_[:, :, 128:256], wgu[:, :, D_FF + j_ * 128:D_FF + (j_ + 1) * 128], writes=[rw_], eng="pool")
                loaded[(sb_, j_)] = (wt_, rw_)

        for sb in range(NSB):
            for j in range(FC):
                prefetch(sb * FC + j + 3)
                if sb == 0 and j < FC // 2:
                    P.dma(wd[:, 2 * j:2 * j + 2, :], wdn[:, 2 * j:2 * j + 2, :], writes=[r_wd], eng="pool")
                wt, rw = loaded.pop((sb, j))
                for tb in range(FFN_TB // 512):
                    t0 = sb * FFN_TB + tb * 512
                    gb = (sb * FFN_TB) // 512 + tb
                    pg, rpg = bank(k, "fg", [0, 1, 2, 3])
                    pu, rpu = bank(k, "fg", [0, 1, 2, 3])
                    for c in range(KC):
                        P.op("pe", lambda e, c=c, pg=pg, wt=wt, t0=t0: e.matmul(
                            pg[:, :], lhsT=wt[:, c, 0:128], rhs=k.bigA[:, c, t0:t0 + 512],
                            start=(c == 0), stop=(c == KC - 1)), reads=[rw, k.r_big[gb]], writes=[rpg])
                    for c in range(KC):
                        P.op("pe", lambda e, c=c, pu=pu, wt=wt, t0=t0: e.matmul(
                            pu[:, :], lhsT=wt[:, c, 128:256], rhs=k.bigA[:, c, t0:t0 + 512],
                            start=(c == 0), stop=(c == KC - 1)), reads=[rw, k.r_big[gb]], writes=[rpu])
                    sg, rsg = sgr.next()
                    P.op("act", lambda e, sg=sg, pg=pg: e.activation(out=sg[:], in_=pg[:, :], func=AF.Silu),
                         writes=[rpg, rsg])
                    P.op("dve", lambda e, sg=sg, pu=pu, j=j, tb=tb: e.tensor_tensor(
                        out=actT[:, j, tb * 512:(tb + 1) * 512], in0=pu[:, :], in1=sg[:], op=ALU.mult),
                        reads=[rsg], writes=[rpu, r_act[tb]])
            for tt in range(FFN_TB // 128):
                tok0 = sb * FFN_TB + tt * 128
                xt, rx = xr.next()
                P.dma(xt[:], x_src[tok0:tok0 + 128, :], reads=k.xr(x_src, tok0 // 128), writes=[rx])
                for dh in range(2):
                    po, rpo = bank(k, "fd", [4, 5, 6, 7])
                    for j in range(FC):
                        P.op("pe", lambda e, j=j, po=po, tt=tt, dh=dh: e.matmul(
                            po[:, :], lhsT=actT[:, j, tt * 128:(tt + 1) * 128], rhs=wd[:, j, dh * 512:(dh + 1) * 512],
                            start=(j == 0), stop=(j == FC - 1)), reads=[r_act[tt // 4], r_wd], writes=[rpo])
                    P.op("dve", lambda e, po=po, xt=xt, dh=dh: e.tensor_tensor(
                        out=xt[:, dh * 512:(dh + 1) * 512], in0=po[:, :], in1=xt[:, dh * 512:(dh + 1) * 512],
                        op=ALU.add), reads=[], writes=[rpo, rx])
                P.dma(k.xres[tok0:tok0 + 128, :], xt[:], reads=[rx], writes=[k.r_xres[tok0 // 128]])
                if nn is not None:
                    norm_tile(k, xt, rx, tok0 // 128, nn, NR, [0, 1])


def gdn_layer(k, j, x_src, nn=None):
    P = k.P
    rc = k.r_const
    NB = S // 512
    w_in = k.gdn_w_in[j].rearrange("(c p) n -> p c n", p=128)
    oTd = k.oT_d.rearrange("(h p) t -> p h t", p=128)
    r_oTd = [Res() for _ in range(NB)]
    with contextlib.ExitStack() as st:
        cw_raw = P.sbuf("g_cwraw", [128, 128], F32, st)
        cwT = P.sbuf("g_cwT", [128, 4, 32], F32, st)
        dtb = P.sbuf("g_dtb", [128, 16], F32, st)
        nA = P.sbuf("g_nA", [128, 16], F32, st)
        gon = P.sbuf("g_gon", [128, 128], F32, st)
        triu = P.sbuf("g_triu", [128, 128], F32, st)
        neglt = P.sbuf("g_neglt", [128, 128], F32, st)
        posue = P.sbuf("g_posue", [128, 128], F32, st)
        r_lc = Res()
        gm = P.sbuf("g_gm", [128, 8, 128], BF16, st)
        P.dma(gm[:], k.gmask, writes=[r_lc], eng="pool")
        P.dma(cw_raw[:], k.gdn_conv_w[j].rearrange("t (c p) -> (t c) p", p=128), writes=[r_lc])
        P.dma(dtb[:], k.gdn_dt_bias[j:j + 1, :].partition_broadcast(128), writes=[r_lc])
        P.dma(nA[:], k.gdn_a_log[j:j + 1, :].partition_broadcast(128), writes=[r_lc])
        P.dma(gon[:], k.gdn_out_norm[j:j + 1, :].partition_broadcast(128), writes=[r_lc])
        P.op("act", lambda e: e.activation(out=nA[:], in_=nA[:], func=AF.Exp), writes=[r_lc])
        P.op("act", lambda e: e.mul(out=nA[:], in_=nA[:], mul=-1.0), writes=[r_lc])
        pc, rpc = bank(k, "gt", [2, 3, 4])
        P.op("pe", lambda e: e.transpose(out=pc[:, 0:128], in_=cw_raw[:, :], identity=k.identf[:]),
             reads=[r_lc, rc], writes=[rpc])
        P.op("dve", lambda e: e.tensor_copy(out=cwT[:].rearrange("p a b -> p (a b)"), in_=pc[:, 0:128]),
             writes=[rpc, r_lc])
        P.op("pool", lambda e: e.memset(triu[:], 1.0), writes=[r_lc])
        P.op("pool", lambda e: e.affine_select(out=triu[:], in_=triu[:], pattern=[[1, 128]], compare_op=ALU.is_ge,
                                               fill=0.0, base=0, channel_multiplier=-1), writes=[r_lc])
        P.op("pool", lambda e: e.memset(neglt[:], 0.0), writes=[r_lc])
        P.op("pool", lambda e: e.affine_select(out=neglt[:], in_=neglt[:], pattern=[[-1, 128]], compare_op=ALU.is_ge,
                                               fill=-30000.0, base=-1, channel_multiplier=1), writes=[r_lc])
        P.op("pool", lambda e: e.memset(posue[:], 0.0), writes=[r_lc])
        P.op("pool", lambda e: e.affine_select(out=posue[:], in_=posue[:], pattern=[[1, 128]], compare_op=ALU.is_ge,
                                               fill=30000.0, base=0, channel_multiplier=-1), writes=[r_lc])
        NH = 16
        gs = {}
        for nm in ("beta", "gc", "eg", "ekd", "egl", "kbs"):
            gs[nm] = P.sbuf("g_" + nm, [128, NT, NH], F32, st)
        r_gs = Res()
        f2 = lambda t: t[:].rearrange("p a b -> p (a b)")
        with contextlib.ExitStack() as stg:
            for nm in ("g", "glb"):
                gs[nm] = P.sbuf("g_" + nm, [128, NT, NH], F32, stg)
            wba = P.sbuf("g_wba", [128, KC, 32], BF16, stg)
            ba = P.sbuf("g_ba", [128, NT, 32], F32, stg)
            tmp = P.sbuf("g_tmp", [128, NT, NH], F32, stg)
            r_wba = Res()
            P.dma(wba[:], w_in[:, :, 6144:6176], writes=[r_wba], eng="pool")
            for half in range(2):
                pb_, rpb_ = bank(k, "gt", [2, 3, 4])
                for tl in range(16):
                    i = half * 16 + tl
                    for c in range(KC):
                        P.op("pe", lambda e, c=c, i=i, tl=tl, pb_=pb_: e.matmul(
                            pb_[:, tl * 32:(tl + 1) * 32], lhsT=k.bigA[:, c, i * 128:(i + 1) * 128], rhs=wba[:, c, :],
                            start=(c == 0), stop=(c == KC - 1)), reads=[r_wba, k.r_big[i // 4]], writes=[rpb_])
                P.op("act", lambda e, half=half, pb_=pb_: e.copy(
                    out=ba[:, half * 16:(half + 1) * 16, :].rearrange("p a b -> p (a b)"), in_=pb_[:, :]),
                    writes=[rpb_, r_gs])
            P.op("act", lambda e: e.activation(out=gs["beta"][:], in_=ba[:, :, 0:16], func=AF.Sigmoid), writes=[r_gs])
            P.op("dve", lambda e: e.tensor_tensor(out=tmp[:], in0=ba[:, :, 16:32],
                                                  in1=dtb[:, 0:16].unsqueeze(1).to_broadcast([128, NT, NH]), op=ALU.add),
                 reads=[r_lc], writes=[r_gs])
            P.op("act", lambda e: e.activation(out=tmp[:], in_=tmp[:], func=AF.Exp), writes=[r_gs])
            P.op("act", lambda e: e.activation(out=tmp[:], in_=tmp[:], func=AF.Ln, bias=1.0), writes=[r_gs])
            P.op("dve", lambda e: e.tensor_tensor(out=gs["g"][:], in0=tmp[:],
                                                  in1=nA[:, 0:16].unsqueeze(1).to_broadcast([128, NT, NH]), op=ALU.mult),
                 reads=[r_lc], writes=[r_gs])
            p1, rp1 = bank(k, "gt", [2, 3, 4])
            P.op("pe", lambda e: e.matmul(p1[:, :], lhsT=triu[:], rhs=f2(gs["g"]), start=True, stop=True),
                 reads=[r_gs, r_lc], writes=[rp1])
            P.op("dve", lambda e: e.tensor_copy(out=f2(gs["gc"]), in_=p1[:, :]), writes=[rp1, r_gs])
            p2, rp2 = bank(k, "gt", [2, 3, 4])
            P.op("pe", lambda e: e.matmul(p2[:, :], lhsT=k.onesf[:], rhs=f2(gs["g"]), start=True, stop=True),
                 reads=[r_gs, rc], writes=[rp2])
            P.op("dve", lambda e: e.tensor_copy(out=f2(gs["glb"]), in_=p2[:, :]), writes=[rp2, r_gs])
            P.op("act", lambda e: e.activation(out=gs["eg"][:], in_=gs["gc"][:], func=AF.Exp), writes=[r_gs])
            P.op("act", lambda e: e.activation(out=gs["egl"][:], in_=gs["glb"][:], func=AF.Exp), writes=[r_gs])
            P.op("dve", lambda e: e.tensor_tensor(out=tmp[:], in0=gs["glb"][:], in1=gs["gc"][:], op=ALU.subtract),
                 writes=[r_gs])
            P.op("act", lambda e: e.activation(out=gs["ekd"][:], in_=tmp[:], func=AF.Exp), writes=[r_gs])
            P.op("dve", lambda e: e.tensor_tensor(out=gs["kbs"][:], in0=gs["beta"][:], in1=gs["eg"][:], op=ALU.mult),
                 writes=[r_gs])
        for nm in ("beta", "gc", "eg", "ekd", "egl", "kbs"):
            dump(k, "gs_" + nm, gs[nm][:], [r_gs])
        P.barrier()
        wfr = Ring(P, "g_wf", [128, KC, 512], BF16, 2, st)
        wzr = Ring(P, "g_wz", [128, KC, 256], BF16, 1, st)
        qT = P.sbuf("g_qT", [128, S], BF16, st)
        kT = P.sbuf("g_kT", [128, S], BF16, st)
        vT = P.sbuf("g_vT", [128, 2, S], BF16, st)
        r_q = [Res() for _ in range(NB)]
        r_k = [Res() for _ in range(NB)]
        r_v = [Res() for _ in range(NB)]
        S32 = P.sbuf("g_S32", [128, 2, 128], F32, st)
        Sb = P.sbuf("g_Sb", [128, 2, 128], BF16, st)
        r_S32, r_Sb = Res(), Res()
        GT = [2, 3, 4]
        GR = [5, 6]
        GO = [7]
        GB = [0, 1]
        NL = GDN_LANES
        bc3 = lambda ap2: ap2.unsqueeze(1).to_broadcast([128, 2, 128])
        bcl = lambda ap2: ap2.unsqueeze(2).to_broadcast([128, 2, 128])
        v3 = lambda ap: ap.rearrange("p (a b) -> p a b", a=2)

        def chunk_gen(kh, tb, ch, wf, rwf, B):
            t0 = tb * 512
            rb = k.r_big[tb]
            chunk = (kh, 8 + kh, 16 + 2 * kh, 17 + 2 * kh)[ch]
            pp, rpp = k.ps[ch], k.rps[ch]
            for c in range(KC):
                P.op("pe", lambda e, c=c: e.matmul(
                    pp[:, :], lhsT=wf[:, c, ch * 128:(ch + 1) * 128], rhs=k.bigA[:, c, t0:t0 + 512],
                    start=(c == 0), stop=(c == KC - 1)), reads=[rwf, rb], writes=[rpp])
            yield
            pre, rpre = B["pre"][ch].next()
            halo, r_halo = B["halo"], B["r_halo"]
            P.op("act", lambda e: e.copy(out=pre[:, 3:515], in_=pp[:, :]), writes=[rpp, rpre])
            if tb == 0:
                P.op("pool", lambda e: e.memset(pre[:, 0:3], 0.0), writes=[rpre])
            else:
                P.op("pool", lambda e: e.tensor_copy(out=pre[:, 0:3], in_=halo[ch][:, 0:3]),
                     reads=[r_halo[ch]], writes=[rpre])
            yield
            P.op("pool", lambda e: e.tensor_copy(out=halo[ch][:, 0:3], in_=pre[:, 512:515]),
                 reads=[rpre], writes=[r_halo[ch]])
            cv, rcv = B["cv"][ch].next()
            P.op("dve", lambda e: e.tensor_scalar(
                out=cv[:], in0=pre[:, 3:515], scalar1=cwT[:, 3, chunk:chunk + 1], scalar2=None, op0=ALU.mult),
                reads=[rpre, r_lc], writes=[rcv])
            for tap in (2, 1, 0):
                P.op("dve", lambda e, tap=tap: e.scalar_tensor_tensor(
                    out=cv[:], in0=pre[:, tap:tap + 512], scalar=cwT[:, tap, chunk:chunk + 1], in1=cv[:],
                    op0=ALU.mult, op1=ALU.add), reads=[rpre, r_lc], writes=[rcv])
            yield
            if ch >= 2:
                P.op("act", lambda e: e.activation(out=vT[:, ch - 2, t0:t0 + 512], in_=cv[:], func=AF.Silu),
                     reads=[rcv], writes=[r_v[tb]])
                yield
                return
            P.op("act", lambda e: e.activation(out=cv[:], in_=cv[:], func=AF.Silu), writes=[rcv])
            yield
            sq, rsq = B["sq"][ch].next()
            P.op("pool", lambda e: e.tensor_tensor(out=sq[:], in0=cv[:], in1=cv[:], op=ALU.mult),
                 reads=[rcv], writes=[rsq])
            yield
            pss, rpss = k.ps[4 + ch], k.rps[4 + ch]
            P.op("pe", lambda e: e.matmul(pss[:, :], lhsT=k.onesb[:], rhs=sq[:], start=True, stop=True),
                 reads=[rsq, rc], writes=[rpss])
            yield
            ln, rln = B["ln"][ch].next()
            P.op("act", lambda e: e.activation(out=ln[:], in_=pss[:, :], func=AF.Ln, bias=k.epsc[:, 0:1]),
                 reads=[rc], writes=[rpss, rln])
            P.op("act", lambda e: e.activation(out=ln[:], in_=ln[:], func=AF.Exp, scale=-0.5), writes=[rln])
            yield
            dst, rdst, mul = ((qT, r_q, 128.0 ** -0.5), (kT, r_k, 1.0))[ch]
            P.op("dve", lambda e: e.scalar_tensor_tensor(
                out=dst[:, t0:t0 + 512], in0=cv[:], scalar=mul, in1=ln[:], op0=ALU.mult, op1=ALU.mult),
                reads=[rcv, rln], writes=[rdst[tb]])
            yield

        def pt_stage(kh, i, L, O):
            h0 = 2 * kh
            tb = i // 4
            tok = slice(i * 128, (i + 1) * 128)
            pair = lambda nm: gs[nm][:, i, h0:h0 + 2]
            col = lambda nm, hh: gs[nm][:, i, h0 + hh:h0 + hh + 1]
            kbg, rkbg = O["kbg"]
            kdec, rkdec = O["kdec"]
            bv, rbv = O["bv"]
            qkm, rqkm = O["qkm"]
            TT, rTT = O["TT"]
            nwT, rnwT = O["nwT"]
            hb = [0]

            def lbank():
                h = hb[0]
                hb[0] ^= 1
                return k.ps[L["bank"]][:, h * 256:(h + 1) * 256], k.rps[L["bank"]]
            pt, rpt = lbank()
            ptb = pt.bitcast(BF16)
            P.op("pe", lambda e: e.transpose(out=ptb[:, 0:128], in_=kT[:, tok], identity=k.ident[:]),
                 reads=[r_k[tb], rc], writes=[rpt])
            for hh in range(2):
                P.op("pe", lambda e, hh=hh: e.transpose(out=ptb[:, 128 + hh * 128:256 + hh * 128], in_=vT[:, hh, tok],
                                                        identity=k.ident[:]), reads=[r_v[tb], rc], writes=[rpt])
            dg, rdg = L["dg"]
            P.op("pool", lambda e: e.tensor_tensor(out=dg[:], in0=bc3(k.identf[:]), in1=bcl(pair("gc")), op=ALU.mult),
                 reads=[rc, r_gs], writes=[rdg])
            yield
            P.op("dve", lambda e: e.tensor_tensor(out=kbg[:], in0=bc3(ptb[:, 0:128]), in1=bcl(pair("kbs")), op=ALU.mult),
                 reads=[r_gs], writes=[rpt, rkbg])
            P.op("dve", lambda e: e.tensor_tensor(out=kdec[:], in0=bc3(ptb[:, 0:128]), in1=bcl(pair("ekd")), op=ALU.mult),
                 reads=[r_gs], writes=[rpt, rkdec])
            P.op("dve", lambda e: e.tensor_tensor(out=bv[:], in0=v3(ptb[:, 128:384]), in1=bcl(pair("beta")), op=ALU.mult),
                 reads=[r_gs], writes=[rpt, rbv])
            pa, rpa = lbank()
            for hh in range(2):
                P.op("pe", lambda e, hh=hh: e.matmul(pa[:, hh * 128:(hh + 1) * 128], lhsT=dg[:, hh, :], rhs=k.onesf[:],
                                                     start=True, stop=False), reads=[rdg, rc], writes=[rpa])
                P.op("pe", lambda e, hh=hh: e.matmul(pa[:, hh * 128:(hh + 1) * 128], lhsT=k.negonesf[:], rhs=dg[:, hh, :],
                                                     start=False, stop=True), reads=[rdg, rc], writes=[rpa])
            pk, rpk = lbank()
            P.op("pe", lambda e: e.matmul(pk[:, 0:128], lhsT=kT[:, tok], rhs=kT[:, tok], start=True, stop=True),
                 reads=[r_k[tb]], writes=[rpk])
            P.op("pe", lambda e: e.matmul(pk[:, 128:256], lhsT=kT[:, tok], rhs=qT[:, tok], start=True, stop=True),
                 reads=[r_k[tb], r_q[tb]], writes=[rpk])
            yield
            dm, rdm = L["dm"]
            dmt, rdmt = L["dmt"]
            P.op("dve", lambda e: e.scalar_tensor_tensor(out=dm[:], in0=v3(pa[:, 0:256]), scalar=0.0, in1=bc3(neglt[:]),
                                                         op0=ALU.min, op1=ALU.add), reads=[r_lc], writes=[rpa, rdm])
            P.op("dve", lambda e: e.scalar_tensor_tensor(out=dmt[:], in0=v3(pa[:, 0:256]), scalar=0.0, in1=bc3(posue[:]),
                                                         op0=ALU.max, op1=ALU.add), reads=[r_lc], writes=[rpa, rdmt])
            yield
            P.op("act", lambda e: e.activation(out=dm[:], in_=dm[:], func=AF.Exp), writes=[rdm])
            P.op("act", lambda e: e.activation(out=dmt[:], in_=dmt[:], func=AF.Exp, scale=-1.0), writes=[rdmt])
            yield
            A, rA = L["A"]
            for hh in range(2):
                P.op("dve", lambda e, hh=hh: e.scalar_tensor_tensor(
                    out=A[:, hh, :], in0=pk[:, 0:128], scalar=col("beta", hh), in1=dm[:, hh, :],
                    op0=ALU.mult, op1=ALU.mult), reads=[rdm, r_gs], writes=[rpk, rA])
            P.op("dve", lambda e: e.tensor_tensor(out=qkm[:], in0=bc3(pk[:, 128:256]), in1=dmt[:], op=ALU.mult),
                 reads=[rdmt], writes=[rpk, rqkm])
            yield
            pm, rpm = lbank()
            pmb = pm.bitcast(BF16)
            for hh in range(2):
                P.op("pe", lambda e, hh=hh: e.transpose(out=pmb[:, hh * 128:(hh + 1) * 128], in_=A[:, hh, :],
                                                        identity=k.ident[:]), reads=[rA, rc], writes=[rpm])
            Am, rAm = L["Mo"][0]
            P.op("pool", lambda e: e.tensor_tensor(out=Am[:], in0=A[:], in1=bc3(gm[:, 0, :]), op=ALU.mult),
                 reads=[rA, r_lc], writes=[rAm])
            yield
            M, rM = L["M"]
            P.op("act", lambda e: e.copy(out=M[:], in_=v3(pmb[:, 0:256])), writes=[rpm, rM])
            UV, rUV = L["UV"][0]
            P.op("dve", lambda e, UV=UV: e.tensor_tensor(out=UV[:, 1], in0=bc3(k.ident[:]), in1=Am[:], op=ALU.subtract),
                 reads=[rAm, rc], writes=[rUV])
            yield
            Mm, rMm = L["Mo"][1]
            P.op("pool", lambda e: e.tensor_tensor(out=Mm[:], in0=M[:], in1=bc3(gm[:, 1, :]), op=ALU.mult),
                 reads=[rM, r_lc], writes=[rMm])
            yield
            P.op("dve", lambda e, UV=UV: e.tensor_tensor(out=UV[:, 0], in0=bc3(k.ident[:]), in1=Mm[:], op=ALU.subtract),
                 reads=[rMm, rc], writes=[rUV])
            Mo, rMo = L["Mo"][0]
            P.op("pool", lambda e, Mo=Mo: e.tensor_tensor(out=Mo[:], in0=M[:], in1=bc3(gm[:, 2, :]), op=ALU.mult),
                 reads=[rM, r_lc], writes=[rMo])
            yield
            pbank, rpbank = k.ps[L["bank"]], k.rps[L["bank"]]
            for lv in range(6):
                pY, rpY = lbank()
                for hh in range(2):
                    P.op("pe", lambda e, hh=hh, pY=pY, Mo=Mo, UV=UV: e.matmul(
                        pY[:, hh * 128:(hh + 1) * 128], lhsT=Mo[:, hh, :], rhs=UV[:, 1, hh, :], start=True, stop=True),
                        reads=[rMo, rUV], writes=[rpY])
                yield
                Y, rY = L["Y"]
                P.op("act", lambda e, Y=Y, pY=pY: e.mul(out=Y[:], in_=v3(pY[:, 0:256]), mul=-1.0), writes=[rpY, rY])
                if lv < 5:
                    Mo2, rMo2 = L["Mo"][(lv + 1) % 2]
                    P.op("pool", lambda e, Mo2=Mo2, lv=lv: e.tensor_tensor(out=Mo2[:], in0=M[:], in1=bc3(gm[:, 3 + lv, :]),
                                                                           op=ALU.mult), reads=[rM, r_lc], writes=[rMo2])
                yield
                for hh in range(2):
                    P.op("pe", lambda e, hh=hh, UV=UV: e.matmul(
                        pbank[:, hh * 128:(hh + 1) * 128], lhsT=k.ident[:], rhs=UV[:, 0, hh, :], start=True, stop=False),
                        reads=[rUV, rc], writes=[rpbank])
                    P.op("pe", lambda e, hh=hh, UV=UV, Y=Y: e.matmul(
                        pbank[:, hh * 128:(hh + 1) * 128], lhsT=Y[:, hh, :], rhs=UV[:, 0, hh, :], start=False, stop=True),
                        reads=[rUV, rY], writes=[rpbank])
                if lv < 5:
                    for hh in range(2):
                        P.op("pe", lambda e, hh=hh, UV=UV: e.matmul(
                            pbank[:, 256 + hh * 128:384 + hh * 128], lhsT=k.ident[:], rhs=UV[:, 1, hh, :], start=True, stop=False),
                            reads=[rUV, rc], writes=[rpbank])
                        P.op("pe", lambda e, hh=hh, UV=UV, Y=Y: e.matmul(
                            pbank[:, 256 + hh * 128:384 + hh * 128], lhsT=UV[:, 0, hh, :], rhs=Y[:, hh, :], start=False, stop=True),
                            reads=[rUV, rY], writes=[rpbank])
                yield
                if lv < 5:
                    UVn, rUVn = L["UV"][(lv + 1) % 2]
                    if lv % 2:
                        P.op("act", lambda e, UVn=UVn: e.copy(out=UVn[:].rearrange("p a b c -> p (a b c)"), in_=pbank[:, :]),
                             writes=[rpbank, rUVn])
                    else:
                        P.op("dve", lambda e, UVn=UVn: e.tensor_copy(out=UVn[:].rearrange("p a b c -> p (a b c)"), in_=pbank[:, :]),
                             writes=[rpbank, rUVn])
                    UV, rUV = UVn, rUVn
                    Mo, rMo = Mo2, rMo2
                else:
                    P.op("dve", lambda e: e.tensor_copy(out=TT[:], in_=v3(pbank[:, 0:256])), writes=[rpbank, rTT])
                hb[0] = 0
                yield
            pw, rpw = lbank()
            for hh in range(2):
                P.op("pe", lambda e, hh=hh: e.matmul(pw[:, hh * 128:(hh + 1) * 128], lhsT=kbg[:, hh, :], rhs=TT[:, hh, :],
                                                     start=True, stop=True), reads=[rkbg, rTT], writes=[rpw])
            yield
            P.op("act", lambda e: e.mul(out=nwT[:], in_=v3(pw[:, 0:256]), mul=-1.0), writes=[rpw, rnwT])
            yield

        def r_stage(kh, i, O, RB, done):
            h0 = 2 * kh
            tb = i // 4
            tok = slice(i * 128, (i + 1) * 128)
            pair = lambda nm: gs[nm][:, i, h0:h0 + 2]
            col = lambda nm, hh: gs[nm][:, i, h0 + hh:h0 + hh + 1]
            TT, rTT = O["TT"]
            nwT, rnwT = O["nwT"]
            bv, rbv = O["bv"]
            kdec, rkdec = O["kdec"]
            qkm, rqkm = O["qkm"]
            pv, rpv = k.ps[4][:, 0:256], k.rps[4]
            for hh in range(2):
                P.op("pe", lambda e, hh=hh: e.matmul(pv[:, hh * 128:(hh + 1) * 128], lhsT=TT[:, hh, :], rhs=bv[:, hh, :],
                                                     start=True, stop=False), reads=[rTT, rbv], writes=[rpv])
                P.op("pe", lambda e, hh=hh: e.matmul(pv[:, hh * 128:(hh + 1) * 128], lhsT=nwT[:, hh, :], rhs=Sb[:, hh, :],
                                                     start=False, stop=True), reads=[rnwT, r_Sb], writes=[rpv])
            pz, rpz = k.ps[5], k.rps[5]
            for hh in range(2):
                P.op("pe", lambda e, hh=hh: e.matmul(pz[:, hh * 128:(hh + 1) * 128], lhsT=qT[:, tok], rhs=Sb[:, hh, :],
                                                     start=True, stop=True), reads=[r_q[tb], r_Sb], writes=[rpz])
            yield
            vn, rvn = RB["vn"].next()
            P.op("act", lambda e: e.copy(out=vn[:], in_=v3(pv[:, 0:256])), writes=[rpv, rvn])
            yield
            pd, rpd = k.ps[4][:, 0:256], k.rps[4]
            for hh in range(2):
                P.op("pe", lambda e, hh=hh: e.matmul(pd[:, hh * 128:(hh + 1) * 128], lhsT=kdec[:, hh, :], rhs=vn[:, hh, :],
                                                     start=True, stop=True), reads=[rkdec, rvn], writes=[rpd])
            for hh in range(2):
                P.op("pe", lambda e, hh=hh: e.matmul(pz[:, 256 + hh * 128:384 + hh * 128], lhsT=qkm[:, hh, :], rhs=vn[:, hh, :],
                                                     start=True, stop=True), reads=[rqkm, rvn], writes=[rpz])
            yield
            for hh in range(2):
                P.op("dve", lambda e, hh=hh: e.scalar_tensor_tensor(
                    out=S32[:, hh, :], in0=S32[:, hh, :], scalar=col("egl", hh), in1=pd[:, hh * 128:(hh + 1) * 128],
                    op0=ALU.mult, op1=ALU.add), reads=[r_gs], writes=[rpd, r_S32])
            zs, rzs = RB["zs"].next()
            P.op("dve", lambda e: e.tensor_tensor(out=zs[:], in0=v3(pz[:, 0:256]), in1=bcl(pair("eg")), op=ALU.mult),
                 reads=[r_gs], writes=[rpz, rzs])
            yield
            P.op("pool", lambda e: e.tensor_copy(out=Sb[:], in_=S32[:]), reads=[r_S32], writes=[r_Sb])
            o32, ro32 = RB["o32"].next()
            P.op("dve", lambda e: e.tensor_tensor(out=o32[:], in0=v3(pz[:, 256:512]), in1=zs[:], op=ALU.add),
                 reads=[rzs], writes=[rpz, ro32])
            done[i] = (o32, ro32)
            yield

        def o_stage(kh, i, o32, ro32, wz, rwz, RB, oT4, roT4):
            h0 = 2 * kh
            tb = i // 4
            tok = slice(i * 128, (i + 1) * 128)
            pzz, rpzz = k.ps[4][:, 256:512], k.rps[4]
            for c in range(KC):
                P.op("pe", lambda e, c=c: e.matmul(pzz[:, 0:256], lhsT=k.bigA[:, c, tok], rhs=wz[:, c, :],
                                                   start=(c == 0), stop=(c == KC - 1)), reads=[rwz, k.r_big[tb]], writes=[rpzz])
            junk, rjunk = RB["junk"].next()
            ssq, rssq = RB["ssq"].next()
            for hh in range(2):
                P.op("act", lambda e, hh=hh: e.activation(out=junk[:, hh, :], in_=o32[:, hh, :], func=AF.Square,
                                                          accum_out=ssq[:, hh:hh + 1]), reads=[ro32], writes=[rjunk, rssq])
            yield
            zz, rzz = RB["zz"].next()
            P.op("act", lambda e: e.activation(out=zz[:], in_=pzz[:, 0:256], func=AF.Silu), writes=[rpzz, rzz])
            P.op("dve", lambda e: e.tensor_scalar(out=ssq[:, 2:4], in0=ssq[:, 0:2], scalar1=1.0 / 128, scalar2=EPS,
                                                  op0=ALU.mult, op1=ALU.add), writes=[rssq])
            yield
            P.op("pool", lambda e: e.tensor_tensor(out=ssq[:, 4:6], in0=ssq[:, 2:4], in1=k.m05[:, 0:1].to_broadcast([128, 2]),
                                                   op=ALU.pow), reads=[rc], writes=[rssq])
            yield
            t1, rt1 = RB["t1"].next()
            for hh in range(2):
                P.op("dve", lambda e, hh=hh: e.scalar_tensor_tensor(
                    out=t1[:, hh, :], in0=o32[:, hh, :], scalar=ssq[:, 4 + hh:5 + hh], in1=gon[:],
                    op0=ALU.mult, op1=ALU.mult), reads=[ro32, rssq, r_lc], writes=[rt1])
            yield
            ob, rob = RB["ob"].next()
            P.op("pool", lambda e: e.tensor_tensor(out=ob[:], in0=t1[:], in1=v3(zz[:]), op=ALU.mult),
                 reads=[rt1, rzz], writes=[rob])
            yield
            po, rpo = k.ps[4][:, 256:512], k.rps[4]
            pob = po.bitcast(BF16)
            for hh in range(2):
                P.op("pe", lambda e, hh=hh: e.transpose(out=pob[:, hh * 128:(hh + 1) * 128], in_=ob[:, hh, :],
                                                        identity=k.ident[:]), reads=[rob, rc], writes=[rpo])
            yield
            q4 = i % 4
            P.op("act", lambda e: e.copy(out=oT4[:, :, q4 * 128:(q4 + 1) * 128], in_=v3(pob[:, 0:256])),
                 writes=[rpo, roT4])
            if q4 == 3:
                P.dma(oTd[:, h0:h0 + 2, tb * 512:(tb + 1) * 512], oT4[:], reads=[roT4], writes=[r_oTd[tb]])
            yield

        def delayed(g, d):
            for _ in range(d):
                yield
            yield from g

        def run_lanes(gens):
            active = list(gens)
            while active:
                for g in list(active):
                    try:
                        next(g)
                    except StopIteration:
                        active.remove(g)

        def group(kh):
            wf, rwf = wfr.next()
            wz, rwz = wzr.next()
            for c in range(0, KC, 4):
                P.dma(wf[:, c:c + 4, 0:128], w_in[:, c:c + 4, kh * 128:(kh + 1) * 128], writes=[rwf], eng="pool")
                P.dma(wf[:, c:c + 4, 128:256], w_in[:, c:c + 4, 1024 + kh * 128:1024 + (kh + 1) * 128], writes=[rwf], eng="pool")
                P.dma(wf[:, c:c + 4, 256:512], w_in[:, c:c + 4, 2048 + kh * 256:2048 + (kh + 1) * 256], writes=[rwf], eng="pool")
                P.dma(wz[:, c:c + 4, :], w_in[:, c:c + 4, 4096 + kh * 256:4096 + (kh + 1) * 256], writes=[rwz], eng="pool")
            with contextlib.ExitStack() as stb:
                B = dict(pre=[Ring(P, "g_pre", [128, 515], F32, 2, stb) for _ in range(4)],
                         halo=[P.sbuf(f"g_halo{ch}", [128, 4], F32, stb) for ch in range(4)],
                         r_halo=[Res() for _ in range(4)],
                         cv=[Ring(P, "g_cv", [128, 512], F32, 2, stb) for _ in range(4)],
                         sq=[Ring(P, "g_sq", [128, 512], BF16, 2, stb) for _ in range(2)],
                         ln=[Ring(P, "g_ln", [128, 512], F32, 2, stb) for _ in range(2)])
                for tb in range(NB):
                    run_lanes([chunk_gen(kh, tb, ch, wf, rwf, B) for ch in range(4)])
            P.barrier()
            with contextlib.ExitStack() as stt:
                T3 = lambda nm, dt: (P.sbuf(nm, [128, 2, 128], dt, stt), Res())
                lanes = []
                NSLOT = NL + 1
                for ln_ in range(NL):
                    dgt = T3("g_dg", F32)
                    At = T3("g_A", BF16)
                    lanes.append(dict(bank=(0, 1, 2, 3, 7, 6)[ln_], dg=dgt, dm=T3("g_dm", F32), dmt=dgt,
                                      A=At, M=T3("g_M", BF16),
                                      Mo=[T3("g_Mo", BF16), T3("g_Mo", BF16)],
                                      UV=[(P.sbuf("g_UV", [128, 2, 2, 128], BF16, stt), Res()) for _ in range(2)],
                                      Y=At))
                oslots = [dict((nm, T3("g_" + nm, BF16)) for nm in ("kbg", "kdec", "bv", "qkm", "TT", "nwT"))
                          for _ in range(NSLOT)]
                R3 = lambda nm, dt, n: Ring(P, nm, [128, 2, 128], dt, n, stt)
                RB = dict(vn=R3("g_vn", BF16, 1), zs=R3("g_zs", F32, 1), o32=R3("g_o32", F32, 4),
                          junk=R3("g_junk", BF16, 1), ssq=Ring(P, "g_ssq", [128, 8], F32, 2, stt),
                          zz=Ring(P, "g_zz", [128, 256], F32, 1, stt), t1=R3("g_t1", F32, 1), ob=R3("g_ob", BF16, 1))
                oT4r = Ring(P, "g_oT4", [128, 2, 512], BF16, 2, stt)
                P.op("pool", lambda e: e.memset(S32[:], 0.0), writes=[r_S32])
                P.op("pool", lambda e: e.memset(Sb[:], 0.0), writes=[r_Sb])
                st4 = {}
                done = {}
                ptdone = set()
                rfin = set()
                ofin = set()

                def pt_worker(ln_):
                    for _ in range(ln_ * GDN_STAGGER):
                        yield
                    for i in range(ln_, NT, NL):
                        while i >= NSLOT and (i - NSLOT) not in rfin:
                            yield
                        yield from pt_stage(kh, i, lanes[ln_], oslots[i % NSLOT])
                        ptdone.add(i)

                def r_worker():
                    for i in range(NT):
                        while i not in ptdone or (i >= 3 and (i - 3) not in ofin):
                            yield
                        yield from r_stage(kh, i, oslots[i % NSLOT], RB, done)
                        rfin.add(i)

                def o_worker():
                    for i in range(NT):
                        while i not in done:
                            yield
                        if i % 4 == 0:
                            st4["o"] = oT4r.next()
                        oT4, roT4 = st4["o"]
                        o32, ro32 = done[i]
                        yield from o_stage(kh, i, o32, ro32, wz, rwz, RB, oT4, roT4)
                        ofin.add(i)

                run_lanes([r_worker(), o_worker()] + [pt_worker(ln_) for ln_ in range(NL)])
            P.barrier()

        for kh in range(8):
            group(kh)
    P.barrier()
    with contextlib.ExitStack() as st:
        otr = Ring(P, "g_oTt", [128, 16, 128], BF16, 3, st)
        cur = {}

        def pre_tile(i):
            t, r = otr.next()
            P.dma(t[:], oTd[:, :, i * 128:(i + 1) * 128], reads=[r_oTd[i // 4]], writes=[r])
            cur["t"], cur["r"] = t, r
            return t, r

        out_proj(k, k.gdn_w_out[j], 16, None, None, x_src, pre_tile=pre_tile, nn=nn)


def mla_layer(k, j, x_src):
    P = k.P
    rc = k.r_const
    NB = S // 512
    scale = 192.0 ** -0.5
    with contextlib.ExitStack() as st:
        rope_table(k, st)
        cqn = P.sbuf("cqn", [128, 3, S], BF16, st)
        ckvn = P.sbuf("ckvn", [128, 2, S], BF16, st)
        k2 = P.sbuf("k2", [128, S], BF16, st)
        r_cq = [Res() for _ in range(NB)]
        r_ckv = [Res() for _ in range(NB)]
        r_k2 = [Res() for _ in range(NB)]
        qg = k.gainT[:, 64 + 3 * j:64 + 3 * j + 3]
        kvg = k.gainT[:, 70 + 2 * j:70 + 2 * j + 2]
        with contextlib.ExitStack() as st1:
            win = P.sbuf("m_win", [128, KC, 768], BF16, st1)
            r_win = Res()
            w_in = k.mla_w_in[j].rearrange("(c p) n -> p c n", p=128)
            for c in range(0, KC, 2):
                P.dma(win[:, c:c + 2, 0:704], w_in[:, c:c + 2, :], writes=[r_win], eng="pool")
            vs = win[:, :, 640:704].rearrange("p c (i two) -> p c i two", two=2)
            vd = win[:, :, 704:768].rearrange("p c (i two) -> p c i two", two=2)
            P.op("act", lambda e: e.mul(out=vd[:, :, :, 0], in_=vs[:, :, :, 1], mul=-1.0), writes=[r_win])
            P.op("act", lambda e: e.copy(out=vd[:, :, :, 1], in_=vs[:, :, :, 0]), writes=[r_win])
            sqr = Ring(P, "m_sq", [128, 512], BF16, 4, st1)
            lnr = Ring(P, "m_ln", [128, 512], F32, 2, st1)
            rsr = Ring(P, "m_rs", [128, 512], F32, 2, st1)
            prr = Ring(P, "m_pr", [128, 512], BF16, 2, st1)
            allb = [0, 1, 2, 3, 4, 5, 6, 7]

            def latent(tb, nch, col0, dst, rdst, gcol, width):
                t0 = tb * 512
                rb = k.r_big[tb]
                pqs = []
                sqs = []
                for c3 in range(nch):
                    pq, rpq = bank(k, "m1", allb)
                    for c in range(KC):
                        P.op("pe", lambda e, pq=pq, c=c, c3=c3: e.matmul(
                            pq[:, :], lhsT=win[:, c, col0 + c3 * 128:col0 + (c3 + 1) * 128],
                            rhs=k.bigA[:, c, t0:t0 + 512], start=(c == 0), stop=(c == KC - 1)),
                            reads=[r_win, rb], writes=[rpq])
                    sq, rsq = sqr.next()
                    P.op("act", lambda e, sq=sq, pq=pq: e.activation(out=sq[:], in_=pq[:, :], func=AF.Square),
                         writes=[rpq, rsq])
                    pqs.append((pq, rpq))
                    sqs.append((sq, rsq))
                pss, rpss = bank(k, "m1", allb)
                for c3 in range(nch):
                    P.op("pe", lambda e, c3=c3, sq=sqs[c3][0]: e.matmul(
                        pss[:, :], lhsT=k.onesb[:], rhs=sq[:], start=(c3 == 0), stop=(c3 == nch - 1)),
                        reads=[sqs[c3][1], rc], writes=[rpss])
                ln, rln = lnr.next()
                rs, rrs = rsr.next()
                P.op("act", lambda e: e.activation(out=ln[:], in_=pss[:, :], func=AF.Ln, scale=1.0 / width,
                                                   bias=k.epsc[:, 0:1]), reads=[rc], writes=[rpss, rln])
                P.op("act", lambda e: e.activation(out=rs[:], in_=ln[:], func=AF.Exp, scale=-0.5),
                     reads=[rln], writes=[rrs])
                for c3 in range(nch):
                    pq, rpq = pqs[c3]
                    P.op("dve", lambda e, pq=pq, c3=c3: e.scalar_tensor_tensor(
                        out=dst[:, c3, t0:t0 + 512], in0=pq[:, :], scalar=gcol[:, c3:c3 + 1], in1=rs[:],
                        op0=ALU.mult, op1=ALU.mult), reads=[rrs, rc], writes=[rpq, rdst[tb]])

            def rope_key(tb):
                t0 = tb * 512
                rb = k.r_big[tb]
                pk, rpk = bank(k, "m1", allb)
                for c in range(KC):
                    P.op("pe", lambda e, c=c: e.matmul(
                        pk[:, :], lhsT=win[:, c, 640:768], rhs=k.bigA[:, c, t0:t0 + 512],
                        start=(c == 0), stop=(c == KC - 1)), reads=[r_win, rb], writes=[rpk])
                pr, rpr = prr.next()
                P.op("dve", lambda e: e.tensor_tensor(out=pr[:], in0=pk[:, :], in1=k.cs[:, t0:t0 + 512],
                                                      op=ALU.mult), reads=[rc], writes=[rpk, rpr])
                pf, rpf = bank(k, "m1", allb)
                P.op("pe", lambda e: e.matmul(pf[:, :], lhsT=k.fold[:], rhs=pr[:], start=True, stop=True),
                     reads=[rpr, rc], writes=[rpf])
                P.op("act", lambda e: e.copy(out=k2[:, t0:t0 + 512], in_=pf[:, :]), writes=[rpf, r_k2[tb]])

            for tb in range(NB):
                latent(tb, 3, 0, cqn, r_cq, qg, 384.0)
                latent(tb, 2, 384, ckvn, r_ckv, kvg, 256.0)
                rope_key(tb)
        dump(k, "cs", k.cs[:, 0:512], [rc], F32)
        dump(k, "cqn", cqn[:, :, 0:512], [r_cq[0]])
        dump(k, "ckvn", ckvn[:, :, 0:512], [r_ckv[0]])
        dump(k, "k2", k2[:, 0:512], [r_k2[0]])
        P.barrier()
        with contextlib.ExitStack() as st2:
            wqr = Ring(P, "m_wq", [128, 3, 256], BF16, 2, st2)
            wkvr = Ring(P, "m_wkv", [128, 2, 256], BF16, 2, st2)
            qn = P.sbuf("m_qn", [128, S], BF16, st2)
            q2 = P.sbuf("m_q2", [128, S], BF16, st2)
            kn = P.sbuf("m_kn", [128, S], BF16, st2)
            vv = P.sbuf("m_v", [128, NT, 128], BF16, st2)
            r_qn = [Res() for _ in range(NB)]
            r_q2 = [Res() for _ in range(NB)]
            r_kn = [Res() for _ in range(NB)]
            r_v = [Res() for _ in range(NB)]
            ptr = Ring(P, "m_pT", [128, 512], BF16, 5, st2)
            rir = Ring(P, "m_ri", [128, 512], F32, 2, st2)
            accr = Ring(P, "m_acc", [128, 512], F32, 4, st2)
            wqu = k.mla_w_q_up[j].rearrange("(c p) n -> p c n", p=128)
            wkvu = k.mla_w_kv_up[j].rearrange("(c p) n -> p c n", p=128)
            pb = [4, 5, 6, 7]

            def head_proj(h, tb, wq, rwq, wkv, rwkv):
                t0 = tb * 512
                p1, rp1 = bank(k, "mp", pb)
                for c3 in range(3):
                    P.op("pe", lambda e, c3=c3: e.matmul(
                        p1[:, :], lhsT=wq[:, c3, 0:128], rhs=cqn[:, c3, t0:t0 + 512], start=(c3 == 0), stop=(c3 == 2)),
                        reads=[rwq, r_cq[tb]], writes=[rp1])
                P.op("act", lambda e: e.copy(out=qn[:, t0:t0 + 512], in_=p1[:, :]), writes=[rp1, r_qn[tb]])
                p2, rp2 = bank(k, "mp", pb)
                for c3 in range(3):
                    P.op("pe", lambda e, c3=c3: e.matmul(
                        p2[:, :], lhsT=wq[:, c3, 128:256], rhs=cqn[:, c3, t0:t0 + 512], start=(c3 == 0), stop=(c3 == 2)),
                        reads=[rwq, r_cq[tb]], writes=[rp2])
                P.op("dve", lambda e: e.tensor_tensor(out=q2[:, t0:t0 + 512], in0=p2[:, :],
                                                      in1=k.cs[:, t0:t0 + 512], op=ALU.mult),
                     reads=[rc], writes=[rp2, r_q2[tb]])
                p3, rp3 = bank(k, "mp", pb)
                for c2 in range(2):
                    P.op("pe", lambda e, c2=c2: e.matmul(
                        p3[:, :], lhsT=wkv[:, c2, 0:128], rhs=ckvn[:, c2, t0:t0 + 512], start=(c2 == 0), stop=(c2 == 1)),
                        reads=[rwkv, r_ckv[tb]], writes=[rp3])
                P.op("act", lambda e: e.copy(out=kn[:, t0:t0 + 512], in_=p3[:, :]), writes=[rp3, r_kn[tb]])
                p4, rp4 = bank(k, "mp", pb)
                for tt in range(4):
                    for c2 in range(2):
                        P.op("pe", lambda e, c2=c2, tt=tt: e.matmul(
                            p4[:, tt * 128:(tt + 1) * 128], lhsT=ckvn[:, c2, t0 + tt * 128:t0 + (tt + 1) * 128],
                            rhs=wkv[:, c2, 128:256], start=(c2 == 0), stop=(c2 == 1)),
                            reads=[rwkv, r_ckv[tb]], writes=[rp4])
                P.op("dve", lambda e: e.tensor_copy(
                    out=vv[:, tb * 4:(tb + 1) * 4, :].rearrange("p a b -> p (a b)"), in_=p4[:, :]),
                    writes=[rp4, r_v[tb]])

            def attn_qk(h, qb, kt, nkt, accs):
                c0 = max(0, kt * 128 - qb * 512)
                w = 512 - c0
                q0 = qb * 512 + c0
                pS, rpS = bank(k, "mp", pb)
                kb = kt // 4
                P.op("pe", lambda e: e.matmul(
                    pS[:, c0:512], lhsT=kn[:, kt * 128:(kt + 1) * 128], rhs=qn[:, q0:q0 + w], start=True, stop=False),
                    reads=[r_kn[kb], r_qn[qb]], writes=[rpS])
                P.op("pe", lambda e: e.matmul(
                    pS[:, c0:512], lhsT=k2[:, kt * 128:(kt + 1) * 128], rhs=q2[:, q0:q0 + w], start=False, stop=True),
                    reads=[r_k2[kb], r_q2[qb]], writes=[rpS])
                pT, rpT = ptr.next()
                P.op("act", lambda e: e.activation(out=pT[:, c0:512], in_=pS[:, c0:512], func=AF.Exp, scale=scale),
                     writes=[rpS, rpT])
                if kt >= 4 * qb:
                    P.op("pool", lambda e: e.memset(pT[64:128, c0:c0 + 64], 0.0), writes=[rpT])
                acc, racc = accs[kt % 2]
                eng = ("pool", "dve")[kt % 2]
                if kt < 2:
                    if c0 > 0:
                        P.op(eng, lambda e: e.memset(acc[:, 0:c0], 0.0), writes=[racc])
                    P.op(eng, lambda e: e.tensor_copy(out=acc[:, c0:512], in_=pT[:, c0:512]), reads=[rpT], writes=[racc])
                else:
                    P.op(eng, lambda e: e.tensor_tensor(out=acc[:, c0:512], in0=acc[:, c0:512], in1=pT[:, c0:512], op=ALU.add),
                         reads=[rpT], writes=[racc])
                return pT, rpT, c0

            def attn_pv(kt, nkt, po, rpo, pT, rpT, c0):
                kb = kt // 4
                P.op("pe", lambda e: e.matmul(
                    po[:, c0:512], lhsT=vv[:, kt, :], rhs=pT[:, c0:512], start=(kt == 0), stop=(kt == nkt - 1)),
                    reads=[r_v[kb], rpT], writes=[rpo])

            def attn_block(h, qb):
                po, rpo = bank(k, "ao", [0, 1])
                prs, rprs = bank(k, "ar", [2, 3])
                nkt = 4 * qb + 4
                accs = [accr.next(), accr.next()]
                LA = 2
                pend = {}
                for idx in range(nkt + LA):
                    if idx < nkt:
                        pend[idx] = attn_qk(h, qb, idx, nkt, accs)
                    if idx >= LA:
                        attn_pv(idx - LA, nkt, po, rpo, *pend.pop(idx - LA))
                P.op("dve", lambda e: e.tensor_tensor(out=accs[0][0][:], in0=accs[0][0][:], in1=accs[1][0][:], op=ALU.add),
                     reads=[accs[1][1]], writes=[accs[0][1]])
                P.op("pe", lambda e: e.matmul(prs[:, :], lhsT=k.onesf[:], rhs=accs[0][0][:], start=True, stop=True),
                     reads=[rc, accs[0][1]], writes=[rprs])
                ri, rri = rir.next()
                P.op("dve", lambda e: e.reciprocal(out=ri[:], in_=prs[:, :]), writes=[rprs, rri])
                P.op("dve", lambda e: e.tensor_tensor(
                    out=k.bigA[:, h, qb * 512:(qb + 1) * 512], in0=po[:, :], in1=ri[:], op=ALU.mult),
                    reads=[rri], writes=[rpo, k.r_big[qb]])

            def head(h):
                wq, rwq = wqr.next()
                wkv, rwkv = wkvr.next()
                P.dma(wq[:, :, 0:192], wqu[:, :, h * 192:(h + 1) * 192], writes=[rwq], eng="pool")
                P.dma(wkv[:, :, :], wkvu[:, :, h * 256:(h + 1) * 256], writes=[rwkv], eng="pool")
                vs = wq[:, :, 128:192].rearrange("p c (i two) -> p c i two", two=2)
                vd = wq[:, :, 192:256].rearrange("p c (i two) -> p c i two", two=2)
                P.op("act", lambda e: e.mul(out=vd[:, :, :, 0], in_=vs[:, :, :, 1], mul=-1.0), writes=[rwq])
                P.op("act", lambda e: e.copy(out=vd[:, :, :, 1], in_=vs[:, :, :, 0]), writes=[rwq])
                for tb in range(NB):
                    head_proj(h, tb, wq, rwq, wkv, rwkv)
                if h == 0:
                    dump(k, "qn", qn[:, 0:512], [r_qn[0]])
                    dump(k, "q2", q2[:, 0:512], [r_q2[0]])
                    dump(k, "kn", kn[:, 0:512], [r_kn[0]])
                    dump(k, "vv", vv[:, 0:4, :], [r_v[0]])
                for qb in range(NB):
                    attn_block(h, qb)

            for h in range(8):
                head(h)
        dump(k, "oT", k.bigA[:, :, 0:1024], [k.r_big[0], k.r_big[1]])
        P.barrier()
        out_proj(k, k.mla_w_out[j], 8, lambda h, i: k.bigA[:, h, i * 128:(i + 1) * 128],
                 lambda i: [k.r_big[i // 4]], x_src)


def out_proj(k, w_dram, nch, lhs_fn, lhs_res_fn, x_src, pre_tile=None, nn=None):
    P = k.P
    with contextlib.ExitStack() as st:
        wo = P.sbuf("wo", [128, nch, D], BF16, st)
        r_wo = Res()
        wv = w_dram.rearrange("(c p) n -> p c n", p=128)
        for c in range(0, nch, 2):
            P.dma(wo[:, c:c + 2, :], wv[:, c:c + 2, :], writes=[r_wo], eng="pool")
        xr = Ring(P, "ox", [128, D], F32, 3, st)
        NR = setup_next_norm(k, nn, st)

        def tile(i):
            if pre_tile is not None:
                lt, lr = pre_tile(i)
                lhs = lambda c: lt[:, c, :]
                lres = [lr]
            else:
                lhs = lambda c: lhs_fn(c, i)
                lres = lhs_res_fn(i)
            xt, rx = xr.next()
            P.dma(xt[:], x_src[i * 128:(i + 1) * 128, :], reads=k.xr(x_src, i), writes=[rx])
            for dh in range(2):
                po, rpo = bank(k, "op", [0, 1, 2, 3])
                for c in range(nch):
                    P.op("pe", lambda e, po=po, c=c, dh=dh: e.matmul(
                        po[:, :], lhsT=lhs(c), rhs=wo[:, c, dh * 512:(dh + 1) * 512],
                        start=(c == 0), stop=(c == nch - 1)), reads=[r_wo] + lres, writes=[rpo])
                P.op("dve", lambda e, po=po, dh=dh: e.tensor_tensor(
                    out=xt[:, dh * 512:(dh + 1) * 512], in0=po[:, :], in1=xt[:, dh * 512:(dh + 1) * 512], op=ALU.add),
                    writes=[rpo, rx])
            P.dma(k.xres[i * 128:(i + 1) * 128, :], xt[:], reads=[rx], writes=[k.r_xres[i]])
            if nn is not None:
                norm_tile(k, xt, rx, i, nn, NR, [4, 5])

        for i in range(NT):
            tile(i)


def _invf_table():
    inv_freq = (10000.0 ** (-np.arange(0, 64, 2, dtype=np.float32) / 64.0)).astype(np.float32)
    t = np.zeros((128, 2), np.float32)
    for p in range(128):
        t[p, 0] = inv_freq[(p % 64) // 2] / (2.0 * math.pi)
        t[p, 1] = 0.25 if p < 64 else 0.0
    return t


def _gdn_masks():
    i = np.arange(128)[:, None]
    j = np.arange(128)[None, :]
    m = np.zeros((128, 8, 128), np.float32)
    m0 = ((i // 2 == j // 2) & (i > j)).astype(np.float32)
    m[:, 0, :] = m0
    m[:, 1, :] = m0.T
    for lv, b in enumerate((2, 4, 8, 16, 32, 64)):
        ma = ((i // (2 * b) == j // (2 * b)) & ((i // b) % 2 == 1) & ((j // b) % 2 == 0)).astype(np.float32)
        m[:, 2 + lv, :] = ma.T
    return m


_NC_CACHE = {}


def kernel(**inputs):
    key = "full"
    if key not in _NC_CACHE:
        _NC_CACHE[key] = build()
    nc = _NC_CACHE[key]
    n = 8
    shared = {}
    for name, v in inputs.items():
        if name in ("x", "positions"):
            continue
        a = np.ascontiguousarray(np.asarray(v))
        if name == "final_norm":
            a = a.reshape(1, D)
        shared[name] = a
    shared["invf"] = _invf_table()
    shared["gmask"] = _gdn_masks()
    x = np.asarray(inputs["x"])
    pos = np.asarray(inputs["positions"])
    in_maps = []
    for i in range(n):
        m = dict(shared)
        m["x"] = np.ascontiguousarray(x[i])
        m["positions"] = np.ascontiguousarray(pos[i].reshape(1, S).astype(np.int32))
        in_maps.append(m)
    res = run_bass_kernel_spmd(nc, in_maps, core_ids=list(range(n)))
    _NC_CACHE["last"] = res
    return np.stack([np.asarray(res.results[i]["out"]) for i in range(n)], axis=0).astype(np.float32)
```

```python
import contextlib
import math
import numpy as np
import concourse.bass as bass
import concourse.mybir as mybir
from concourse.alu_op_type import AluOpType as ALU
from concourse.bass_utils import run_bass_kernel_spmd

AF = mybir.ActivationFunctionType
F32 = mybir.dt.float32
BF16 = mybir.dt.bfloat16
I32 = mybir.dt.int32

SEM_LIM = 30000
N_DMA_SEMS = 24

S = 4096
D = 1024
NT = S // 128
KC = D // 128
DEPTH = 4
EPS = 1e-6
D_FF = 2816
FC = D_FF // 128
GDN_IN = 6176
MLA_IN = 704


class Res:
    __slots__ = ("last_w", "readers")

    def __init__(self):
        self.last_w = None
        self.readers = []


class Op:
    __slots__ = ("eng", "fn", "deps", "is_dma", "signal", "sem", "val", "dsem", "dprev")

    def __init__(self, eng, fn, is_dma):
        self.eng = eng
        self.fn = fn
        self.is_dma = is_dma
        self.deps = ()
        self.signal = False
        self.sem = None
        self.val = 0
        self.dsem = None
        self.dprev = 0


class Prog:
    ENGS = ("pe", "act", "dve", "pool", "sp")

    def __init__(self, nc):
        self.nc = nc
        self.ops = {e: [] for e in self.ENGS}
        self.stack = contextlib.ExitStack()
        self.dma_cnt = [0] * N_DMA_SEMS
        self.dma_last = [None] * N_DMA_SEMS
        self.dma_rr = 0
        self.uid = 0

    def sbuf(self, name, shape, dtype, stack=None):
        self.uid += 1
        return (stack or self.stack).enter_context(
            self.nc.sbuf_tensor(f"{name}_{self.uid}", list(shape), dtype))

    def psum(self, name, shape, dtype):
        return self.stack.enter_context(self.nc.psum_tensor(name, list(shape), dtype))

    def op(self, eng, fn, reads=(), writes=(), dma=False, extra_deps=()):
        o = Op(eng, fn, dma)
        deps = set(extra_deps)
        for r in reads:
            if r.last_w is not None:
                deps.add(r.last_w)
        for r in writes:
            if r.last_w is not None:
                deps.add(r.last_w)
            deps.update(r.readers)
        for r in reads:
            r.readers.append(o)
        for r in writes:
            r.last_w = o
            r.readers = []
        deps.discard(o)
        if eng == "pe":
            deps = {d for d in deps if d.is_dma or d.eng != "pe"}
        o.deps = deps
        for d in deps:
            d.signal = True
        if dma:
            k = self.dma_rr
            self.dma_rr = (self.dma_rr + 1) % N_DMA_SEMS
            o.dprev = self.dma_cnt[k] * 16
            self.dma_cnt[k] += 1
            o.dsem = k
            o.val = self.dma_cnt[k] * 16
            self.dma_last[k] = o
        self.ops[eng].append(o)
        return o

    def dma(self, out, in_, reads=(), writes=(), eng="sp", **kw):
        return self.op(eng, lambda e: e.dma_start(out=out, in_=in_, **kw), reads, writes, dma=True)

    def barrier(self):
        lasts = []
        for e in self.ENGS:
            for o in reversed(self.ops[e]):
                if not o.is_dma and o.fn is not None:
                    lasts.append(o)
                    break
        lasts += [o for o in self.dma_last if o is not None]
        for e in self.ENGS:
            self.op(e, None, extra_deps=lasts)

    def emit(self, final_wait_ops=()):
        nc = self.nc
        st = self.stack
        nsem = {}
        for e in self.ENGS:
            cnt = 0
            for o in self.ops[e]:
                if o.is_dma or o.fn is None:
                    continue
                if o.signal:
                    o.sem = (e, cnt // SEM_LIM)
                    o.val = cnt % SEM_LIM + 1
                    cnt += 1
            nsem[e] = (cnt + SEM_LIM - 1) // SEM_LIM
        sems = {}
        for e in self.ENGS:
            for k in range(nsem[e]):
                sems[(e, k)] = st.enter_context(nc.semaphore(f"s_{e}{k}"))
        dsems = [st.enter_context(nc.semaphore(f"s_dma{k}")) for k in range(N_DMA_SEMS)]
        block = st.enter_context(nc.Block())
        battr = {"pe": "tensor", "act": "scalar", "dve": "vector", "pool": "gpsimd", "sp": "sync"}

        def run_engine(ename, eng):
            waited = {}
            for o in self.ops[ename]:
                need = {}
                for d in o.deps:
                    if d.is_dma:
                        key = ("d", d.dsem)
                        s = dsems[d.dsem]
                    elif d.fn is None:
                        continue
                    else:
                        key = d.sem
                        s = sems[d.sem]
                    if need.get(key, (None, 0))[1] < d.val:
                        need[key] = (s, d.val)
                if o.is_dma and o.dprev > 0:
                    key = ("d", o.dsem)
                    if need.get(key, (None, 0))[1] < o.dprev:
                        need[key] = (dsems[o.dsem], o.dprev)
                for key, (s, v) in need.items():
                    if waited.get(key, 0) < v:
                        eng.wait_ge(s, v)
                        waited[key] = v
                if o.fn is None:
                    continue
                ins = o.fn(eng)
                if o.is_dma:
                    ins.then_inc(dsems[o.dsem], 16)
                elif o.signal:
                    ins.then_inc(sems[o.sem], 1)
            if ename == "sp":
                for o in final_wait_ops:
                    eng.wait_ge(dsems[o.dsem], o.val)

        for ename in self.ENGS:
            def mk(ename=ename):
                def body(eng):
                    run_engine(ename, eng)
                return body
            getattr(block, battr[ename])(mk())


class Ring:
    def __init__(self, P, name, shape, dtype, n, stack=None):
        self.t = [P.sbuf(f"{name}{i}", shape, dtype, stack) for i in range(n)]
        self.r = [Res() for _ in range(n)]
        self.i = 0
        self.n = n

    def next(self):
        k = self.i
        self.i = (self.i + 1) % self.n
        return self.t[k], self.r[k]


class K:
    pass


def dump(k, name, ap, reads, dt=BF16):
    if not getattr(k, "dbg", False):
        return
    shape = list(ap.shape)
    t = k.nc.dram_tensor("dbg_" + name, shape, F32, kind="ExternalOutput").ap()
    k.dbg_ops.append(k.P.dma(t, ap, reads=reads, eng="pool"))


def build(n_layers=DEPTH, do_mix=True, do_ffn=True, dbg=False):
    nc = bass.Bass("TRN2", target_bir_lowering=False)
    P = Prog(nc)
    k = K()
    k.nc, k.P = nc, P
    k.dbg = dbg
    k.dbg_ops = []

    def din(name, shape, dt=F32):
        return nc.dram_tensor(name, list(shape), dt, kind="ExternalInput").ap()

    k.x_in = din("x", [S, D])
    k.pos = din("positions", [1, S], I32)
    k.norm_mix = din("norm_mix", [DEPTH, D])
    k.norm_ffn = din("norm_ffn", [DEPTH, D])
    k.gdn_w_in = din("gdn_w_in", [2, D, GDN_IN])
    k.gdn_conv_w = din("gdn_conv_w", [2, 4, 4096])
    k.gdn_a_log = din("gdn_a_log", [2, 16])
    k.gdn_dt_bias = din("gdn_dt_bias", [2, 16])
    k.gdn_out_norm = din("gdn_out_norm", [2, 128])
    k.gdn_w_out = din("gdn_w_out", [2, 2048, D])
    k.mla_w_in = din("mla_w_in", [2, D, MLA_IN])
    k.mla_q_norm = din("mla_q_norm", [2, 384])
    k.mla_w_q_up = din("mla_w_q_up", [2, 384, 1536])
    k.mla_kv_norm = din("mla_kv_norm", [2, 256])
    k.mla_w_kv_up = din("mla_w_kv_up", [2, 256, 2048])
    k.mla_w_out = din("mla_w_out", [2, D, D])
    k.ffn_w_gate_up = din("ffn_w_gate_up", [DEPTH, D, 2 * D_FF])
    k.ffn_w_down = din("ffn_w_down", [DEPTH, D_FF, D])
    k.final_norm = din("final_norm", [1, D])
    k.invf = din("invf", [128, 2])
    k.gmask = din("gmask", [128, 8, 128])
    k.out = nc.dram_tensor("out", [S, D], F32, kind="ExternalOutput").ap()
    k.xres = nc.dram_tensor("xres", [S, D], F32, kind="Internal").ap()
    k.oT_d = nc.dram_tensor("oT_d", [2048, S], BF16, kind="Internal").ap()
    k.r_xres = [Res() for _ in range(NT)]
    k.xr = lambda src, i: [k.r_xres[i]] if src is k.xres else []

    with P.stack:
        setup_consts(k)
        x_src = k.x_in
        k.final_ops = []
        fuse = (do_mix is True) and do_ffn
        hT_ready = False
        for L in range(n_layers):
            if do_mix is True or (do_mix is not False and do_mix == L % 2):
                if not hT_ready:
                    rms_to_hT(k, x_src, k.gainT[:, L * KC:(L + 1) * KC])
                    P.barrier()
                hT_ready = False
                if L % 2 == 0:
                    nn = ("hT", k.gainT[:, (4 + L) * KC:(5 + L) * KC]) if fuse else None
                    gdn_layer(k, L // 2, x_src, nn)
                    hT_ready = nn is not None
                else:
                    mla_layer(k, L // 2, x_src)
                P.barrier()
                x_src = k.xres
            if do_ffn:
                if not hT_ready:
                    rms_to_hT(k, x_src, k.gainT[:, (4 + L) * KC:(5 + L) * KC])
                    P.barrier()
                hT_ready = False
                nn = None
                if fuse:
                    nn = ("final",) if L == n_layers - 1 else ("hT", k.gainT[:, (L + 1) * KC:(L + 2) * KC])
                ffn_layer(k, L, x_src, nn)
                hT_ready = nn is not None and nn[0] == "hT"
                P.barrier()
                x_src = k.xres
        if fuse:
            outs = k.final_ops
        else:
            outs = final_norm(k, x_src)
        P.emit(final_wait_ops=outs + k.dbg_ops)
    return nc


def setup_consts(k):
    P, nc = k.P, k.nc
    k.ps = [P.psum(f"ps{i}", [128, 512], F32) for i in range(8)]
    k.rps = [Res() for _ in range(8)]
    k.bank_rr = {}
    k.identf = P.sbuf("identf", [128, 128], F32)
    k.ident = P.sbuf("ident", [128, 128], BF16)
    k.onesf = P.sbuf("onesf", [128, 128], F32)
    k.negonesf = P.sbuf("negonesf", [128, 128], F32)
    k.onesb = P.sbuf("onesb", [128, 128], BF16)
    k.r_const = Res()
    rc = k.r_const
    P.op("pool", lambda e: e.memset(k.identf[:], 0.0), writes=[rc])
    P.op("pool", lambda e: e.affine_select(out=k.identf[:], in_=k.identf[:], pattern=[[1, 128]],
                                           compare_op=ALU.not_equal, fill=1.0, base=0,
                                           channel_multiplier=-1), writes=[rc])
    P.op("dve", lambda e: e.tensor_copy(out=k.ident[:], in_=k.identf[:]), reads=[rc], writes=[rc])
    P.op("pool", lambda e: e.memset(k.onesf[:], 1.0), writes=[rc])
    P.op("pool", lambda e: e.memset(k.negonesf[:], -1.0), writes=[rc])
    P.op("pool", lambda e: e.memset(k.onesb[:], 1.0), writes=[rc])
    k.gainT = P.sbuf("gainT", [128, 74], F32)
    g_raw = P.sbuf("g_raw", [74, 128], F32)
    r_g = Res()
    P.dma(g_raw[0:32, :], k.norm_mix.rearrange("r (c p) -> (r c) p", p=128), writes=[r_g])
    P.dma(g_raw[32:64, :], k.norm_ffn.rearrange("r (c p) -> (r c) p", p=128), writes=[r_g])
    P.dma(g_raw[64:70, :], k.mla_q_norm.rearrange("r (c p) -> (r c) p", p=128), writes=[r_g])
    P.dma(g_raw[70:74, :], k.mla_kv_norm.rearrange("r (c p) -> (r c) p", p=128), writes=[r_g])
    P.op("pe", lambda e: e.transpose(out=k.ps[0][:, 0:74], in_=g_raw[:, :], identity=k.identf[0:74, 0:74]),
         reads=[r_g, rc], writes=[k.rps[0]])
    P.op("dve", lambda e: e.tensor_copy(out=k.gainT[:], in_=k.ps[0][:, 0:74]), writes=[k.rps[0], rc])
    k.bigA = P.sbuf("bigA", [128, KC, S], BF16)
    k.r_big = [Res() for _ in range(S // 512)]
    k.m05 = P.sbuf("m05", [128, 1], F32)
    P.op("pool", lambda e: e.memset(k.m05[:], -0.5), writes=[rc])
    k.epsc = P.sbuf("epsc", [128, 1], F32)
    P.op("pool", lambda e: e.memset(k.epsc[:], EPS), writes=[rc])
    k.fold = P.sbuf("fold", [128, 128], BF16)
    for (a, b) in ((0, 0), (64, 64), (0, 64), (64, 0)):
        P.op("dve", lambda e, a=a, b=b: e.tensor_copy(out=k.fold[a:a + 64, b:b + 64], in_=k.identf[a:a + 64, a:a + 64]),
             reads=[rc], writes=[rc])


def rope_table(k, st_out):
    P = k.P
    rc = k.r_const
    k.cs = P.sbuf("cs", [128, S], F32, st_out)
    with contextlib.ExitStack() as st:
        invf = P.sbuf("invf", [128, 2], F32, st)
        posi = P.sbuf("posi", [128, S], I32, st)
        t = P.sbuf("rt", [128, S], F32, st)
        ti = P.sbuf("rti", [128, S], I32, st)
        m = P.sbuf("rm", [128, S], F32, st)
        tf = m
        r = Res()
        P.dma(invf[:], k.invf, writes=[r])
        P.dma(posi[:], k.pos.partition_broadcast(128), writes=[r])
        P.op("dve", lambda e: e.tensor_copy(out=t[:], in_=posi[:]), writes=[r])
        P.op("dve", lambda e: e.tensor_scalar(out=t[:], in0=t[:], scalar1=invf[:, 0:1], scalar2=invf[:, 1:2],
                                              op0=ALU.mult, op1=ALU.add), writes=[r])
        P.op("dve", lambda e: e.tensor_copy(out=ti[:], in_=t[:]), writes=[r])
        P.op("dve", lambda e: e.tensor_copy(out=tf[:], in_=ti[:]), writes=[r])
        P.op("dve", lambda e: e.tensor_tensor(out=t[:], in0=t[:], in1=tf[:], op=ALU.subtract), writes=[r])
        P.op("dve", lambda e: e.tensor_scalar(out=m[:], in0=t[:], scalar1=0.5, scalar2=None, op0=ALU.is_gt), writes=[r])
        P.op("dve", lambda e: e.tensor_tensor(out=t[:], in0=t[:], in1=m[:], op=ALU.subtract), writes=[r])
        P.op("dve", lambda e: e.tensor_scalar(out=m[:], in0=t[:], scalar1=-0.5, scalar2=None, op0=ALU.is_lt), writes=[r])
        P.op("dve", lambda e: e.tensor_tensor(out=t[:], in0=t[:], in1=m[:], op=ALU.add), writes=[r])
        P.op("act", lambda e: e.activation(out=k.cs[:], in_=t[:], func=AF.Sin, scale=6.283185), reads=[r], writes=[rc])
    P.barrier()


def bank(k, cls, banks):
    i = k.bank_rr.get(cls, 0)
    k.bank_rr[cls] = i + 1
    b = banks[i % len(banks)]
    return k.ps[b], k.rps[b]


def rms_to_hT(k, x_src, gT):
    P = k.P
    rc = k.r_const
    with contextlib.ExitStack() as st:
        xr = Ring(P, "nx", [128, D], F32, 3, st)
        sqr = Ring(P, "nsq", [128, D], F32, 2, st)
        ybr = Ring(P, "nyb", [128, D], BF16, 2, st)
        ssr = Ring(P, "nss", [128, 4], F32, 4, st)
        for i in range(NT):
            xt, rx = xr.next()
            sq, rsq = sqr.next()
            yb, ryb = ybr.next()
            ss, rss = ssr.next()
            P.dma(xt[:], x_src[i * 128:(i + 1) * 128, :], reads=k.xr(x_src, i), writes=[rx])
            P.op("act", lambda e, sq=sq, xt=xt, ss=ss: e.activation(
                out=sq[:], in_=xt[:], func=AF.Square, accum_out=ss[:, 0:1]), reads=[rx], writes=[rsq, rss])
            P.op("dve", lambda e, ss=ss: e.tensor_scalar(out=ss[:, 1:2], in0=ss[:, 0:1], scalar1=1.0 / D,
                                                         scalar2=EPS, op0=ALU.mult, op1=ALU.add),
                 reads=[rss], writes=[rss])
            P.op("pool", lambda e, ss=ss: e.tensor_tensor(out=ss[:, 2:3], in0=ss[:, 1:2], in1=k.m05[:],
                                                          op=ALU.pow), reads=[rss, rc], writes=[rss])
            P.op("act", lambda e, yb=yb, xt=xt, ss=ss: e.activation(out=yb[:], in_=xt[:], func=AF.Copy,
                                                                    scale=ss[:, 2:3]),
                 reads=[rx, rss], writes=[ryb])
            pt, rp = bank(k, "n", [0, 1])
            psb = pt[:].bitcast(BF16)
            for c in range(KC):
                P.op("pe", lambda e, c=c, psb=psb, yb=yb: e.transpose(
                    out=psb[:, c * 128:(c + 1) * 128], in_=yb[:, c * 128:(c + 1) * 128], identity=k.ident[:]),
                    reads=[ryb, rc], writes=[rp])
            P.op("dve", lambda e, i=i, psb=psb: e.tensor_tensor(
                out=k.bigA[:, :, i * 128:(i + 1) * 128],
                in0=psb[:, 0:D].rearrange("p (c n) -> p c n", c=KC),
                in1=gT.unsqueeze(2).to_broadcast([128, KC, 128]), op=ALU.mult),
                reads=[rc], writes=[rp, k.r_big[i // 4]])


def final_norm(k, x_src):
    P = k.P
    rc = k.r_const
    outs = []
    with contextlib.ExitStack() as st:
        k.gfin = P.sbuf("gfin", [128, D], F32, st)
        P.dma(k.gfin[:], k.final_norm.partition_broadcast(128), writes=[rc])
        xr = Ring(P, "fx", [128, D], F32, 3, st)
        sqr = Ring(P, "fsq", [128, D], F32, 2, st)
        yr = Ring(P, "fy", [128, D], F32, 3, st)
        ssr = Ring(P, "fss", [128, 4], F32, 4, st)
        for i in range(NT):
            xt, rx = xr.next()
            sq, rsq = sqr.next()
            yt, ry = yr.next()
            ss, rss = ssr.next()
            P.dma(xt[:], x_src[i * 128:(i + 1) * 128, :], reads=k.xr(x_src, i), writes=[rx])
            P.op("act", lambda e, sq=sq, xt=xt, ss=ss: e.activation(
                out=sq[:], in_=xt[:], func=AF.Square, accum_out=ss[:, 0:1]), reads=[rx], writes=[rsq, rss])
            P.op("dve", lambda e, ss=ss: e.tensor_scalar(out=ss[:, 1:2], in0=ss[:, 0:1], scalar1=1.0 / D,
                                                         scalar2=EPS, op0=ALU.mult, op1=ALU.add),
                 reads=[rss], writes=[rss])
            P.op("pool", lambda e, ss=ss: e.tensor_tensor(out=ss[:, 2:3], in0=ss[:, 1:2], in1=k.m05[:],
                                                          op=ALU.pow), reads=[rss, rc], writes=[rss])
            P.op("dve", lambda e, yt=yt, xt=xt, ss=ss: e.scalar_tensor_tensor(
                out=yt[:], in0=xt[:], scalar=ss[:, 2:3], in1=k.gfin[:], op0=ALU.mult, op1=ALU.mult),
                reads=[rx, rss, rc], writes=[ry])
            outs.append(P.dma(k.out[i * 128:(i + 1) * 128, :], yt[:], reads=[ry]))
    return outs


def norm_rings(k, st):
    P = k.P
    return dict(sq=Ring(P, "nf_sq", [128, D], BF16, 1, st), yb=Ring(P, "nf_yb", [128, D], BF16, 2, st),
                ss=Ring(P, "nf_ss", [128, 4], F32, 4, st))


def norm_tile(k, xt, rx, i, nn, NR, banks):
    P = k.P
    rc = k.r_const
    sq, rsq = NR["sq"].next()
    ss, rss = NR["ss"].next()
    P.op("act", lambda e: e.activation(out=sq[:], in_=xt[:], func=AF.Square, accum_out=ss[:, 0:1]),
         reads=[rx], writes=[rsq, rss])
    P.op("dve", lambda e: e.tensor_scalar(out=ss[:, 1:2], in0=ss[:, 0:1], scalar1=1.0 / D, scalar2=EPS,
                                          op0=ALU.mult, op1=ALU.add), writes=[rss])
    P.op("pool", lambda e: e.tensor_tensor(out=ss[:, 2:3], in0=ss[:, 1:2], in1=k.m05[:], op=ALU.pow),
         reads=[rc], writes=[rss])
    if nn[0] == "final":
        yt, ry = NR["y"].next()
        P.op("dve", lambda e: e.scalar_tensor_tensor(out=yt[:], in0=xt[:], scalar=ss[:, 2:3], in1=NR["gfin"][:],
                                                     op0=ALU.mult, op1=ALU.mult), reads=[rx, rss, rc], writes=[ry])
        k.final_ops.append(P.dma(k.out[i * 128:(i + 1) * 128, :], yt[:], reads=[ry]))
        return
    gT = nn[1]
    yb, ryb = NR["yb"].next()
    P.op("act", lambda e: e.activation(out=yb[:], in_=xt[:], func=AF.Copy, scale=ss[:, 2:3]),
         reads=[rx, rss], writes=[ryb])
    pt, rp = bank(k, "nf", banks)
    psb = pt[:].bitcast(BF16)
    for c in range(KC):
        P.op("pe", lambda e, c=c: e.transpose(out=psb[:, c * 128:(c + 1) * 128], in_=yb[:, c * 128:(c + 1) * 128],
                                              identity=k.ident[:]), reads=[ryb, rc], writes=[rp])
    P.op("dve", lambda e: e.tensor_tensor(
        out=k.bigA[:, :, i * 128:(i + 1) * 128], in0=psb[:, 0:D].rearrange("p (c n) -> p c n", c=KC),
        in1=gT.unsqueeze(2).to_broadcast([128, KC, 128]), op=ALU.mult),
        reads=[rc], writes=[rp, k.r_big[i // 4]])


def setup_next_norm(k, nn, st):
    if nn is None:
        return None
    NR = norm_rings(k, st)
    if nn[0] == "final":
        P = k.P
        NR["gfin"] = P.sbuf("nf_gfin", [128, D], F32, st)
        P.dma(NR["gfin"][:], k.final_norm.partition_broadcast(128), writes=[k.r_const])
        NR["y"] = Ring(P, "nf_y", [128, D], F32, 2, st)
    return NR


FFN_TB = 1024
GDN_LANES = 5
GDN_STAGGER = 9


def ffn_layer(k, L, x_src, nn=None):
    P = k.P
    wgu = k.ffn_w_gate_up[L].rearrange("(c p) n -> p c n", p=128)
    wdn = k.ffn_w_down[L].rearrange("(c p) n -> p c n", p=128)
    with contextlib.ExitStack() as st:
        wd = P.sbuf("wd", [128, FC, D], BF16, st)
        r_wd = Res()
        actT = P.sbuf("actT", [128, FC, FFN_TB], BF16, st)
        r_act = [Res() for _ in range(FFN_TB // 512)]
        wr = Ring(P, "wgu", [128, KC, 256], BF16, 4, st)
        pre_w = []
        for j in range(3):
            wt, rw = wr.next()
            P.dma(wt[:, :, 0:128], wgu[:, :, j * 128:(j + 1) * 128], writes=[rw], eng="pool")
            P.dma(wt[:, :, 128:256], wgu[:, :, D_FF + j * 128:D_FF + (j + 1) * 128], writes=[rw], eng="pool")
            pre_w.append((wt, rw))
        sgr = Ring(P, "sg", [128, 512], F32, 2, st)
        xr = Ring(P, "fx", [128, D], F32, 3, st)
        NR = setup_next_norm(k, nn, st)
        NSB = S // FFN_TB
        seq = [(sb, j) for sb in range(NSB) for j in range(FC)]
        loaded = {}
        for n_, (sb_, j_) in enumerate(seq[:3]):
            loaded[(sb_, j_)] = pre_w[n_]

        def prefetch(idx):
            if idx < len(seq):
                sb_, j_ = seq[idx]
                wt_, rw_ = wr.next()
                P.dma(wt_[:, :, 0:128], wgu[:, :, j_ * 128:(j_ + 1) * 128], writes=[rw_], eng="pool")
                P.dma(wt_[:, :, 128:256], wgu[:, :, D_FF + j_ * 128:D_FF + (j_ + 1) * 128], writes=[rw_], eng="pool")
                loaded[(sb_, j_)] = (wt_, rw_)

        for sb in range(NSB):
            for j in range(FC):
                prefetch(sb * FC + j + 3)
                if sb == 0 and j < FC // 2:
                    P.dma(wd[:, 2 * j:2 * j + 2, :], wdn[:, 2 * j:2 * j + 2, :], writes=[r_wd], eng="pool")
                wt, rw = loaded.pop((sb, j))
                for tb in range(FFN_TB // 512):
                    t0 = sb * FFN_TB + tb * 512
                    gb = (sb * FFN_TB) // 512 + tb
                    pg, rpg = bank(k, "fg", [0, 1, 2, 3])
                    pu, rpu = bank(k, "fg", [0, 1, 2, 3])
                    for c in range(KC):
                        P.op("pe", lambda e, c=c, pg=pg, wt=wt, t0=t0: e.matmul(
                            pg[:, :], lhsT=wt[:, c, 0:128], rhs=k.bigA[:, c, t0:t0 + 512],
                            start=(c == 0), stop=(c == KC - 1)), reads=[rw, k.r_big[gb]], writes=[rpg])
                    for c in range(KC):
                        P.op("pe", lambda e, c=c, pu=pu, wt=wt, t0=t0: e.matmul(
                            pu[:, :], lhsT=wt[:, c, 128:256], rhs=k.bigA[:, c, t0:t0 + 512],
                            start=(c == 0), stop=(c == KC - 1)), reads=[rw, k.r_big[gb]], writes=[rpu])
                    sg, rsg = sgr.next()
                    P.op("act", lambda e, sg=sg, pg=pg: e.activation(out=sg[:], in_=pg[:, :], func=AF.Silu),
                         writes=[rpg, rsg])
                    P.op("dve", lambda e, sg=sg, pu=pu, j=j, tb=tb: e.tensor_tensor(
                        out=actT[:, j, tb * 512:(tb + 1) * 512], in0=pu[:, :], in1=sg[:], op=ALU.mult),
                        reads=[rsg], writes=[rpu, r_act[tb]])
            for tt in range(FFN_TB // 128):
                tok0 = sb * FFN_TB + tt * 128
                xt, rx = xr.next()
                P.dma(xt[:], x_src[tok0:tok0 + 128, :], reads=k.xr(x_src, tok0 // 128), writes=[rx])
                for dh in range(2):
                    po, rpo = bank(k, "fd", [4, 5, 6, 7])
                    for j in range(FC):
                        P.op("pe", lambda e, j=j, po=po, tt=tt, dh=dh: e.matmul(
                            po[:, :], lhsT=actT[:, j, tt * 128:(tt + 1) * 128], rhs=wd[:, j, dh * 512:(dh + 1) * 512],
                            start=(j == 0), stop=(j == FC - 1)), reads=[r_act[tt // 4], r_wd], writes=[rpo])
                    P.op("dve", lambda e, po=po, xt=xt, dh=dh: e.tensor_tensor(
                        out=xt[:, dh * 512:(dh + 1) * 512], in0=po[:, :], in1=xt[:, dh * 512:(dh + 1) * 512],
                        op=ALU.add), reads=[], writes=[rpo, rx])
                P.dma(k.xres[tok0:tok0 + 128, :], xt[:], reads=[rx], writes=[k.r_xres[tok0 // 128]])
                if nn is not None:
                    norm_tile(k, xt, rx, tok0 // 128, nn, NR, [0, 1])


def gdn_layer(k, j, x_src, nn=None):
    P = k.P
    rc = k.r_const
    NB = S // 512
    w_in = k.gdn_w_in[j].rearrange("(c p) n -> p c n", p=128)
    oTd = k.oT_d.rearrange("(h p) t -> p h t", p=128)
    r_oTd = [Res() for _ in range(NB)]
    with contextlib.ExitStack() as st:
        cw_raw = P.sbuf("g_cwraw", [128, 128], F32, st)
        cwT = P.sbuf("g_cwT", [128, 4, 32], F32, st)
        dtb = P.sbuf("g_dtb", [128, 16], F32, st)
        nA = P.sbuf("g_nA", [128, 16], F32, st)
        gon = P.sbuf("g_gon", [128, 128], F32, st)
        triu = P.sbuf("g_triu", [128, 128], F32, st)
        neglt = P.sbuf("g_neglt", [128, 128], F32, st)
        posue = P.sbuf("g_posue", [128, 128], F32, st)
        r_lc = Res()
        gm = P.sbuf("g_gm", [128, 8, 128], BF16, st)
        P.dma(gm[:], k.gmask, writes=[r_lc], eng="pool")
        P.dma(cw_raw[:], k.gdn_conv_w[j].rearrange("t (c p) -> (t c) p", p=128), writes=[r_lc])
        P.dma(dtb[:], k.gdn_dt_bias[j:j + 1, :].partition_broadcast(128), writes=[r_lc])
        P.dma(nA[:], k.gdn_a_log[j:j + 1, :].partition_broadcast(128), writes=[r_lc])
        P.dma(gon[:], k.gdn_out_norm[j:j + 1, :].partition_broadcast(128), writes=[r_lc])
        P.op("act", lambda e: e.activation(out=nA[:], in_=nA[:], func=AF.Exp), writes=[r_lc])
        P.op("act", lambda e: e.mul(out=nA[:], in_=nA[:], mul=-1.0), writes=[r_lc])
        pc, rpc = bank(k, "gt", [2, 3, 4])
        P.op("pe", lambda e: e.transpose(out=pc[:, 0:128], in_=cw_raw[:, :], identity=k.identf[:]),
             reads=[r_lc, rc], writes=[rpc])
        P.op("dve", lambda e: e.tensor_copy(out=cwT[:].rearrange("p a b -> p (a b)"), in_=pc[:, 0:128]),
             writes=[rpc, r_lc])
        P.op("pool", lambda e: e.memset(triu[:], 1.0), writes=[r_lc])
        P.op("pool", lambda e: e.affine_select(out=triu[:], in_=triu[:], pattern=[[1, 128]], compare_op=ALU.is_ge,
                                               fill=0.0, base=0, channel_multiplier=-1), writes=[r_lc])
        P.op("pool", lambda e: e.memset(neglt[:], 0.0), writes=[r_lc])
        P.op("pool", lambda e: e.affine_select(out=neglt[:], in_=neglt[:], pattern=[[-1, 128]], compare_op=ALU.is_ge,
                                               fill=-30000.0, base=-1, channel_multiplier=1), writes=[r_lc])
        P.op("pool", lambda e: e.memset(posue[:], 0.0), writes=[r_lc])
        P.op("pool", lambda e: e.affine_select(out=posue[:], in_=posue[:], pattern=[[1, 128]], compare_op=ALU.is_ge,
                                               fill=30000.0, base=0, channel_multiplier=-1), writes=[r_lc])
        NH = 16
        gs = {}
        for nm in ("beta", "gc", "eg", "ekd", "egl", "kbs"):
            gs[nm] = P.sbuf("g_" + nm, [128, NT, NH], F32, st)
        r_gs = Res()
        f2 = lambda t: t[:].rearrange("p a b -> p (a b)")
        with contextlib.ExitStack() as stg:
            for nm in ("g", "glb"):
                gs[nm] = P.sbuf("g_" + nm, [128, NT, NH], F32, stg)
            wba = P.sbuf("g_wba", [128, KC, 32], BF16, stg)
            ba = P.sbuf("g_ba", [128, NT, 32], F32, stg)
            tmp = P.sbuf("g_tmp", [128, NT, NH], F32, stg)
            r_wba = Res()
            P.dma(wba[:], w_in[:, :, 6144:6176], writes=[r_wba], eng="pool")
            for half in range(2):
                pb_, rpb_ = bank(k, "gt", [2, 3, 4])
                for tl in range(16):
                    i = half * 16 + tl
                    for c in range(KC):
                        P.op("pe", lambda e, c=c, i=i, tl=tl, pb_=pb_: e.matmul(
                            pb_[:, tl * 32:(tl + 1) * 32], lhsT=k.bigA[:, c, i * 128:(i + 1) * 128], rhs=wba[:, c, :],
                            start=(c == 0), stop=(c == KC - 1)), reads=[r_wba, k.r_big[i // 4]], writes=[rpb_])
                P.op("act", lambda e, half=half, pb_=pb_: e.copy(
                    out=ba[:, half * 16:(half + 1) * 16, :].rearrange("p a b -> p (a b)"), in_=pb_[:, :]),
                    writes=[rpb_, r_gs])
            P.op("act", lambda e: e.activation(out=gs["beta"][:], in_=ba[:, :, 0:16], func=AF.Sigmoid), writes=[r_gs])
            P.op("dve", lambda e: e.tensor_tensor(out=tmp[:], in0=ba[:, :, 16:32],
                                                  in1=dtb[:, 0:16].unsqueeze(1).to_broadcast([128, NT, NH]), op=ALU.add),
                 reads=[r_lc], writes=[r_gs])
            P.op("act", lambda e: e.activation(out=tmp[:], in_=tmp[:], func=AF.Exp), writes=[r_gs])
            P.op("act", lambda e: e.activation(out=tmp[:], in_=tmp[:], func=AF.Ln, bias=1.0), writes=[r_gs])
            P.op("dve", lambda e: e.tensor_tensor(out=gs["g"][:], in0=tmp[:],
                                                  in1=nA[:, 0:16].unsqueeze(1).to_broadcast([128, NT, NH]), op=ALU.mult),
                 reads=[r_lc], writes=[r_gs])
            p1, rp1 = bank(k, "gt", [2, 3, 4])
            P.op("pe", lambda e: e.matmul(p1[:, :], lhsT=triu[:], rhs=f2(gs["g"]), start=True, stop=True),
                 reads=[r_gs, r_lc], writes=[rp1])
            P.op("dve", lambda e: e.tensor_copy(out=f2(gs["gc"]), in_=p1[:, :]), writes=[rp1, r_gs])
            p2, rp2 = bank(k, "gt", [2, 3, 4])
            P.op("pe", lambda e: e.matmul(p2[:, :], lhsT=k.onesf[:], rhs=f2(gs["g"]), start=True, stop=True),
                 reads=[r_gs, rc], writes=[rp2])
            P.op("dve", lambda e: e.tensor_copy(out=f2(gs["glb"]), in_=p2[:, :]), writes=[rp2, r_gs])
            P.op("act", lambda e: e.activation(out=gs["eg"][:], in_=gs["gc"][:], func=AF.Exp), writes=[r_gs])
            P.op("act", lambda e: e.activation(out=gs["egl"][:], in_=gs["glb"][:], func=AF.Exp), writes=[r_gs])
            P.op("dve", lambda e: e.tensor_tensor(out=tmp[:], in0=gs["glb"][:], in1=gs["gc"][:], op=ALU.subtract),
                 writes=[r_gs])
            P.op("act", lambda e: e.activation(out=gs["ekd"][:], in_=tmp[:], func=AF.Exp), writes=[r_gs])
            P.op("dve", lambda e: e.tensor_tensor(out=gs["kbs"][:], in0=gs["beta"][:], in1=gs["eg"][:], op=ALU.mult),
                 writes=[r_gs])
        for nm in ("beta", "gc", "eg", "ekd", "egl", "kbs"):
            dump(k, "gs_" + nm, gs[nm][:], [r_gs])
        P.barrier()
        wfr = Ring(P, "g_wf", [128, KC, 512], BF16, 2, st)
        wzr = Ring(P, "g_wz", [128, KC, 256], BF16, 1, st)
        qT = P.sbuf("g_qT", [128, S], BF16, st)
        kT = P.sbuf("g_kT", [128, S], BF16, st)
        vT = P.sbuf("g_vT", [128, 2, S], BF16, st)
        r_q = [Res() for _ in range(NB)]
        r_k = [Res() for _ in range(NB)]
        r_v = [Res() for _ in range(NB)]
        S32 = P.sbuf("g_S32", [128, 2, 128], F32, st)
        Sb = P.sbuf("g_Sb", [128, 2, 128], BF16, st)
        r_S32, r_Sb = Res(), Res()
        GT = [2, 3, 4]
        GR = [5, 6]
        GO = [7]
        GB = [0, 1]
        NL = GDN_LANES
        bc3 = lambda ap2: ap2.unsqueeze(1).to_broadcast([128, 2, 128])
        bcl = lambda ap2: ap2.unsqueeze(2).to_broadcast([128, 2, 128])
        v3 = lambda ap: ap.rearrange("p (a b) -> p a b", a=2)

        def chunk_gen(kh, tb, ch, wf, rwf, B):
            t0 = tb * 512
            rb = k.r_big[tb]
            chunk = (kh, 8 + kh, 16 + 2 * kh, 17 + 2 * kh)[ch]
            pp, rpp = k.ps[ch], k.rps[ch]
            for c in range(KC):
                P.op("pe", lambda e, c=c: e.matmul(
                    pp[:, :], lhsT=wf[:, c, ch * 128:(ch + 1) * 128], rhs=k.bigA[:, c, t0:t0 + 512],
                    start=(c == 0), stop=(c == KC - 1)), reads=[rwf, rb], writes=[rpp])
            yield
            pre, rpre = B["pre"][ch].next()
            halo, r_halo = B["halo"], B["r_halo"]
            P.op("act", lambda e: e.copy(out=pre[:, 3:515], in_=pp[:, :]), writes=[rpp, rpre])
            if tb == 0:
                P.op("pool", lambda e: e.memset(pre[:, 0:3], 0.0), writes=[rpre])
            else:
                P.op("pool", lambda e: e.tensor_copy(out=pre[:, 0:3], in_=halo[ch][:, 0:3]),
                     reads=[r_halo[ch]], writes=[rpre])
            yield
            P.op("pool", lambda e: e.tensor_copy(out=halo[ch][:, 0:3], in_=pre[:, 512:515]),
                 reads=[rpre], writes=[r_halo[ch]])
            cv, rcv = B["cv"][ch].next()
            P.op("dve", lambda e: e.tensor_scalar(
                out=cv[:], in0=pre[:, 3:515], scalar1=cwT[:, 3, chunk:chunk + 1], scalar2=None, op0=ALU.mult),
                reads=[rpre, r_lc], writes=[rcv])
            for tap in (2, 1, 0):
                P.op("dve", lambda e, tap=tap: e.scalar_tensor_tensor(
                    out=cv[:], in0=pre[:, tap:tap + 512], scalar=cwT[:, tap, chunk:chunk + 1], in1=cv[:],
                    op0=ALU.mult, op1=ALU.add), reads=[rpre, r_lc], writes=[rcv])
            yield
            if ch >= 2:
                P.op("act", lambda e: e.activation(out=vT[:, ch - 2, t0:t0 + 512], in_=cv[:], func=AF.Silu),
                     reads=[rcv], writes=[r_v[tb]])
                yield
                return
            P.op("act", lambda e: e.activation(out=cv[:], in_=cv[:], func=AF.Silu), writes=[rcv])
            yield
            sq, rsq = B["sq"][ch].next()
            P.op("pool", lambda e: e.tensor_tensor(out=sq[:], in0=cv[:], in1=cv[:], op=ALU.mult),
                 reads=[rcv], writes=[rsq])
            yield
            pss, rpss = k.ps[4 + ch], k.rps[4 + ch]
            P.op("pe", lambda e: e.matmul(pss[:, :], lhsT=k.onesb[:], rhs=sq[:], start=True, stop=True),
                 reads=[rsq, rc], writes=[rpss])
            yield
            ln, rln = B["ln"][ch].next()
            P.op("act", lambda e: e.activation(out=ln[:], in_=pss[:, :], func=AF.Ln, bias=k.epsc[:, 0:1]),
                 reads=[rc], writes=[rpss, rln])
            P.op("act", lambda e: e.activation(out=ln[:], in_=ln[:], func=AF.Exp, scale=-0.5), writes=[rln])
            yield
            dst, rdst, mul = ((qT, r_q, 128.0 ** -0.5), (kT, r_k, 1.0))[ch]
            P.op("dve", lambda e: e.scalar_tensor_tensor(
                out=dst[:, t0:t0 + 512], in0=cv[:], scalar=mul, in1=ln[:], op0=ALU.mult, op1=ALU.mult),
                reads=[rcv, rln], writes=[rdst[tb]])
            yield

        def pt_stage(kh, i, L, O):
            h0 = 2 * kh
            tb = i // 4
            tok = slice(i * 128, (i + 1) * 128)
            pair = lambda nm: gs[nm][:, i, h0:h0 + 2]
            col = lambda nm, hh: gs[nm][:, i, h0 + hh:h0 + hh + 1]
            kbg, rkbg = O["kbg"]
            kdec, rkdec = O["kdec"]
            bv, rbv = O["bv"]
            qkm, rqkm = O["qkm"]
            TT, rTT = O["TT"]
            nwT, rnwT = O["nwT"]
            hb = [0]

            def lbank():
                h = hb[0]
                hb[0] ^= 1
                return k.ps[L["bank"]][:, h * 256:(h + 1) * 256], k.rps[L["bank"]]
            pt, rpt = lbank()
            ptb = pt.bitcast(BF16)
            P.op("pe", lambda e: e.transpose(out=ptb[:, 0:128], in_=kT[:, tok], identity=k.ident[:]),
                 reads=[r_k[tb], rc], writes=[rpt])
            for hh in range(2):
                P.op("pe", lambda e, hh=hh: e.transpose(out=ptb[:, 128 + hh * 128:256 + hh * 128], in_=vT[:, hh, tok],
                                                        identity=k.ident[:]), reads=[r_v[tb], rc], writes=[rpt])
            dg, rdg = L["dg"]
            P.op("pool", lambda e: e.tensor_tensor(out=dg[:], in0=bc3(k.identf[:]), in1=bcl(pair("gc")), op=ALU.mult),
                 reads=[rc, r_gs], writes=[rdg])
            yield
            P.op("dve", lambda e: e.tensor_tensor(out=kbg[:], in0=bc3(ptb[:, 0:128]), in1=bcl(pair("kbs")), op=ALU.mult),
                 reads=[r_gs], writes=[rpt, rkbg])
            P.op("dve", lambda e: e.tensor_tensor(out=kdec[:], in0=bc3(ptb[:, 0:128]), in1=bcl(pair("ekd")), op=ALU.mult),
                 reads=[r_gs], writes=[rpt, rkdec])
            P.op("dve", lambda e: e.tensor_tensor(out=bv[:], in0=v3(ptb[:, 128:384]), in1=bcl(pair("beta")), op=ALU.mult),
                 reads=[r_gs], writes=[rpt, rbv])
            pa, rpa = lbank()
            for hh in range(2):
                P.op("pe", lambda e, hh=hh: e.matmul(pa[:, hh * 128:(hh + 1) * 128], lhsT=dg[:, hh, :], rhs=k.onesf[:],
                                                     start=True, stop=False), reads=[rdg, rc], writes=[rpa])
                P.op("pe", lambda e, hh=hh: e.matmul(pa[:, hh * 128:(hh + 1) * 128], lhsT=k.negonesf[:], rhs=dg[:, hh, :],
                                                     start=False, stop=True), reads=[rdg, rc], writes=[rpa])
            pk, rpk = lbank()
            P.op("pe", lambda e: e.matmul(pk[:, 0:128], lhsT=kT[:, tok], rhs=kT[:, tok], start=True, stop=True),
                 reads=[r_k[tb]], writes=[rpk])
            P.op("pe", lambda e: e.matmul(pk[:, 128:256], lhsT=kT[:, tok], rhs=qT[:, tok], start=True, stop=True),
                 reads=[r_k[tb], r_q[tb]], writes=[rpk])
            yield
            dm, rdm = L["dm"]
            dmt, rdmt = L["dmt"]
            P.op("dve", lambda e: e.scalar_tensor_tensor(out=dm[:], in0=v3(pa[:, 0:256]), scalar=0.0, in1=bc3(neglt[:]),
                                                         op0=ALU.min, op1=ALU.add), reads=[r_lc], writes=[rpa, rdm])
            P.op("dve", lambda e: e.scalar_tensor_tensor(out=dmt[:], in0=v3(pa[:, 0:256]), scalar=0.0, in1=bc3(posue[:]),
                                                         op0=ALU.max, op1=ALU.add), reads=[r_lc], writes=[rpa, rdmt])
            yield
            P.op("act", lambda e: e.activation(out=dm[:], in_=dm[:], func=AF.Exp), writes=[rdm])
            P.op("act", lambda e: e.activation(out=dmt[:], in_=dmt[:], func=AF.Exp, scale=-1.0), writes=[rdmt])
            yield
            A, rA = L["A"]
            for hh in range(2):
                P.op("dve", lambda e, hh=hh: e.scalar_tensor_tensor(
                    out=A[:, hh, :], in0=pk[:, 0:128], scalar=col("beta", hh), in1=dm[:, hh, :],
                    op0=ALU.mult, op1=ALU.mult), reads=[rdm, r_gs], writes=[rpk, rA])
            P.op("dve", lambda e: e.tensor_tensor(out=qkm[:], in0=bc3(pk[:, 128:256]), in1=dmt[:], op=ALU.mult),
                 reads=[rdmt], writes=[rpk, rqkm])
            yield
            pm, rpm = lbank()
            pmb = pm.bitcast(BF16)
            for hh in range(2):
                P.op("pe", lambda e, hh=hh: e.transpose(out=pmb[:, hh * 128:(hh + 1) * 128], in_=A[:, hh, :],
                                                        identity=k.ident[:]), reads=[rA, rc], writes=[rpm])
            Am, rAm = L["Mo"][0]
            P.op("pool", lambda e: e.tensor_tensor(out=Am[:], in0=A[:], in1=bc3(gm[:, 0, :]), op=ALU.mult),
                 reads=[rA, r_lc], writes=[rAm])
            yield
            M, rM = L["M"]
            P.op("act", lambda e: e.copy(out=M[:], in_=v3(pmb[:, 0:256])), writes=[rpm, rM])
            UV, rUV = L["UV"][0]
            P.op("dve", lambda e, UV=UV: e.tensor_tensor(out=UV[:, 1], in0=bc3(k.ident[:]), in1=Am[:], op=ALU.subtract),
                 reads=[rAm, rc], writes=[rUV])
            yield
            Mm, rMm = L["Mo"][1]
            P.op("pool", lambda e: e.tensor_tensor(out=Mm[:], in0=M[:], in1=bc3(gm[:, 1, :]), op=ALU.mult),
                 reads=[rM, r_lc], writes=[rMm])
            yield
            P.op("dve", lambda e, UV=UV: e.tensor_tensor(out=UV[:, 0], in0=bc3(k.ident[:]), in1=Mm[:], op=ALU.subtract),
                 reads=[rMm, rc], writes=[rUV])
            Mo, rMo = L["Mo"][0]
            P.op("pool", lambda e, Mo=Mo: e.tensor_tensor(out=Mo[:], in0=M[:], in1=bc3(gm[:, 2, :]), op=ALU.mult),
                 reads=[rM, r_lc], writes=[rMo])
            yield
            pbank, rpbank = k.ps[L["bank"]], k.rps[L["bank"]]
            for lv in range(6):
                pY, rpY = lbank()
                for hh in range(2):
                    P.op("pe", lambda e, hh=hh, pY=pY, Mo=Mo, UV=UV: e.matmul(
                        pY[:, hh * 128:(hh + 1) * 128], lhsT=Mo[:, hh, :], rhs=UV[:, 1, hh, :], start=True, stop=True),
                        reads=[rMo, rUV], writes=[rpY])
                yield
                Y, rY = L["Y"]
                P.op("act", lambda e, Y=Y, pY=pY: e.mul(out=Y[:], in_=v3(pY[:, 0:256]), mul=-1.0), writes=[rpY, rY])
                if lv < 5:
                    Mo2, rMo2 = L["Mo"][(lv + 1) % 2]
                    P.op("pool", lambda e, Mo2=Mo2, lv=lv: e.tensor_tensor(out=Mo2[:], in0=M[:], in1=bc3(gm[:, 3 + lv, :]),
                                                                           op=ALU.mult), reads=[rM, r_lc], writes=[rMo2])
                yield
                for hh in range(2):
                    P.op("pe", lambda e, hh=hh, UV=UV: e.matmul(
                        pbank[:, hh * 128:(hh + 1) * 128], lhsT=k.ident[:], rhs=UV[:, 0, hh, :], start=True, stop=False),
                        reads=[rUV, rc], writes=[rpbank])
                    P.op("pe", lambda e, hh=hh, UV=UV, Y=Y: e.matmul(
                        pbank[:, hh * 128:(hh + 1) * 128], lhsT=Y[:, hh, :], rhs=UV[:, 0, hh, :], start=False, stop=True),
                        reads=[rUV, rY], writes=[rpbank])
                if lv < 5:
                    for hh in range(2):
                        P.op("pe", lambda e, hh=hh, UV=UV: e.matmul(
                            pbank[:, 256 + hh * 128:384 + hh * 128], lhsT=k.ident[:], rhs=UV[:, 1, hh, :], start=True, stop=False),
                            reads=[rUV, rc], writes=[rpbank])
                        P.op("pe", lambda e, hh=hh, UV=UV, Y=Y: e.matmul(
                            pbank[:, 256 + hh * 128:384 + hh * 128], lhsT=UV[:, 0, hh, :], rhs=Y[:, hh, :], start=False, stop=True),
                            reads=[rUV, rY], writes=[rpbank])
                yield
                if lv < 5:
                    UVn, rUVn = L["UV"][(lv + 1) % 2]
                    if lv % 2:
                        P.op("act", lambda e, UVn=UVn: e.copy(out=UVn[:].rearrange("p a b c -> p (a b c)"), in_=pbank[:, :]),
                             writes=[rpbank, rUVn])
                    else:
                        P.op("dve", lambda e, UVn=UVn: e.tensor_copy(out=UVn[:].rearrange("p a b c -> p (a b c)"), in_=pbank[:, :]),
                             writes=[rpbank, rUVn])
                    UV, rUV = UVn, rUVn
                    Mo, rMo = Mo2, rMo2
                else:
                    P.op("dve", lambda e: e.tensor_copy(out=TT[:], in_=v3(pbank[:, 0:256])), writes=[rpbank, rTT])
                hb[0] = 0
                yield
            pw, rpw = lbank()
            for hh in range(2):
                P.op("pe", lambda e, hh=hh: e.matmul(pw[:, hh * 128:(hh + 1) * 128], lhsT=kbg[:, hh, :], rhs=TT[:, hh, :],
                                                     start=True, stop=True), reads=[rkbg, rTT], writes=[rpw])
            yield
            P.op("act", lambda e: e.mul(out=nwT[:], in_=v3(pw[:, 0:256]), mul=-1.0), writes=[rpw, rnwT])
            yield

        def r_stage(kh, i, O, RB, done):
            h0 = 2 * kh
            tb = i // 4
            tok = slice(i * 128, (i + 1) * 128)
            pair = lambda nm: gs[nm][:, i, h0:h0 + 2]
            col = lambda nm, hh: gs[nm][:, i, h0 + hh:h0 + hh + 1]
            TT, rTT = O["TT"]
            nwT, rnwT = O["nwT"]
            bv, rbv = O["bv"]
            kdec, rkdec = O["kdec"]
            qkm, rqkm = O["qkm"]
            pv, rpv = k.ps[4][:, 0:256], k.rps[4]
            for hh in range(2):
                P.op("pe", lambda e, hh=hh: e.matmul(pv[:, hh * 128:(hh + 1) * 128], lhsT=TT[:, hh, :], rhs=bv[:, hh, :],
                                                     start=True, stop=False), reads=[rTT, rbv], writes=[rpv])
                P.op("pe", lambda e, hh=hh: e.matmul(pv[:, hh * 128:(hh + 1) * 128], lhsT=nwT[:, hh, :], rhs=Sb[:, hh, :],
                                                     start=False, stop=True), reads=[rnwT, r_Sb], writes=[rpv])
            pz, rpz = k.ps[5], k.rps[5]
            for hh in range(2):
                P.op("pe", lambda e, hh=hh: e.matmul(pz[:, hh * 128:(hh + 1) * 128], lhsT=qT[:, tok], rhs=Sb[:, hh, :],
                                                     start=True, stop=True), reads=[r_q[tb], r_Sb], writes=[rpz])
            yield
            vn, rvn = RB["vn"].next()
            P.op("act", lambda e: e.copy(out=vn[:], in_=v3(pv[:, 0:256])), writes=[rpv, rvn])
            yield
            pd, rpd = k.ps[4][:, 256:512], k.rps[4]
            for hh in range(2):
                P.op("pe", lambda e, hh=hh: e.matmul(pd[:, hh * 128:(hh + 1) * 128], lhsT=kdec[:, hh, :], rhs=vn[:, hh, :],
                                                     start=True, stop=True), reads=[rkdec, rvn], writes=[rpd])
            for hh in range(2):
                P.op("pe", lambda e, hh=hh: e.matmul(pz[:, 256 + hh * 128:384 + hh * 128], lhsT=qkm[:, hh, :], rhs=vn[:, hh, :],
                                                     start=True, stop=True), reads=[rqkm, rvn], writes=[rpz])
            yield
            for hh in range(2):
                P.op("dve", lambda e, hh=hh: e.scalar_tensor_tensor(
                    out=S32[:, hh, :], in0=S32[:, hh, :], scalar=col("egl", hh), in1=pd[:, hh * 128:(hh + 1) * 128],
                    op0=ALU.mult, op1=ALU.add), reads=[r_gs], writes=[rpd, r_S32])
            zs, rzs = RB["zs"].next()
            P.op("dve", lambda e: e.tensor_tensor(out=zs[:], in0=v3(pz[:, 0:256]), in1=bcl(pair("eg")), op=ALU.mult),
                 reads=[r_gs], writes=[rpz, rzs])
            yield
            P.op("pool", lambda e: e.tensor_copy(out=Sb[:], in_=S32[:]), reads=[r_S32], writes=[r_Sb])
            o32, ro32 = RB["o32"].next()
            P.op("dve", lambda e: e.tensor_tensor(out=o32[:], in0=v3(pz[:, 256:512]), in1=zs[:], op=ALU.add),
                 reads=[rzs], writes=[rpz, ro32])
            done[i] = (o32, ro32)
            yield

        def o_stage(kh, i, o32, ro32, wz, rwz, RB, oT4, roT4):
            h0 = 2 * kh
            tb = i // 4
            tok = slice(i * 128, (i + 1) * 128)
            pzz, rpzz = k.ps[6][:, 0:256], k.rps[6]
            for c in range(KC):
                P.op("pe", lambda e, c=c: e.matmul(pzz[:, 0:256], lhsT=k.bigA[:, c, tok], rhs=wz[:, c, :],
                                                   start=(c == 0), stop=(c == KC - 1)), reads=[rwz, k.r_big[tb]], writes=[rpzz])
            junk, rjunk = RB["junk"].next()
            ssq, rssq = RB["ssq"].next()
            for hh in range(2):
                P.op("act", lambda e, hh=hh: e.activation(out=junk[:, hh, :], in_=o32[:, hh, :], func=AF.Square,
                                                          accum_out=ssq[:, hh:hh + 1]), reads=[ro32], writes=[rjunk, rssq])
            yield
            zz, rzz = RB["zz"].next()
            P.op("act", lambda e: e.activation(out=zz[:], in_=pzz[:, 0:256], func=AF.Silu), writes=[rpzz, rzz])
            P.op("dve", lambda e: e.tensor_scalar(out=ssq[:, 2:4], in0=ssq[:, 0:2], scalar1=1.0 / 128, scalar2=EPS,
                                                  op0=ALU.mult, op1=ALU.add), writes=[rssq])
            yield
            P.op("pool", lambda e: e.tensor_tensor(out=ssq[:, 4:6], in0=ssq[:, 2:4], in1=k.m05[:, 0:1].to_broadcast([128, 2]),
                                                   op=ALU.pow), reads=[rc], writes=[rssq])
            yield
            t1, rt1 = RB["t1"].next()
            for hh in range(2):
                P.op("dve", lambda e, hh=hh: e.scalar_tensor_tensor(
                    out=t1[:, hh, :], in0=o32[:, hh, :], scalar=ssq[:, 4 + hh:5 + hh], in1=gon[:],
                    op0=ALU.mult, op1=ALU.mult), reads=[ro32, rssq, r_lc], writes=[rt1])
            yield
            ob, rob = RB["ob"].next()
            P.op("pool", lambda e: e.tensor_tensor(out=ob[:], in0=t1[:], in1=v3(zz[:]), op=ALU.mult),
                 reads=[rt1, rzz], writes=[rob])
            yield
            po, rpo = k.ps[6][:, 256:512], k.rps[6]
            pob = po.bitcast(BF16)
            for hh in range(2):
                P.op("pe", lambda e, hh=hh: e.transpose(out=pob[:, hh * 128:(hh + 1) * 128], in_=ob[:, hh, :],
                                                        identity=k.ident[:]), reads=[rob, rc], writes=[rpo])
            yield
            q4 = i % 4
            P.op("act", lambda e: e.copy(out=oT4[:, :, q4 * 128:(q4 + 1) * 128], in_=v3(pob[:, 0:256])),
                 writes=[rpo, roT4])
            if q4 == 3:
                P.dma(oTd[:, h0:h0 + 2, tb * 512:(tb + 1) * 512], oT4[:], reads=[roT4], writes=[r_oTd[tb]])
            yield

        def delayed(g, d):
            for _ in range(d):
                yield
            yield from g

        def run_lanes(gens):
            active = list(gens)
            while active:
                for g in list(active):
                    try:
                        next(g)
                    except StopIteration:
                        active.remove(g)

        def group(kh):
            wf, rwf = wfr.next()
            wz, rwz = wzr.next()
            for c in range(0, KC, 4):
                P.dma(wf[:, c:c + 4, 0:128], w_in[:, c:c + 4, kh * 128:(kh + 1) * 128], writes=[rwf], eng="pool")
                P.dma(wf[:, c:c + 4, 128:256], w_in[:, c:c + 4, 1024 + kh * 128:1024 + (kh + 1) * 128], writes=[rwf], eng="pool")
                P.dma(wf[:, c:c + 4, 256:512], w_in[:, c:c + 4, 2048 + kh * 256:2048 + (kh + 1) * 256], writes=[rwf], eng="pool")
                P.dma(wz[:, c:c + 4, :], w_in[:, c:c + 4, 4096 + kh * 256:4096 + (kh + 1) * 256], writes=[rwz], eng="pool")
            with contextlib.ExitStack() as stb:
                B = dict(pre=[Ring(P, "g_pre", [128, 515], F32, 2, stb) for _ in range(4)],
                         halo=[P.sbuf(f"g_halo{ch}", [128, 4], F32, stb) for ch in range(4)],
                         r_halo=[Res() for _ in range(4)],
                         cv=[Ring(P, "g_cv", [128, 512], F32, 2, stb) for _ in range(4)],
                         sq=[Ring(P, "g_sq", [128, 512], BF16, 2, stb) for _ in range(2)],
                         ln=[Ring(P, "g_ln", [128, 512], F32, 2, stb) for _ in range(2)])
                for tb in range(NB):
                    run_lanes([chunk_gen(kh, tb, ch, wf, rwf, B) for ch in range(4)])
            P.barrier()
            with contextlib.ExitStack() as stt:
                T3 = lambda nm, dt: (P.sbuf(nm, [128, 2, 128], dt, stt), Res())
                lanes = []
                NSLOT = NL + 2
                for ln_ in range(NL):
                    dgt = T3("g_dg", F32)
                    At = T3("g_A", BF16)
                    lanes.append(dict(bank=(0, 1, 2, 3, 7)[ln_], dg=dgt, dm=T3("g_dm", F32), dmt=dgt,
                                      A=At, M=T3("g_M", BF16),
                                      Mo=[T3("g_Mo", BF16), T3("g_Mo", BF16)],
                                      UV=[(P.sbuf("g_UV", [128, 2, 2, 128], BF16, stt), Res()) for _ in range(2)],
                                      Y=At))
                oslots = [dict((nm, T3("g_" + nm, BF16)) for nm in ("kbg", "kdec", "bv", "qkm", "TT", "nwT"))
                          for _ in range(NSLOT)]
                R3 = lambda nm, dt, n: Ring(P, nm, [128, 2, 128], dt, n, stt)
                RB = dict(vn=R3("g_vn", BF16, 2), zs=R3("g_zs", F32, 1), o32=R3("g_o32", F32, 4),
                          junk=R3("g_junk", F32, 1), ssq=Ring(P, "g_ssq", [128, 8], F32, 2, stt),
                          zz=Ring(P, "g_zz", [128, 256], F32, 2, stt), t1=R3("g_t1", F32, 1), ob=R3("g_ob", BF16, 2))
                oT4r = Ring(P, "g_oT4", [128, 2, 512], BF16, 2, stt)
                P.op("pool", lambda e: e.memset(S32[:], 0.0), writes=[r_S32])
                P.op("pool", lambda e: e.memset(Sb[:], 0.0), writes=[r_Sb])
                st4 = {}
                done = {}
                ptdone = set()
                rfin = set()
                ofin = set()

                def pt_worker(ln_):
                    for _ in range(ln_ * GDN_STAGGER):
                        yield
                    for i in range(ln_, NT, NL):
                        while i >= NSLOT and (i - NSLOT) not in rfin:
                            yield
                        yield from pt_stage(kh, i, lanes[ln_], oslots[i % NSLOT])
                        ptdone.add(i)

                def r_worker():
                    for i in range(NT):
                        while i not in ptdone or (i >= 3 and (i - 3) not in ofin):
                            yield
                        yield from r_stage(kh, i, oslots[i % NSLOT], RB, done)
                        rfin.add(i)

                def o_worker():
                    for i in range(NT):
                        while i not in done:
                            yield
                        if i % 4 == 0:
                            st4["o"] = oT4r.next()
                        oT4, roT4 = st4["o"]
                        o32, ro32 = done[i]
                        yield from o_stage(kh, i, o32, ro32, wz, rwz, RB, oT4, roT4)
                        ofin.add(i)

                run_lanes([r_worker(), o_worker()] + [pt_worker(ln_) for ln_ in range(NL)])
            P.barrier()

        for kh in range(8):
            group(kh)
    P.barrier()
    with contextlib.ExitStack() as st:
        otr = Ring(P, "g_oTt", [128, 16, 128], BF16, 3, st)
        cur = {}

        def pre_tile(i):
            t, r = otr.next()
            P.dma(t[:], oTd[:, :, i * 128:(i + 1) * 128], reads=[r_oTd[i // 4]], writes=[r])
            cur["t"], cur["r"] = t, r
            return t, r

        out_proj(k, k.gdn_w_out[j], 16, None, None, x_src, pre_tile=pre_tile, nn=nn)


def mla_layer(k, j, x_src):
    P = k.P
    rc = k.r_const
    NB = S // 512
    scale = 192.0 ** -0.5
    with contextlib.ExitStack() as st:
        rope_table(k, st)
        cqn = P.sbuf("cqn", [128, 3, S], BF16, st)
        ckvn = P.sbuf("ckvn", [128, 2, S], BF16, st)
        k2 = P.sbuf("k2", [128, S], BF16, st)
        r_cq = [Res() for _ in range(NB)]
        r_ckv = [Res() for _ in range(NB)]
        r_k2 = [Res() for _ in range(NB)]
        qg = k.gainT[:, 64 + 3 * j:64 + 3 * j + 3]
        kvg = k.gainT[:, 70 + 2 * j:70 + 2 * j + 2]
        with contextlib.ExitStack() as st1:
            win = P.sbuf("m_win", [128, KC, 768], BF16, st1)
            r_win = Res()
            w_in = k.mla_w_in[j].rearrange("(c p) n -> p c n", p=128)
            for c in range(0, KC, 2):
                P.dma(win[:, c:c + 2, 0:704], w_in[:, c:c + 2, :], writes=[r_win], eng="pool")
            vs = win[:, :, 640:704].rearrange("p c (i two) -> p c i two", two=2)
            vd = win[:, :, 704:768].rearrange("p c (i two) -> p c i two", two=2)
            P.op("act", lambda e: e.mul(out=vd[:, :, :, 0], in_=vs[:, :, :, 1], mul=-1.0), writes=[r_win])
            P.op("act", lambda e: e.copy(out=vd[:, :, :, 1], in_=vs[:, :, :, 0]), writes=[r_win])
            sqr = Ring(P, "m_sq", [128, 512], BF16, 4, st1)
            lnr = Ring(P, "m_ln", [128, 512], F32, 2, st1)
            rsr = Ring(P, "m_rs", [128, 512], F32, 2, st1)
            prr = Ring(P, "m_pr", [128, 512], BF16, 2, st1)
            allb = [0, 1, 2, 3, 4, 5, 6, 7]

            def latent(tb, nch, col0, dst, rdst, gcol, width):
                t0 = tb * 512
                rb = k.r_big[tb]
                pqs = []
                sqs = []
                for c3 in range(nch):
                    pq, rpq = bank(k, "m1", allb)
                    for c in range(KC):
                        P.op("pe", lambda e, pq=pq, c=c, c3=c3: e.matmul(
                            pq[:, :], lhsT=win[:, c, col0 + c3 * 128:col0 + (c3 + 1) * 128],
                            rhs=k.bigA[:, c, t0:t0 + 512], start=(c == 0), stop=(c == KC - 1)),
                            reads=[r_win, rb], writes=[rpq])
                    sq, rsq = sqr.next()
                    P.op("act", lambda e, sq=sq, pq=pq: e.activation(out=sq[:], in_=pq[:, :], func=AF.Square),
                         writes=[rpq, rsq])
                    pqs.append((pq, rpq))
                    sqs.append((sq, rsq))
                pss, rpss = bank(k, "m1", allb)
                for c3 in range(nch):
                    P.op("pe", lambda e, c3=c3, sq=sqs[c3][0]: e.matmul(
                        pss[:, :], lhsT=k.onesb[:], rhs=sq[:], start=(c3 == 0), stop=(c3 == nch - 1)),
                        reads=[sqs[c3][1], rc], writes=[rpss])
                ln, rln = lnr.next()
                rs, rrs = rsr.next()
                P.op("act", lambda e: e.activation(out=ln[:], in_=pss[:, :], func=AF.Ln, scale=1.0 / width,
                                                   bias=k.epsc[:, 0:1]), reads=[rc], writes=[rpss, rln])
                P.op("act", lambda e: e.activation(out=rs[:], in_=ln[:], func=AF.Exp, scale=-0.5),
                     reads=[rln], writes=[rrs])
                for c3 in range(nch):
                    pq, rpq = pqs[c3]
                    P.op("dve", lambda e, pq=pq, c3=c3: e.scalar_tensor_tensor(
                        out=dst[:, c3, t0:t0 + 512], in0=pq[:, :], scalar=gcol[:, c3:c3 + 1], in1=rs[:],
                        op0=ALU.mult, op1=ALU.mult), reads=[rrs, rc], writes=[rpq, rdst[tb]])

            def rope_key(tb):
                t0 = tb * 512
                rb = k.r_big[tb]
                pk, rpk = bank(k, "m1", allb)
                for c in range(KC):
                    P.op("pe", lambda e, c=c: e.matmul(
                        pk[:, :], lhsT=win[:, c, 640:768], rhs=k.bigA[:, c, t0:t0 + 512],
                        start=(c == 0), stop=(c == KC - 1)), reads=[r_win, rb], writes=[rpk])
                pr, rpr = prr.next()
                P.op("dve", lambda e: e.tensor_tensor(out=pr[:], in0=pk[:, :], in1=k.cs[:, t0:t0 + 512],
                                                      op=ALU.mult), reads=[rc], writes=[rpk, rpr])
                pf, rpf = bank(k, "m1", allb)
                P.op("pe", lambda e: e.matmul(pf[:, :], lhsT=k.fold[:], rhs=pr[:], start=True, stop=True),
                     reads=[rpr, rc], writes=[rpf])
                P.op("act", lambda e: e.copy(out=k2[:, t0:t0 + 512], in_=pf[:, :]), writes=[rpf, r_k2[tb]])

            for tb in range(NB):
                latent(tb, 3, 0, cqn, r_cq, qg, 384.0)
                latent(tb, 2, 384, ckvn, r_ckv, kvg, 256.0)
                rope_key(tb)
        dump(k, "cs", k.cs[:, 0:512], [rc], F32)
        dump(k, "cqn", cqn[:, :, 0:512], [r_cq[0]])
        dump(k, "ckvn", ckvn[:, :, 0:512], [r_ckv[0]])
        dump(k, "k2", k2[:, 0:512], [r_k2[0]])
        P.barrier()
        with contextlib.ExitStack() as st2:
            wqr = Ring(P, "m_wq", [128, 3, 256], BF16, 2, st2)
            wkvr = Ring(P, "m_wkv", [128, 2, 256], BF16, 2, st2)
            qn = P.sbuf("m_qn", [128, S], BF16, st2)
            q2 = P.sbuf("m_q2", [128, S], BF16, st2)
            kn = P.sbuf("m_kn", [128, S], BF16, st2)
            vv = P.sbuf("m_v", [128, NT, 128], BF16, st2)
            r_qn = [Res() for _ in range(NB)]
            r_q2 = [Res() for _ in range(NB)]
            r_kn = [Res() for _ in range(NB)]
            r_v = [Res() for _ in range(NB)]
            ptr = Ring(P, "m_pT", [128, 512], BF16, 6, st2)
            rir = Ring(P, "m_ri", [128, 512], F32, 2, st2)
            accr = Ring(P, "m_acc", [128, 512], F32, 4, st2)
            wqu = k.mla_w_q_up[j].rearrange("(c p) n -> p c n", p=128)
            wkvu = k.mla_w_kv_up[j].rearrange("(c p) n -> p c n", p=128)
            pb = [4, 5, 6, 7]

            def head_proj(h, tb, wq, rwq, wkv, rwkv):
                t0 = tb * 512
                p1, rp1 = bank(k, "mp", pb)
                for c3 in range(3):
                    P.op("pe", lambda e, c3=c3: e.matmul(
                        p1[:, :], lhsT=wq[:, c3, 0:128], rhs=cqn[:, c3, t0:t0 + 512], start=(c3 == 0), stop=(c3 == 2)),
                        reads=[rwq, r_cq[tb]], writes=[rp1])
                P.op("act", lambda e: e.copy(out=qn[:, t0:t0 + 512], in_=p1[:, :]), writes=[rp1, r_qn[tb]])
                p2, rp2 = bank(k, "mp", pb)
                for c3 in range(3):
                    P.op("pe", lambda e, c3=c3: e.matmul(
                        p2[:, :], lhsT=wq[:, c3, 128:256], rhs=cqn[:, c3, t0:t0 + 512], start=(c3 == 0), stop=(c3 == 2)),
                        reads=[rwq, r_cq[tb]], writes=[rp2])
                P.op("dve", lambda e: e.tensor_tensor(out=q2[:, t0:t0 + 512], in0=p2[:, :],
                                                      in1=k.cs[:, t0:t0 + 512], op=ALU.mult),
                     reads=[rc], writes=[rp2, r_q2[tb]])
                p3, rp3 = bank(k, "mp", pb)
                for c2 in range(2):
                    P.op("pe", lambda e, c2=c2: e.matmul(
                        p3[:, :], lhsT=wkv[:, c2, 0:128], rhs=ckvn[:, c2, t0:t0 + 512], start=(c2 == 0), stop=(c2 == 1)),
                        reads=[rwkv, r_ckv[tb]], writes=[rp3])
                P.op("act", lambda e: e.copy(out=kn[:, t0:t0 + 512], in_=p3[:, :]), writes=[rp3, r_kn[tb]])
                p4, rp4 = bank(k, "mp", pb)
                for tt in range(4):
                    for c2 in range(2):
                        P.op("pe", lambda e, c2=c2, tt=tt: e.matmul(
                            p4[:, tt * 128:(tt + 1) * 128], lhsT=ckvn[:, c2, t0 + tt * 128:t0 + (tt + 1) * 128],
                            rhs=wkv[:, c2, 128:256], start=(c2 == 0), stop=(c2 == 1)),
                            reads=[rwkv, r_ckv[tb]], writes=[rp4])
                P.op("dve", lambda e: e.tensor_copy(
                    out=vv[:, tb * 4:(tb + 1) * 4, :].rearrange("p a b -> p (a b)"), in_=p4[:, :]),
                    writes=[rp4, r_v[tb]])

            def attn_qk(h, qb, kt, nkt, accs):
                c0 = max(0, kt * 128 - qb * 512)
                w = 512 - c0
                q0 = qb * 512 + c0
                pS, rpS = bank(k, "mp", pb)
                kb = kt // 4
                P.op("pe", lambda e: e.matmul(
                    pS[:, c0:512], lhsT=kn[:, kt * 128:(kt + 1) * 128], rhs=qn[:, q0:q0 + w], start=True, stop=False),
                    reads=[r_kn[kb], r_qn[qb]], writes=[rpS])
                P.op("pe", lambda e: e.matmul(
                    pS[:, c0:512], lhsT=k2[:, kt * 128:(kt + 1) * 128], rhs=q2[:, q0:q0 + w], start=False, stop=True),
                    reads=[r_k2[kb], r_q2[qb]], writes=[rpS])
                pT, rpT = ptr.next()
                P.op("act", lambda e: e.activation(out=pT[:, c0:512], in_=pS[:, c0:512], func=AF.Exp, scale=scale),
                     writes=[rpS, rpT])
                if kt >= 4 * qb:
                    P.op("pool", lambda e: e.memset(pT[64:128, c0:c0 + 64], 0.0), writes=[rpT])
                acc, racc = accs[kt % 2]
                eng = ("pool", "dve")[kt % 2]
                if kt < 2:
                    if c0 > 0:
                        P.op(eng, lambda e: e.memset(acc[:, 0:c0], 0.0), writes=[racc])
                    P.op(eng, lambda e: e.tensor_copy(out=acc[:, c0:512], in_=pT[:, c0:512]), reads=[rpT], writes=[racc])
                else:
                    P.op(eng, lambda e: e.tensor_tensor(out=acc[:, c0:512], in0=acc[:, c0:512], in1=pT[:, c0:512], op=ALU.add),
                         reads=[rpT], writes=[racc])
                return pT, rpT, c0

            def attn_pv(kt, nkt, po, rpo, pT, rpT, c0):
                kb = kt // 4
                P.op("pe", lambda e: e.matmul(
                    po[:, c0:512], lhsT=vv[:, kt, :], rhs=pT[:, c0:512], start=(kt == 0), stop=(kt == nkt - 1)),
                    reads=[r_v[kb], rpT], writes=[rpo])

            def attn_block(h, qb):
                po, rpo = bank(k, "ao", [0, 1])
                prs, rprs = bank(k, "ar", [2, 3])
                nkt = 4 * qb + 4
                accs = [accr.next(), accr.next()]
                LA = 3
                pend = {}
                for idx in range(nkt + LA):
                    if idx < nkt:
                        pend[idx] = attn_qk(h, qb, idx, nkt, accs)
                    if idx >= LA:
                        attn_pv(idx - LA, nkt, po, rpo, *pend.pop(idx - LA))
                P.op("dve", lambda e: e.tensor_tensor(out=accs[0][0][:], in0=accs[0][0][:], in1=accs[1][0][:], op=ALU.add),
                     reads=[accs[1][1]], writes=[accs[0][1]])
                P.op("pe", lambda e: e.matmul(prs[:, :], lhsT=k.onesf[:], rhs=accs[0][0][:], start=True, stop=True),
                     reads=[rc, accs[0][1]], writes=[rprs])
                ri, rri = rir.next()
                P.op("dve", lambda e: e.reciprocal(out=ri[:], in_=prs[:, :]), writes=[rprs, rri])
                P.op("dve", lambda e: e.tensor_tensor(
                    out=k.bigA[:, h, qb * 512:(qb + 1) * 512], in0=po[:, :], in1=ri[:], op=ALU.mult),
                    reads=[rri], writes=[rpo, k.r_big[qb]])

            def head(h):
                wq, rwq = wqr.next()
                wkv, rwkv = wkvr.next()
                P.dma(wq[:, :, 0:192], wqu[:, :, h * 192:(h + 1) * 192], writes=[rwq], eng="pool")
                P.dma(wkv[:, :, :], wkvu[:, :, h * 256:(h + 1) * 256], writes=[rwkv], eng="pool")
                vs = wq[:, :, 128:192].rearrange("p c (i two) -> p c i two", two=2)
                vd = wq[:, :, 192:256].rearrange("p c (i two) -> p c i two", two=2)
                P.op("act", lambda e: e.mul(out=vd[:, :, :, 0], in_=vs[:, :, :, 1], mul=-1.0), writes=[rwq])
                P.op("act", lambda e: e.copy(out=vd[:, :, :, 1], in_=vs[:, :, :, 0]), writes=[rwq])
                for tb in range(NB):
                    head_proj(h, tb, wq, rwq, wkv, rwkv)
                if h == 0:
                    dump(k, "qn", qn[:, 0:512], [r_qn[0]])
                    dump(k, "q2", q2[:, 0:512], [r_q2[0]])
                    dump(k, "kn", kn[:, 0:512], [r_kn[0]])
                    dump(k, "vv", vv[:, 0:4, :], [r_v[0]])
                for qb in range(NB):
                    attn_block(h, qb)

            for h in range(8):
                head(h)
        dump(k, "oT", k.bigA[:, :, 0:1024], [k.r_big[0], k.r_big[1]])
        P.barrier()
        out_proj(k, k.mla_w_out[j], 8, lambda h, i: k.bigA[:, h, i * 128:(i + 1) * 128],
                 lambda i: [k.r_big[i // 4]], x_src)


def out_proj(k, w_dram, nch, lhs_fn, lhs_res_fn, x_src, pre_tile=None, nn=None):
    P = k.P
    with contextlib.ExitStack() as st:
        wo = P.sbuf("wo", [128, nch, D], BF16, st)
        r_wo = Res()
        wv = w_dram.rearrange("(c p) n -> p c n", p=128)
        for c in range(0, nch, 2):
            P.dma(wo[:, c:c + 2, :], wv[:, c:c + 2, :], writes=[r_wo], eng="pool")
        xr = Ring(P, "ox", [128, D], F32, 3, st)
        NR = setup_next_norm(k, nn, st)

        def tile(i):
            if pre_tile is not None:
                lt, lr = pre_tile(i)
                lhs = lambda c: lt[:, c, :]
                lres = [lr]
            else:
                lhs = lambda c: lhs_fn(c, i)
                lres = lhs_res_fn(i)
            xt, rx = xr.next()
            P.dma(xt[:], x_src[i * 128:(i + 1) * 128, :], reads=k.xr(x_src, i), writes=[rx])
            for dh in range(2):
                po, rpo = bank(k, "op", [0, 1, 2, 3])
                for c in range(nch):
                    P.op("pe", lambda e, po=po, c=c, dh=dh: e.matmul(
                        po[:, :], lhsT=lhs(c), rhs=wo[:, c, dh * 512:(dh + 1) * 512],
                        start=(c == 0), stop=(c == nch - 1)), reads=[r_wo] + lres, writes=[rpo])
                P.op("dve", lambda e, po=po, dh=dh: e.tensor_tensor(
                    out=xt[:, dh * 512:(dh + 1) * 512], in0=po[:, :], in1=xt[:, dh * 512:(dh + 1) * 512], op=ALU.add),
                    writes=[rpo, rx])
            P.dma(k.xres[i * 128:(i + 1) * 128, :], xt[:], reads=[rx], writes=[k.r_xres[i]])
            if nn is not None:
                norm_tile(k, xt, rx, i, nn, NR, [4, 5])

        for i in range(NT):
            tile(i)


def _invf_table():
    inv_freq = (10000.0 ** (-np.arange(0, 64, 2, dtype=np.float32) / 64.0)).astype(np.float32)
    t = np.zeros((128, 2), np.float32)
    for p in range(128):
        t[p, 0] = inv_freq[(p % 64) // 2] / (2.0 * math.pi)
        t[p, 1] = 0.25 if p < 64 else 0.0
    return t


def _gdn_masks():
    i = np.arange(128)[:, None]
    j = np.arange(128)[None, :]
    m = np.zeros((128, 8, 128), np.float32)
    m0 = ((i // 2 == j // 2) & (i > j)).astype(np.float32)
    m[:, 0, :] = m0
    m[:, 1, :] = m0.T
    for lv, b in enumerate((2, 4, 8, 16, 32, 64)):
        ma = ((i // (2 * b) == j // (2 * b)) & ((i // b) % 2 == 1) & ((j // b) % 2 == 0)).astype(np.float32)
        m[:, 2 + lv, :] = ma.T
    return m


_NC_CACHE = {}


def kernel(**inputs):
    key = "full"
    if key not in _NC_CACHE:
        _NC_CACHE[key] = build()
    nc = _NC_CACHE[key]
    n = 8
    shared = {}
    for name, v in inputs.items():
        if name in ("x", "positions"):
            continue
        a = np.ascontiguousarray(np.asarray(v))
        if name == "final_norm":
            a = a.reshape(1, D)
        shared[name] = a
    shared["invf"] = _invf_table()
    shared["gmask"] = _gdn_masks()
    x = np.asarray(inputs["x"])
    pos = np.asarray(inputs["positions"])
    in_maps = []
    for i in range(n):
        m = dict(shared)
        m["x"] = np.ascontiguousarray(x[i])
        m["positions"] = np.ascontiguousarray(pos[i].reshape(1, S).astype(np.int32))
        in_maps.append(m)
    res = run_bass_kernel_spmd(nc, in_maps, core_ids=list(range(n)))
    _NC_CACHE["last"] = res
    return np.stack([np.asarray(res.results[i]["out"]) for i in range(n)], axis=0).astype(np.float32)
```

```python
import contextlib
import math
import numpy as np
import concourse.bass as bass
import concourse.mybir as mybir
from concourse.alu_op_type import AluOpType as ALU
from concourse.bass_utils import run_bass_kernel_spmd

AF = mybir.ActivationFunctionType
F32 = mybir.dt.float32
BF16 = mybir.dt.bfloat16
I32 = mybir.dt.int32

SEM_LIM = 30000
N_DMA_SEMS = 24

S = 4096
D = 1024
NT = S // 128
KC = D // 128
DEPTH = 4
EPS = 1e-6
D_FF = 2816
FC = D_FF // 128
GDN_IN = 6176
MLA_IN = 704


class Res:
    __slots__ = ("last_w", "readers")

    def __init__(self):
        self.last_w = None
        self.readers = []


class Op:
    __slots__ = ("eng", "fn", "deps", "is_dma", "signal", "sem", "val", "dsem", "dprev")

    def __init__(self, eng, fn, is_dma):
        self.eng = eng
        self.fn = fn
        self.is_dma = is_dma
        self.deps = ()
        self.signal = False
        self.sem = None
        self.val = 0
        self.dsem = None
        self.dprev = 0


class Prog:
    ENGS = ("pe", "act", "dve", "pool", "sp")

    def __init__(self, nc):
        self.nc = nc
        self.ops = {e: [] for e in self.ENGS}
        self.stack = contextlib.ExitStack()
        self.dma_cnt = [0] * N_DMA_SEMS
        self.dma_last = [None] * N_DMA_SEMS
        self.dma_rr = 0
        self.uid = 0

    def sbuf(self, name, shape, dtype, stack=None):
        self.uid += 1
        return (stack or self.stack).enter_context(
            self.nc.sbuf_tensor(f"{name}_{self.uid}", list(shape), dtype))

    def psum(self, name, shape, dtype):
        return self.stack.enter_context(self.nc.psum_tensor(name, list(shape), dtype))

    def op(self, eng, fn, reads=(), writes=(), dma=False, extra_deps=()):
        o = Op(eng, fn, dma)
        deps = set(extra_deps)
        for r in reads:
            if r.last_w is not None:
                deps.add(r.last_w)
        for r in writes:
            if r.last_w is not None:
                deps.add(r.last_w)
            deps.update(r.readers)
        for r in reads:
            r.readers.append(o)
        for r in writes:
            r.last_w = o
            r.readers = []
        deps.discard(o)
        if eng == "pe":
            deps = {d for d in deps if d.is_dma or d.eng != "pe"}
        o.deps = deps
        for d in deps:
            d.signal = True
        if dma:
            k = self.dma_rr
            self.dma_rr = (self.dma_rr + 1) % N_DMA_SEMS
            o.dprev = self.dma_cnt[k] * 16
            self.dma_cnt[k] += 1
            o.dsem = k
            o.val = self.dma_cnt[k] * 16
            self.dma_last[k] = o
        self.ops[eng].append(o)
        return o

    def dma(self, out, in_, reads=(), writes=(), eng="sp", **kw):
        return self.op(eng, lambda e: e.dma_start(out=out, in_=in_, **kw), reads, writes, dma=True)

    def barrier(self):
        lasts = []
        for e in self.ENGS:
            for o in reversed(self.ops[e]):
                if not o.is_dma and o.fn is not None:
                    lasts.append(o)
                    break
        lasts += [o for o in self.dma_last if o is not None]
        for e in self.ENGS:
            self.op(e, None, extra_deps=lasts)

    def emit(self, final_wait_ops=()):
        nc = self.nc
        st = self.stack
        nsem = {}
        for e in self.ENGS:
            cnt = 0
            for o in self.ops[e]:
                if o.is_dma or o.fn is None:
                    continue
                if o.signal:
                    o.sem = (e, cnt // SEM_LIM)
                    o.val = cnt % SEM_LIM + 1
                    cnt += 1
            nsem[e] = (cnt + SEM_LIM - 1) // SEM_LIM
        sems = {}
        for e in self.ENGS:
            for k in range(nsem[e]):
                sems[(e, k)] = st.enter_context(nc.semaphore(f"s_{e}{k}"))
        dsems = [st.enter_context(nc.semaphore(f"s_dma{k}")) for k in range(N_DMA_SEMS)]
        block = st.enter_context(nc.Block())
        battr = {"pe": "tensor", "act": "scalar", "dve": "vector", "pool": "gpsimd", "sp": "sync"}

        def run_engine(ename, eng):
            waited = {}
            for o in self.ops[ename]:
                need = {}
                for d in o.deps:
                    if d.is_dma:
                        key = ("d", d.dsem)
                        s = dsems[d.dsem]
                    elif d.fn is None:
                        continue
                    else:
                        key = d.sem
                        s = sems[d.sem]
                    if need.get(key, (None, 0))[1] < d.val:
                        need[key] = (s, d.val)
                if o.is_dma and o.dprev > 0:
                    key = ("d", o.dsem)
                    if need.get(key, (None, 0))[1] < o.dprev:
                        need[key] = (dsems[o.dsem], o.dprev)
                for key, (s, v) in need.items():
                    if waited.get(key, 0) < v:
                        eng.wait_ge(s, v)
                        waited[key] = v
                if o.fn is None:
                    continue
                ins = o.fn(eng)
                if o.is_dma:
                    ins.then_inc(dsems[o.dsem], 16)
                elif o.signal:
                    ins.then_inc(sems[o.sem], 1)
            if ename == "sp":
                for o in final_wait_ops:
                    eng.wait_ge(dsems[o.dsem], o.val)

        for ename in self.ENGS:
            def mk(ename=ename):
                def body(eng):
                    run_engine(ename, eng)
                return body
            getattr(block, battr[ename])(mk())


class Ring:
    def __init__(self, P, name, shape, dtype, n, stack=None):
        self.t = [P.sbuf(f"{name}{i}", shape, dtype, stack) for i in range(n)]
        self.r = [Res() for _ in range(n)]
        self.i = 0
        self.n = n

    def next(self):
        k = self.i
        self.i = (self.i + 1) % self.n
        return self.t[k], self.r[k]


class K:
    pass


def dump(k, name, ap, reads, dt=BF16):
    if not getattr(k, "dbg", False):
        return
    shape = list(ap.shape)
    t = k.nc.dram_tensor("dbg_" + name, shape, F32, kind="ExternalOutput").ap()
    k.dbg_ops.append(k.P.dma(t, ap, reads=reads, eng="pool"))


def build(n_layers=DEPTH, do_mix=True, do_ffn=True, dbg=False):
    nc = bass.Bass("TRN2", target_bir_lowering=False)
    P = Prog(nc)
    k = K()
    k.nc, k.P = nc, P
    k.dbg = dbg
    k.dbg_ops = []

    def din(name, shape, dt=F32):
        return nc.dram_tensor(name, list(shape), dt, kind="ExternalInput").ap()

    k.x_in = din("x", [S, D])
    k.pos = din("positions", [1, S], I32)
    k.norm_mix = din("norm_mix", [DEPTH, D])
    k.norm_ffn = din("norm_ffn", [DEPTH, D])
    k.gdn_w_in = din("gdn_w_in", [2, D, GDN_IN])
    k.gdn_conv_w = din("gdn_conv_w", [2, 4, 4096])
    k.gdn_a_log = din("gdn_a_log", [2, 16])
    k.gdn_dt_bias = din("gdn_dt_bias", [2, 16])
    k.gdn_out_norm = din("gdn_out_norm", [2, 128])
    k.gdn_w_out = din("gdn_w_out", [2, 2048, D])
    k.mla_w_in = din("mla_w_in", [2, D, MLA_IN])
    k.mla_q_norm = din("mla_q_norm", [2, 384])
    k.mla_w_q_up = din("mla_w_q_up", [2, 384, 1536])
    k.mla_kv_norm = din("mla_kv_norm", [2, 256])
    k.mla_w_kv_up = din("mla_w_kv_up", [2, 256, 2048])
    k.mla_w_out = din("mla_w_out", [2, D, D])
    k.ffn_w_gate_up = din("ffn_w_gate_up", [DEPTH, D, 2 * D_FF])
    k.ffn_w_down = din("ffn_w_down", [DEPTH, D_FF, D])
    k.final_norm = din("final_norm", [1, D])
    k.invf = din("invf", [128, 2])
    k.gmask = din("gmask", [128, 8, 128])
    k.out = nc.dram_tensor("out", [S, D], F32, kind="ExternalOutput").ap()
    k.xres = nc.dram_tensor("xres", [S, D], F32, kind="Internal").ap()
    k.oT_d = nc.dram_tensor("oT_d", [2048, S], BF16, kind="Internal").ap()
    k.r_xres = [Res() for _ in range(NT)]
    k.xr = lambda src, i: [k.r_xres[i]] if src is k.xres else []

    with P.stack:
        setup_consts(k)
        x_src = k.x_in
        k.final_ops = []
        fuse = (do_mix is True) and do_ffn
        hT_ready = False
        for L in range(n_layers):
            if do_mix is True or (do_mix is not False and do_mix == L % 2):
                if not hT_ready:
                    rms_to_hT(k, x_src, k.gainT[:, L * KC:(L + 1) * KC])
                    P.barrier()
                hT_ready = False
                if L % 2 == 0:
                    nn = ("hT", k.gainT[:, (4 + L) * KC:(5 + L) * KC]) if fuse else None
                    gdn_layer(k, L // 2, x_src, nn)
                    hT_ready = nn is not None
                else:
                    mla_layer(k, L // 2, x_src)
                P.barrier()
                x_src = k.xres
            if do_ffn:
                if not hT_ready:
                    rms_to_hT(k, x_src, k.gainT[:, (4 + L) * KC:(5 + L) * KC])
                    P.barrier()
                hT_ready = False
                nn = None
                if fuse:
                    nn = ("final",) if L == n_layers - 1 else ("hT", k.gainT[:, (L + 1) * KC:(L + 2) * KC])
                ffn_layer(k, L, x_src, nn)
                hT_ready = nn is not None and nn[0] == "hT"
                P.barrier()
                x_src = k.xres
        if fuse:
            outs = k.final_ops
        else:
            outs = final_norm(k, x_src)
        P.emit(final_wait_ops=outs + k.dbg_ops)
    return nc


def setup_consts(k):
    P, nc = k.P, k.nc
    k.ps = [P.psum(f"ps{i}", [128, 512], F32) for i in range(8)]
    k.rps = [Res() for _ in range(8)]
    k.bank_rr = {}
    k.identf = P.sbuf("identf", [128, 128], F32)
    k.ident = P.sbuf("ident", [128, 128], BF16)
    k.onesf = P.sbuf("onesf", [128, 128], F32)
    k.negonesf = P.sbuf("negonesf", [128, 128], F32)
    k.onesb = P.sbuf("onesb", [128, 128], BF16)
    k.r_const = Res()
    rc = k.r_const
    P.op("pool", lambda e: e.memset(k.identf[:], 0.0), writes=[rc])
    P.op("pool", lambda e: e.affine_select(out=k.identf[:], in_=k.identf[:], pattern=[[1, 128]],
                                           compare_op=ALU.not_equal, fill=1.0, base=0,
                                           channel_multiplier=-1), writes=[rc])
    P.op("dve", lambda e: e.tensor_copy(out=k.ident[:], in_=k.identf[:]), reads=[rc], writes=[rc])
    P.op("pool", lambda e: e.memset(k.onesf[:], 1.0), writes=[rc])
    P.op("pool", lambda e: e.memset(k.negonesf[:], -1.0), writes=[rc])
    P.op("pool", lambda e: e.memset(k.onesb[:], 1.0), writes=[rc])
    k.gainT = P.sbuf("gainT", [128, 74], F32)
    g_raw = P.sbuf("g_raw", [74, 128], F32)
    r_g = Res()
    P.dma(g_raw[0:32, :], k.norm_mix.rearrange("r (c p) -> (r c) p", p=128), writes=[r_g])
    P.dma(g_raw[32:64, :], k.norm_ffn.rearrange("r (c p) -> (r c) p", p=128), writes=[r_g])
    P.dma(g_raw[64:70, :], k.mla_q_norm.rearrange("r (c p) -> (r c) p", p=128), writes=[r_g])
    P.dma(g_raw[70:74, :], k.mla_kv_norm.rearrange("r (c p) -> (r c) p", p=128), writes=[r_g])
    P.op("pe", lambda e: e.transpose(out=k.ps[0][:, 0:74], in_=g_raw[:, :], identity=k.identf[0:74, 0:74]),
         reads=[r_g, rc], writes=[k.rps[0]])
    P.op("dve", lambda e: e.tensor_copy(out=k.gainT[:], in_=k.ps[0][:, 0:74]), writes=[k.rps[0], rc])
    k.bigA = P.sbuf("bigA", [128, KC, S], BF16)
    k.r_big = [Res() for _ in range(S // 512)]
    k.m05 = P.sbuf("m05", [128, 1], F32)
    P.op("pool", lambda e: e.memset(k.m05[:], -0.5), writes=[rc])
    k.epsc = P.sbuf("epsc", [128, 1], F32)
    P.op("pool", lambda e: e.memset(k.epsc[:], EPS), writes=[rc])
    k.fold = P.sbuf("fold", [128, 128], BF16)
    for (a, b) in ((0, 0), (64, 64), (0, 64), (64, 0)):
        P.op("dve", lambda e, a=a, b=b: e.tensor_copy(out=k.fold[a:a + 64, b:b + 64], in_=k.identf[a:a + 64, a:a + 64]),
             reads=[rc], writes=[rc])


def rope_table(k, st_out):
    P = k.P
    rc = k.r_const
    k.cs = P.sbuf("cs", [128, S], F32, st_out)
    with contextlib.ExitStack() as st:
        invf = P.sbuf("invf", [128, 2], F32, st)
        posi = P.sbuf("posi", [128, S], I32, st)
        t = P.sbuf("rt", [128, S], F32, st)
        ti = P.sbuf("rti", [128, S], I32, st)
        m = P.sbuf("rm", [128, S], F32, st)
        tf = m
        r = Res()
        P.dma(invf[:], k.invf, writes=[r])
        P.dma(posi[:], k.pos.partition_broadcast(128), writes=[r])
        P.op("dve", lambda e: e.tensor_copy(out=t[:], in_=posi[:]), writes=[r])
        P.op("dve", lambda e: e.tensor_scalar(out=t[:], in0=t[:], scalar1=invf[:, 0:1], scalar2=invf[:, 1:2],
                                              op0=ALU.mult, op1=ALU.add), writes=[r])
        P.op("dve", lambda e: e.tensor_copy(out=ti[:], in_=t[:]), writes=[r])
        P.op("dve", lambda e: e.tensor_copy(out=tf[:], in_=ti[:]), writes=[r])
        P.op("dve", lambda e: e.tensor_tensor(out=t[:], in0=t[:], in1=tf[:], op=ALU.subtract), writes=[r])
        P.op("dve", lambda e: e.tensor_scalar(out=m[:], in0=t[:], scalar1=0.5, scalar2=None, op0=ALU.is_gt), writes=[r])
        P.op("dve", lambda e: e.tensor_tensor(out=t[:], in0=t[:], in1=m[:], op=ALU.subtract), writes=[r])
        P.op("dve", lambda e: e.tensor_scalar(out=m[:], in0=t[:], scalar1=-0.5, scalar2=None, op0=ALU.is_lt), writes=[r])
        P.op("dve", lambda e: e.tensor_tensor(out=t[:], in0=t[:], in1=m[:], op=ALU.add), writes=[r])
        P.op("act", lambda e: e.activation(out=k.cs[:], in_=t[:], func=AF.Sin, scale=6.283185), reads=[r], writes=[rc])
    P.barrier()


def bank(k, cls, banks):
    i = k.bank_rr.get(cls, 0)
    k.bank_rr[cls] = i + 1
    b = banks[i % len(banks)]
    return k.ps[b], k.rps[b]


def rms_to_hT(k, x_src, gT):
    P = k.P
    rc = k.r_const
    with contextlib.ExitStack() as st:
        xr = Ring(P, "nx", [128, D], F32, 3, st)
        sqr = Ring(P, "nsq", [128, D], F32, 2, st)
        ybr = Ring(P, "nyb", [128, D], BF16, 2, st)
        ssr = Ring(P, "nss", [128, 4], F32, 4, st)
        for i in range(NT):
            xt, rx = xr.next()
            sq, rsq = sqr.next()
            yb, ryb = ybr.next()
            ss, rss = ssr.next()
            P.dma(xt[:], x_src[i * 128:(i + 1) * 128, :], reads=k.xr(x_src, i), writes=[rx])
            P.op("act", lambda e, sq=sq, xt=xt, ss=ss: e.activation(
                out=sq[:], in_=xt[:], func=AF.Square, accum_out=ss[:, 0:1]), reads=[rx], writes=[rsq, rss])
            P.op("dve", lambda e, ss=ss: e.tensor_scalar(out=ss[:, 1:2], in0=ss[:, 0:1], scalar1=1.0 / D,
                                                         scalar2=EPS, op0=ALU.mult, op1=ALU.add),
                 reads=[rss], writes=[rss])
            P.op("pool", lambda e, ss=ss: e.tensor_tensor(out=ss[:, 2:3], in0=ss[:, 1:2], in1=k.m05[:],
                                                          op=ALU.pow), reads=[rss, rc], writes=[rss])
            P.op("act", lambda e, yb=yb, xt=xt, ss=ss: e.activation(out=yb[:], in_=xt[:], func=AF.Copy,
                                                                    scale=ss[:, 2:3]),
                 reads=[rx, rss], writes=[ryb])
            pt, rp = bank(k, "n", [0, 1])
            psb = pt[:].bitcast(BF16)
            for c in range(KC):
                P.op("pe", lambda e, c=c, psb=psb, yb=yb: e.transpose(
                    out=psb[:, c * 128:(c + 1) * 128], in_=yb[:, c * 128:(c + 1) * 128], identity=k.ident[:]),
                    reads=[ryb, rc], writes=[rp])
            P.op("dve", lambda e, i=i, psb=psb: e.tensor_tensor(
                out=k.bigA[:, :, i * 128:(i + 1) * 128],
                in0=psb[:, 0:D].rearrange("p (c n) -> p c n", c=KC),
                in1=gT.unsqueeze(2).to_broadcast([128, KC, 128]), op=ALU.mult),
                reads=[rc], writes=[rp, k.r_big[i // 4]])


def final_norm(k, x_src):
    P = k.P
    rc = k.r_const
    outs = []
    with contextlib.ExitStack() as st:
        k.gfin = P.sbuf("gfin", [128, D], F32, st)
        P.dma(k.gfin[:], k.final_norm.partition_broadcast(128), writes=[rc])
        xr = Ring(P, "fx", [128, D], F32, 3, st)
        sqr = Ring(P, "fsq", [128, D], F32, 2, st)
        yr = Ring(P, "fy", [128, D], F32, 3, st)
        ssr = Ring(P, "fss", [128, 4], F32, 4, st)
        for i in range(NT):
            xt, rx = xr.next()
            sq, rsq = sqr.next()
            yt, ry = yr.next()
            ss, rss = ssr.next()
            P.dma(xt[:], x_src[i * 128:(i + 1) * 128, :], reads=k.xr(x_src, i), writes=[rx])
            P.op("act", lambda e, sq=sq, xt=xt, ss=ss: e.activation(
                out=sq[:], in_=xt[:], func=AF.Square, accum_out=ss[:, 0:1]), reads=[rx], writes=[rsq, rss])
            P.op("dve", lambda e, ss=ss: e.tensor_scalar(out=ss[:, 1:2], in0=ss[:, 0:1], scalar1=1.0 / D,
                                                         scalar2=EPS, op0=ALU.mult, op1=ALU.add),
                 reads=[rss], writes=[rss])
            P.op("pool", lambda e, ss=ss: e.tensor_tensor(out=ss[:, 2:3], in0=ss[:, 1:2], in1=k.m05[:],
                                                          op=ALU.pow), reads=[rss, rc], writes=[rss])
            P.op("dve", lambda e, yt=yt, xt=xt, ss=ss: e.scalar_tensor_tensor(
                out=yt[:], in0=xt[:], scalar=ss[:, 2:3], in1=k.gfin[:], op0=ALU.mult, op1=ALU.mult),
                reads=[rx, rss, rc], writes=[ry])
            outs.append(P.dma(k.out[i * 128:(i + 1) * 128, :], yt[:], reads=[ry]))
    return outs


def norm_rings(k, st):
    P = k.P
    return dict(sq=Ring(P, "nf_sq", [128, D], BF16, 1, st), yb=Ring(P, "nf_yb", [128, D], BF16, 2, st),
                ss=Ring(P, "nf_ss", [128, 4], F32, 4, st))


def norm_tile(k, xt, rx, i, nn, NR, banks):
    P = k.P
    rc = k.r_const
    sq, rsq = NR["sq"].next()
    ss, rss = NR["ss"].next()
    P.op("act", lambda e: e.activation(out=sq[:], in_=xt[:], func=AF.Square, accum_out=ss[:, 0:1]),
         reads=[rx], writes=[rsq, rss])
    P.op("dve", lambda e: e.tensor_scalar(out=ss[:, 1:2], in0=ss[:, 0:1], scalar1=1.0 / D, scalar2=EPS,
                                          op0=ALU.mult, op1=ALU.add), writes=[rss])
    P.op("pool", lambda e: e.tensor_tensor(out=ss[:, 2:3], in0=ss[:, 1:2], in1=k.m05[:], op=ALU.pow),
         reads=[rc], writes=[rss])
    if nn[0] == "final":
        yt, ry = NR["y"].next()
        P.op("dve", lambda e: e.scalar_tensor_tensor(out=yt[:], in0=xt[:], scalar=ss[:, 2:3], in1=NR["gfin"][:],
                                                     op0=ALU.mult, op1=ALU.mult), reads=[rx, rss, rc], writes=[ry])
        k.final_ops.append(P.dma(k.out[i * 128:(i + 1) * 128, :], yt[:], reads=[ry]))
        return
    gT = nn[1]
    yb, ryb = NR["yb"].next()
    P.op("act", lambda e: e.activation(out=yb[:], in_=xt[:], func=AF.Copy, scale=ss[:, 2:3]),
         reads=[rx, rss], writes=[ryb])
    pt, rp = bank(k, "nf", banks)
    psb = pt[:].bitcast(BF16)
    for c in range(KC):
        P.op("pe", lambda e, c=c: e.transpose(out=psb[:, c * 128:(c + 1) * 128], in_=yb[:, c * 128:(c + 1) * 128],
                                              identity=k.ident[:]), reads=[ryb, rc], writes=[rp])
    P.op("dve", lambda e: e.tensor_tensor(
        out=k.bigA[:, :, i * 128:(i + 1) * 128], in0=psb[:, 0:D].rearrange("p (c n) -> p c n", c=KC),
        in1=gT.unsqueeze(2).to_broadcast([128, KC, 128]), op=ALU.mult),
        reads=[rc], writes=[rp, k.r_big[i // 4]])


def setup_next_norm(k, nn, st):
    if nn is None:
        return None
    NR = norm_rings(k, st)
    if nn[0] == "final":
        P = k.P
        NR["gfin"] = P.sbuf("nf_gfin", [128, D], F32, st)
        P.dma(NR["gfin"][:], k.final_norm.partition_broadcast(128), writes=[k.r_const])
        NR["y"] = Ring(P, "nf_y", [128, D], F32, 2, st)
    return NR


FFN_TB = 1024
GDN_LANES = 5
GDN_STAGGER = 9


def ffn_layer(k, L, x_src, nn=None):
    P = k.P
    wgu = k.ffn_w_gate_up[L].rearrange("(c p) n -> p c n", p=128)
    wdn = k.ffn_w_down[L].rearrange("(c p) n -> p c n", p=128)
    with contextlib.ExitStack() as st:
        wd = P.sbuf("wd", [128, FC, D], BF16, st)
        r_wd = Res()
        actT = P.sbuf("actT", [128, FC, FFN_TB], BF16, st)
        r_act = [Res() for _ in range(FFN_TB // 512)]
        wr = Ring(P, "wgu", [128, KC, 256], BF16, 4, st)
        pre_w = []
        for j in range(3):
            wt, rw = wr.next()
            P.dma(wt[:, :, 0:128], wgu[:, :, j * 128:(j + 1) * 128], writes=[rw], eng="pool")
            P.dma(wt[:, :, 128:256], wgu[:, :, D_FF + j * 128:D_FF + (j + 1) * 128], writes=[rw], eng="pool")
            pre_w.append((wt, rw))
        sgr = Ring(P, "sg", [128, 512], F32, 2, st)
        xr = Ring(P, "fx", [128, D], F32, 3, st)
        NR = setup_next_norm(k, nn, st)
        NSB = S // FFN_TB
        seq = [(sb, j) for sb in range(NSB) for j in range(FC)]
        loaded = {}
        for n_, (sb_, j_) in enumerate(seq[:3]):
            loaded[(sb_, j_)] = pre_w[n_]

        def prefetch(idx):
            if idx < len(seq):
                sb_, j_ = seq[idx]
                wt_, rw_ = wr.next()
                P.dma(wt_[:, :, 0:128], wgu[:, :, j_ * 128:(j_ + 1) * 128], writes=[rw_], eng="pool")
                P.dma(wt_[:, :, 128:256], wgu[:, :, D_FF + j_ * 128:D_FF + (j_ + 1) * 128], writes=[rw_], eng="pool")
                loaded[(sb_, j_)] = (wt_, rw_)

        for sb in range(NSB):
            for j in range(FC):
                prefetch(sb * FC + j + 3)
                if sb == 0 and j < FC // 2:
                    P.dma(wd[:, 2 * j:2 * j + 2, :], wdn[:, 2 * j:2 * j + 2, :], writes=[r_wd], eng="pool")
                wt, rw = loaded.pop((sb, j))
                for tb in range(FFN_TB // 512):
                    t0 = sb * FFN_TB + tb * 512
                    gb = (sb * FFN_TB) // 512 + tb
                    pg, rpg = bank(k, "fg", [0, 1, 2, 3])
                    pu, rpu = bank(k, "fg", [0, 1, 2, 3])
                    for c in range(KC):
                        P.op("pe", lambda e, c=c, pg=pg, wt=wt, t0=t0: e.matmul(
                            pg[:, :], lhsT=wt[:, c, 0:128], rhs=k.bigA[:, c, t0:t0 + 512],
                            start=(c == 0), stop=(c == KC - 1)), reads=[rw, k.r_big[gb]], writes=[rpg])
                    for c in range(KC):
                        P.op("pe", lambda e, c=c, pu=pu, wt=wt, t0=t0: e.matmul(
                            pu[:, :], lhsT=wt[:, c, 128:256], rhs=k.bigA[:, c, t0:t0 + 512],
                            start=(c == 0), stop=(c == KC - 1)), reads=[rw, k.r_big[gb]], writes=[rpu])
                    sg, rsg = sgr.next()
                    P.op("act", lambda e, sg=sg, pg=pg: e.activation(out=sg[:], in_=pg[:, :], func=AF.Silu),
                         writes=[rpg, rsg])
                    P.op("dve", lambda e, sg=sg, pu=pu, j=j, tb=tb: e.tensor_tensor(
                        out=actT[:, j, tb * 512:(tb + 1) * 512], in0=pu[:, :], in1=sg[:], op=ALU.mult),
                        reads=[rsg], writes=[rpu, r_act[tb]])
            for tt in range(FFN_TB // 128):
                tok0 = sb * FFN_TB + tt * 128
                xt, rx = xr.next()
                P.dma(xt[:], x_src[tok0:tok0 + 128, :], reads=k.xr(x_src, tok0 // 128), writes=[rx])
                for dh in range(2):
                    po, rpo = bank(k, "fd", [4, 5, 6, 7])
                    for j in range(FC):
                        P.op("pe", lambda e, j=j, po=po, tt=tt, dh=dh: e.matmul(
                            po[:, :], lhsT=actT[:, j, tt * 128:(tt + 1) * 128], rhs=wd[:, j, dh * 512:(dh + 1) * 512],
                            start=(j == 0), stop=(j == FC - 1)), reads=[r_act[tt // 4], r_wd], writes=[rpo])
                    P.op("dve", lambda e, po=po, xt=xt, dh=dh: e.tensor_tensor(
                        out=xt[:, dh * 512:(dh + 1) * 512], in0=po[:, :], in1=xt[:, dh * 512:(dh + 1) * 512],
                        op=ALU.add), reads=[], writes=[rpo, rx])
                P.dma(k.xres[tok0:tok0 + 128, :], xt[:], reads=[rx], writes=[k.r_xres[tok0 // 128]], eng="act")
                if nn is not None:
                    norm_tile(k, xt, rx, tok0 // 128, nn, NR, [0, 1])


def gdn_layer(k, j, x_src, nn=None):
    P = k.P
    rc = k.r_const
    NB = S // 512
    w_in = k.gdn_w_in[j].rearrange("(c p) n -> p c n", p=128)
    oTd = k.oT_d.rearrange("(h p) t -> p h t", p=128)
    r_oTd = [Res() for _ in range(NB)]
    with contextlib.ExitStack() as st:
        cw_raw = P.sbuf("g_cwraw", [128, 128], F32, st)
        cwT = P.sbuf("g_cwT", [128, 4, 32], F32, st)
        dtb = P.sbuf("g_dtb", [128, 16], F32, st)
        nA = P.sbuf("g_nA", [128, 16], F32, st)
        gon = P.sbuf("g_gon", [128, 128], F32, st)
        triu = P.sbuf("g_triu", [128, 128], F32, st)
        neglt = P.sbuf("g_neglt", [128, 128], F32, st)
        posue = P.sbuf("g_posue", [128, 128], F32, st)
        r_lc = Res()
        gm = P.sbuf("g_gm", [128, 8, 128], BF16, st)
        P.dma(gm[:], k.gmask, writes=[r_lc], eng="pool")
        P.dma(cw_raw[:], k.gdn_conv_w[j].rearrange("t (c p) -> (t c) p", p=128), writes=[r_lc])
        P.dma(dtb[:], k.gdn_dt_bias[j:j + 1, :].partition_broadcast(128), writes=[r_lc])
        P.dma(nA[:], k.gdn_a_log[j:j + 1, :].partition_broadcast(128), writes=[r_lc])
        P.dma(gon[:], k.gdn_out_norm[j:j + 1, :].partition_broadcast(128), writes=[r_lc])
        P.op("act", lambda e: e.activation(out=nA[:], in_=nA[:], func=AF.Exp), writes=[r_lc])
        P.op("act", lambda e: e.mul(out=nA[:], in_=nA[:], mul=-1.0), writes=[r_lc])
        pc, rpc = bank(k, "gt", [2, 3, 4])
        P.op("pe", lambda e: e.transpose(out=pc[:, 0:128], in_=cw_raw[:, :], identity=k.identf[:]),
             reads=[r_lc, rc], writes=[rpc])
        P.op("dve", lambda e: e.tensor_copy(out=cwT[:].rearrange("p a b -> p (a b)"), in_=pc[:, 0:128]),
             writes=[rpc, r_lc])
        P.op("pool", lambda e: e.memset(triu[:], 1.0), writes=[r_lc])
        P.op("pool", lambda e: e.affine_select(out=triu[:], in_=triu[:], pattern=[[1, 128]], compare_op=ALU.is_ge,
                                               fill=0.0, base=0, channel_multiplier=-1), writes=[r_lc])
        P.op("pool", lambda e: e.memset(neglt[:], 0.0), writes=[r_lc])
        P.op("pool", lambda e: e.affine_select(out=neglt[:], in_=neglt[:], pattern=[[-1, 128]], compare_op=ALU.is_ge,
                                               fill=-30000.0, base=-1, channel_multiplier=1), writes=[r_lc])
        P.op("pool", lambda e: e.memset(posue[:], 0.0), writes=[r_lc])
        P.op("pool", lambda e: e.affine_select(out=posue[:], in_=posue[:], pattern=[[1, 128]], compare_op=ALU.is_ge,
                                               fill=30000.0, base=0, channel_multiplier=-1), writes=[r_lc])
        NH = 16
        gs = {}
        for nm in ("beta", "gc", "eg", "ekd", "egl", "kbs"):
            gs[nm] = P.sbuf("g_" + nm, [128, NT, NH], F32, st)
        r_gs = Res()
        f2 = lambda t: t[:].rearrange("p a b -> p (a b)")
        with contextlib.ExitStack() as stg:
            for nm in ("g", "glb"):
                gs[nm] = P.sbuf("g_" + nm, [128, NT, NH], F32, stg)
            wba = P.sbuf("g_wba", [128, KC, 32], BF16, stg)
            ba = P.sbuf("g_ba", [128, NT, 32], F32, stg)
            tmp = P.sbuf("g_tmp", [128, NT, NH], F32, stg)
            r_wba = Res()
            P.dma(wba[:], w_in[:, :, 6144:6176], writes=[r_wba], eng="pool")
            for half in range(2):
                pb_, rpb_ = bank(k, "gt", [2, 3, 4])
                for tl in range(16):
                    i = half * 16 + tl
                    for c in range(KC):
                        P.op("pe", lambda e, c=c, i=i, tl=tl, pb_=pb_: e.matmul(
                            pb_[:, tl * 32:(tl + 1) * 32], lhsT=k.bigA[:, c, i * 128:(i + 1) * 128], rhs=wba[:, c, :],
                            start=(c == 0), stop=(c == KC - 1)), reads=[r_wba, k.r_big[i // 4]], writes=[rpb_])
                P.op("act", lambda e, half=half, pb_=pb_: e.copy(
                    out=ba[:, half * 16:(half + 1) * 16, :].rearrange("p a b -> p (a b)"), in_=pb_[:, :]),
                    writes=[rpb_, r_gs])
            P.op("act", lambda e: e.activation(out=gs["beta"][:], in_=ba[:, :, 0:16], func=AF.Sigmoid), writes=[r_gs])
            P.op("dve", lambda e: e.tensor_tensor(out=tmp[:], in0=ba[:, :, 16:32],
                                                  in1=dtb[:, 0:16].unsqueeze(1).to_broadcast([128, NT, NH]), op=ALU.add),
                 reads=[r_lc], writes=[r_gs])
            P.op("act", lambda e: e.activation(out=tmp[:], in_=tmp[:], func=AF.Exp), writes=[r_gs])
            P.op("act", lambda e: e.activation(out=tmp[:], in_=tmp[:], func=AF.Ln, bias=1.0), writes=[r_gs])
            P.op("dve", lambda e: e.tensor_tensor(out=gs["g"][:], in0=tmp[:],
                                                  in1=nA[:, 0:16].unsqueeze(1).to_broadcast([128, NT, NH]), op=ALU.mult),
                 reads=[r_lc], writes=[r_gs])
            p1, rp1 = bank(k, "gt", [2, 3, 4])
            P.op("pe", lambda e: e.matmul(p1[:, :], lhsT=triu[:], rhs=f2(gs["g"]), start=True, stop=True),
                 reads=[r_gs, r_lc], writes=[rp1])
            P.op("dve", lambda e: e.tensor_copy(out=f2(gs["gc"]), in_=p1[:, :]), writes=[rp1, r_gs])
            p2, rp2 = bank(k, "gt", [2, 3, 4])
            P.op("pe", lambda e: e.matmul(p2[:, :], lhsT=k.onesf[:], rhs=f2(gs["g"]), start=True, stop=True),
                 reads=[r_gs, rc], writes=[rp2])
            P.op("dve", lambda e: e.tensor_copy(out=f2(gs["glb"]), in_=p2[:, :]), writes=[rp2, r_gs])
            P.op("act", lambda e: e.activation(out=gs["eg"][:], in_=gs["gc"][:], func=AF.Exp), writes=[r_gs])
            P.op("act", lambda e: e.activation(out=gs["egl"][:], in_=gs["glb"][:], func=AF.Exp), writes=[r_gs])
            P.op("dve", lambda e: e.tensor_tensor(out=tmp[:], in0=gs["glb"][:], in1=gs["gc"][:], op=ALU.subtract),
                 writes=[r_gs])
            P.op("act", lambda e: e.activation(out=gs["ekd"][:], in_=tmp[:], func=AF.Exp), writes=[r_gs])
            P.op("dve", lambda e: e.tensor_tensor(out=gs["kbs"][:], in0=gs["beta"][:], in1=gs["eg"][:], op=ALU.mult),
                 writes=[r_gs])
        for nm in ("beta", "gc", "eg", "ekd", "egl", "kbs"):
            dump(k, "gs_" + nm, gs[nm][:], [r_gs])
        P.barrier()
        wfr = Ring(P, "g_wf", [128, KC, 512], BF16, 2, st)
        wzr = Ring(P, "g_wz", [128, KC, 256], BF16, 1, st)
        qT = P.sbuf("g_qT", [128, S], BF16, st)
        kT = P.sbuf("g_kT", [128, S], BF16, st)
        vT = P.sbuf("g_vT", [128, 2, S], BF16, st)
        r_q = [Res() for _ in range(NB)]
        r_k = [Res() for _ in range(NB)]
        r_v = [Res() for _ in range(NB)]
        S32 = P.sbuf("g_S32", [128, 2, 128], F32, st)
        Sb = P.sbuf("g_Sb", [128, 2, 128], BF16, st)
        r_S32, r_Sb = Res(), Res()
        GT = [2, 3, 4]
        GR = [5, 6]
        GO = [7]
        GB = [0, 1]
        NL = GDN_LANES
        bc3 = lambda ap2: ap2.unsqueeze(1).to_broadcast([128, 2, 128])
        bcl = lambda ap2: ap2.unsqueeze(2).to_broadcast([128, 2, 128])
        v3 = lambda ap: ap.rearrange("p (a b) -> p a b", a=2)

        def chunk_gen(kh, tb, ch, wf, rwf, B):
            t0 = tb * 512
            rb = k.r_big[tb]
            chunk = (kh, 8 + kh, 16 + 2 * kh, 17 + 2 * kh)[ch]
            pp, rpp = k.ps[ch], k.rps[ch]
            for c in range(KC):
                P.op("pe", lambda e, c=c: e.matmul(
                    pp[:, :], lhsT=wf[:, c, ch * 128:(ch + 1) * 128], rhs=k.bigA[:, c, t0:t0 + 512],
                    start=(c == 0), stop=(c == KC - 1)), reads=[rwf, rb], writes=[rpp])
            yield
            pre, rpre = B["pre"][ch].next()
            halo, r_halo = B["halo"], B["r_halo"]
            P.op("act", lambda e: e.copy(out=pre[:, 3:515], in_=pp[:, :]), writes=[rpp, rpre])
            if tb == 0:
                P.op("pool", lambda e: e.memset(pre[:, 0:3], 0.0), writes=[rpre])
            else:
                P.op("pool", lambda e: e.tensor_copy(out=pre[:, 0:3], in_=halo[ch][:, 0:3]),
                     reads=[r_halo[ch]], writes=[rpre])
            yield
            P.op("pool", lambda e: e.tensor_copy(out=halo[ch][:, 0:3], in_=pre[:, 512:515]),
                 reads=[rpre], writes=[r_halo[ch]])
            cv, rcv = B["cv"][ch].next()
            P.op("dve", lambda e: e.tensor_scalar(
                out=cv[:], in0=pre[:, 3:515], scalar1=cwT[:, 3, chunk:chunk + 1], scalar2=None, op0=ALU.mult),
                reads=[rpre, r_lc], writes=[rcv])
            for tap in (2, 1, 0):
                P.op("dve", lambda e, tap=tap: e.scalar_tensor_tensor(
                    out=cv[:], in0=pre[:, tap:tap + 512], scalar=cwT[:, tap, chunk:chunk + 1], in1=cv[:],
                    op0=ALU.mult, op1=ALU.add), reads=[rpre, r_lc], writes=[rcv])
            yield
            if ch >= 2:
                P.op("act", lambda e: e.activation(out=vT[:, ch - 2, t0:t0 + 512], in_=cv[:], func=AF.Silu),
                     reads=[rcv], writes=[r_v[tb]])
                yield
                return
            P.op("act", lambda e: e.activation(out=cv[:], in_=cv[:], func=AF.Silu), writes=[rcv])
            yield
            sq, rsq = B["sq"][ch].next()
            P.op("pool", lambda e: e.tensor_tensor(out=sq[:], in0=cv[:], in1=cv[:], op=ALU.mult),
                 reads=[rcv], writes=[rsq])
            yield
            pss, rpss = k.ps[4 + ch], k.rps[4 + ch]
            P.op("pe", lambda e: e.matmul(pss[:, :], lhsT=k.onesb[:], rhs=sq[:], start=True, stop=True),
                 reads=[rsq, rc], writes=[rpss])
            yield
            ln, rln = B["ln"][ch].next()
            P.op("act", lambda e: e.activation(out=ln[:], in_=pss[:, :], func=AF.Ln, bias=k.epsc[:, 0:1]),
                 reads=[rc], writes=[rpss, rln])
            P.op("act", lambda e: e.activation(out=ln[:], in_=ln[:], func=AF.Exp, scale=-0.5), writes=[rln])
            yield
            dst, rdst, mul = ((qT, r_q, 128.0 ** -0.5), (kT, r_k, 1.0))[ch]
            P.op("dve", lambda e: e.scalar_tensor_tensor(
                out=dst[:, t0:t0 + 512], in0=cv[:], scalar=mul, in1=ln[:], op0=ALU.mult, op1=ALU.mult),
                reads=[rcv, rln], writes=[rdst[tb]])
            yield

        def pt_stage(kh, i, L, O):
            h0 = 2 * kh
            tb = i // 4
            tok = slice(i * 128, (i + 1) * 128)
            pair = lambda nm: gs[nm][:, i, h0:h0 + 2]
            col = lambda nm, hh: gs[nm][:, i, h0 + hh:h0 + hh + 1]
            kbg, rkbg = O["kbg"]
            kdec, rkdec = O["kdec"]
            bv, rbv = O["bv"]
            qkm, rqkm = O["qkm"]
            TT, rTT = O["TT"]
            nwT, rnwT = O["nwT"]
            hb = [0]

            def lbank():
                h = hb[0]
                hb[0] ^= 1
                return k.ps[L["bank"]][:, h * 256:(h + 1) * 256], k.rps[L["bank"]]
            pt, rpt = lbank()
            ptb = pt.bitcast(BF16)
            P.op("pe", lambda e: e.transpose(out=ptb[:, 0:128], in_=kT[:, tok], identity=k.ident[:]),
                 reads=[r_k[tb], rc], writes=[rpt])
            for hh in range(2):
                P.op("pe", lambda e, hh=hh: e.transpose(out=ptb[:, 128 + hh * 128:256 + hh * 128], in_=vT[:, hh, tok],
                                                        identity=k.ident[:]), reads=[r_v[tb], rc], writes=[rpt])
            dg, rdg = L["dg"]
            P.op("pool", lambda e: e.tensor_tensor(out=dg[:], in0=bc3(k.identf[:]), in1=bcl(pair("gc")), op=ALU.mult),
                 reads=[rc, r_gs], writes=[rdg])
            yield
            P.op("dve", lambda e: e.tensor_tensor(out=kbg[:], in0=bc3(ptb[:, 0:128]), in1=bcl(pair("kbs")), op=ALU.mult),
                 reads=[r_gs], writes=[rpt, rkbg])
            P.op("dve", lambda e: e.tensor_tensor(out=kdec[:], in0=bc3(ptb[:, 0:128]), in1=bcl(pair("ekd")), op=ALU.mult),
                 reads=[r_gs], writes=[rpt, rkdec])
            P.op("dve", lambda e: e.tensor_tensor(out=bv[:], in0=v3(ptb[:, 128:384]), in1=bcl(pair("beta")), op=ALU.mult),
                 reads=[r_gs], writes=[rpt, rbv])
            pa, rpa = lbank()
            for hh in range(2):
                P.op("pe", lambda e, hh=hh: e.matmul(pa[:, hh * 128:(hh + 1) * 128], lhsT=dg[:, hh, :], rhs=k.onesf[:],
                                                     start=True, stop=False), reads=[rdg, rc], writes=[rpa])
                P.op("pe", lambda e, hh=hh: e.matmul(pa[:, hh * 128:(hh + 1) * 128], lhsT=k.negonesf[:], rhs=dg[:, hh, :],
                                                     start=False, stop=True), reads=[rdg, rc], writes=[rpa])
            pk, rpk = lbank()
            P.op("pe", lambda e: e.matmul(pk[:, 0:128], lhsT=kT[:, tok], rhs=kT[:, tok], start=True, stop=True),
                 reads=[r_k[tb]], writes=[rpk])
            P.op("pe", lambda e: e.matmul(pk[:, 128:256], lhsT=kT[:, tok], rhs=qT[:, tok], start=True, stop=True),
                 reads=[r_k[tb], r_q[tb]], writes=[rpk])
            yield
            dm, rdm = L["dm"]
            dmt, rdmt = L["dmt"]
            P.op("dve", lambda e: e.scalar_tensor_tensor(out=dm[:], in0=v3(pa[:, 0:256]), scalar=0.0, in1=bc3(neglt[:]),
                                                         op0=ALU.min, op1=ALU.add), reads=[r_lc], writes=[rpa, rdm])
            P.op("dve", lambda e: e.scalar_tensor_tensor(out=dmt[:], in0=v3(pa[:, 0:256]), scalar=0.0, in1=bc3(posue[:]),
                                                         op0=ALU.max, op1=ALU.add), reads=[r_lc], writes=[rpa, rdmt])
            yield
            P.op("act", lambda e: e.activation(out=dm[:], in_=dm[:], func=AF.Exp), writes=[rdm])
            P.op("act", lambda e: e.activation(out=dmt[:], in_=dmt[:], func=AF.Exp, scale=-1.0), writes=[rdmt])
            yield
            A, rA = L["A"]
            for hh in range(2):
                P.op("dve", lambda e, hh=hh: e.scalar_tensor_tensor(
                    out=A[:, hh, :], in0=pk[:, 0:128], scalar=col("beta", hh), in1=dm[:, hh, :],
                    op0=ALU.mult, op1=ALU.mult), reads=[rdm, r_gs], writes=[rpk, rA])
            P.op("dve", lambda e: e.tensor_tensor(out=qkm[:], in0=bc3(pk[:, 128:256]), in1=dmt[:], op=ALU.mult),
                 reads=[rdmt], writes=[rpk, rqkm])
            yield
            pm, rpm = lbank()
            pmb = pm.bitcast(BF16)
            for hh in range(2):
                P.op("pe", lambda e, hh=hh: e.transpose(out=pmb[:, hh * 128:(hh + 1) * 128], in_=A[:, hh, :],
                                                        identity=k.ident[:]), reads=[rA, rc], writes=[rpm])
            Am, rAm = L["Mo"][0]
            P.op("pool", lambda e: e.tensor_tensor(out=Am[:], in0=A[:], in1=bc3(gm[:, 0, :]), op=ALU.mult),
                 reads=[rA, r_lc], writes=[rAm])
            yield
            M, rM = L["M"]
            P.op("act", lambda e: e.copy(out=M[:], in_=v3(pmb[:, 0:256])), writes=[rpm, rM])
            UV, rUV = L["UV"][0]
            P.op("dve", lambda e, UV=UV: e.tensor_tensor(out=UV[:, 1], in0=bc3(k.ident[:]), in1=Am[:], op=ALU.subtract),
                 reads=[rAm, rc], writes=[rUV])
            yield
            Mm, rMm = L["Mo"][1]
            P.op("pool", lambda e: e.tensor_tensor(out=Mm[:], in0=M[:], in1=bc3(gm[:, 1, :]), op=ALU.mult),
                 reads=[rM, r_lc], writes=[rMm])
            yield
            P.op("dve", lambda e, UV=UV: e.tensor_tensor(out=UV[:, 0], in0=bc3(k.ident[:]), in1=Mm[:], op=ALU.subtract),
                 reads=[rMm, rc], writes=[rUV])
            Mo, rMo = L["Mo"][0]
            P.op("pool", lambda e, Mo=Mo: e.tensor_tensor(out=Mo[:], in0=M[:], in1=bc3(gm[:, 2, :]), op=ALU.mult),
                 reads=[rM, r_lc], writes=[rMo])
            yield
            pbank, rpbank = k.ps[L["bank"]], k.rps[L["bank"]]
            for lv in range(6):
                pY, rpY = lbank()
                for hh in range(2):
                    P.op("pe", lambda e, hh=hh, pY=pY, Mo=Mo, UV=UV: e.matmul(
                        pY[:, hh * 128:(hh + 1) * 128], lhsT=Mo[:, hh, :], rhs=UV[:, 1, hh, :], start=True, stop=True),
                        reads=[rMo, rUV], writes=[rpY])
                yield
                Y, rY = L["Y"]
                P.op("act", lambda e, Y=Y, pY=pY: e.mul(out=Y[:], in_=v3(pY[:, 0:256]), mul=-1.0), writes=[rpY, rY])
                if lv < 5:
                    Mo2, rMo2 = L["Mo"][(lv + 1) % 2]
                    P.op("pool", lambda e, Mo2=Mo2, lv=lv: e.tensor_tensor(out=Mo2[:], in0=M[:], in1=bc3(gm[:, 3 + lv, :]),
                                                                           op=ALU.mult), reads=[rM, r_lc], writes=[rMo2])
                yield
                for hh in range(2):
                    P.op("pe", lambda e, hh=hh, UV=UV: e.matmul(
                        pbank[:, hh * 128:(hh + 1) * 128], lhsT=k.ident[:], rhs=UV[:, 0, hh, :], start=True, stop=False),
                        reads=[rUV, rc], writes=[rpbank])
                    P.op("pe", lambda e, hh=hh, UV=UV, Y=Y: e.matmul(
                        pbank[:, hh * 128:(hh + 1) * 128], lhsT=Y[:, hh, :], rhs=UV[:, 0, hh, :], start=False, stop=True),
                        reads=[rUV, rY], writes=[rpbank])
                if lv < 5:
                    for hh in range(2):
                        P.op("pe", lambda e, hh=hh, UV=UV: e.matmul(
                            pbank[:, 256 + hh * 128:384 + hh * 128], lhsT=k.ident[:], rhs=UV[:, 1, hh, :], start=True, stop=False),
                            reads=[rUV, rc], writes=[rpbank])
                        P.op("pe", lambda e, hh=hh, UV=UV, Y=Y: e.matmul(
                            pbank[:, 256 + hh * 128:384 + hh * 128], lhsT=UV[:, 0, hh, :], rhs=Y[:, hh, :], start=False, stop=True),
                            reads=[rUV, rY], writes=[rpbank])
                yield
                if lv < 5:
                    UVn, rUVn = L["UV"][(lv + 1) % 2]
                    if lv % 2:
                        P.op("act", lambda e, UVn=UVn: e.copy(out=UVn[:].rearrange("p a b c -> p (a b c)"), in_=pbank[:, :]),
                             writes=[rpbank, rUVn])
                    else:
                        P.op("dve", lambda e, UVn=UVn: e.tensor_copy(out=UVn[:].rearrange("p a b c -> p (a b c)"), in_=pbank[:, :]),
                             writes=[rpbank, rUVn])
                    UV, rUV = UVn, rUVn
                    Mo, rMo = Mo2, rMo2
                else:
                    P.op("dve", lambda e: e.tensor_copy(out=TT[:], in_=v3(pbank[:, 0:256])), writes=[rpbank, rTT])
                hb[0] = 0
                yield
            pw, rpw = lbank()
            for hh in range(2):
                P.op("pe", lambda e, hh=hh: e.matmul(pw[:, hh * 128:(hh + 1) * 128], lhsT=kbg[:, hh, :], rhs=TT[:, hh, :],
                                                     start=True, stop=True), reads=[rkbg, rTT], writes=[rpw])
            yield
            P.op("act", lambda e: e.mul(out=nwT[:], in_=v3(pw[:, 0:256]), mul=-1.0), writes=[rpw, rnwT])
            yield

        def r_stage(kh, i, O, RB, done):
            h0 = 2 * kh
            tb = i // 4
            tok = slice(i * 128, (i + 1) * 128)
            pair = lambda nm: gs[nm][:, i, h0:h0 + 2]
            col = lambda nm, hh: gs[nm][:, i, h0 + hh:h0 + hh + 1]
            TT, rTT = O["TT"]
            nwT, rnwT = O["nwT"]
            bv, rbv = O["bv"]
            kdec, rkdec = O["kdec"]
            qkm, rqkm = O["qkm"]
            pv, rpv = k.ps[4][:, 0:256], k.rps[4]
            for hh in range(2):
                P.op("pe", lambda e, hh=hh: e.matmul(pv[:, hh * 128:(hh + 1) * 128], lhsT=TT[:, hh, :], rhs=bv[:, hh, :],
                                                     start=True, stop=False), reads=[rTT, rbv], writes=[rpv])
                P.op("pe", lambda e, hh=hh: e.matmul(pv[:, hh * 128:(hh + 1) * 128], lhsT=nwT[:, hh, :], rhs=Sb[:, hh, :],
                                                     start=False, stop=True), reads=[rnwT, r_Sb], writes=[rpv])
            pz, rpz = k.ps[5], k.rps[5]
            for hh in range(2):
                P.op("pe", lambda e, hh=hh: e.matmul(pz[:, hh * 128:(hh + 1) * 128], lhsT=qT[:, tok], rhs=Sb[:, hh, :],
                                                     start=True, stop=True), reads=[r_q[tb], r_Sb], writes=[rpz])
            yield
            vn, rvn = RB["vn"].next()
            P.op("act", lambda e: e.copy(out=vn[:], in_=v3(pv[:, 0:256])), writes=[rpv, rvn])
            yield
            pd, rpd = k.ps[4][:, 256:512], k.rps[4]
            for hh in range(2):
                P.op("pe", lambda e, hh=hh: e.matmul(pd[:, hh * 128:(hh + 1) * 128], lhsT=kdec[:, hh, :], rhs=vn[:, hh, :],
                                                     start=True, stop=True), reads=[rkdec, rvn], writes=[rpd])
            for hh in range(2):
                P.op("pe", lambda e, hh=hh: e.matmul(pz[:, 256 + hh * 128:384 + hh * 128], lhsT=qkm[:, hh, :], rhs=vn[:, hh, :],
                                                     start=True, stop=True), reads=[rqkm, rvn], writes=[rpz])
            yield
            for hh in range(2):
                P.op("dve", lambda e, hh=hh: e.scalar_tensor_tensor(
                    out=S32[:, hh, :], in0=S32[:, hh, :], scalar=col("egl", hh), in1=pd[:, hh * 128:(hh + 1) * 128],
                    op0=ALU.mult, op1=ALU.add), reads=[r_gs], writes=[rpd, r_S32])
            zs, rzs = RB["zs"].next()
            P.op("dve", lambda e: e.tensor_tensor(out=zs[:], in0=v3(pz[:, 0:256]), in1=bcl(pair("eg")), op=ALU.mult),
                 reads=[r_gs], writes=[rpz, rzs])
            yield
            P.op("pool", lambda e: e.tensor_copy(out=Sb[:], in_=S32[:]), reads=[r_S32], writes=[r_Sb])
            o32, ro32 = RB["o32"].next()
            P.op("dve", lambda e: e.tensor_tensor(out=o32[:], in0=v3(pz[:, 256:512]), in1=zs[:], op=ALU.add),
                 reads=[rzs], writes=[rpz, ro32])
            done[i] = (o32, ro32)
            yield

        def o_stage(kh, i, o32, ro32, wz, rwz, RB, oT4, roT4):
            h0 = 2 * kh
            tb = i // 4
            tok = slice(i * 128, (i + 1) * 128)
            pzz, rpzz = k.ps[6][:, 0:256], k.rps[6]
            for c in range(KC):
                P.op("pe", lambda e, c=c: e.matmul(pzz[:, 0:256], lhsT=k.bigA[:, c, tok], rhs=wz[:, c, :],
                                                   start=(c == 0), stop=(c == KC - 1)), reads=[rwz, k.r_big[tb]], writes=[rpzz])
            junk, rjunk = RB["junk"].next()
            ssq, rssq = RB["ssq"].next()
            for hh in range(2):
                P.op("act", lambda e, hh=hh: e.activation(out=junk[:, hh, :], in_=o32[:, hh, :], func=AF.Square,
                                                          accum_out=ssq[:, hh:hh + 1]), reads=[ro32], writes=[rjunk, rssq])
            yield
            zz, rzz = RB["zz"].next()
            P.op("act", lambda e: e.activation(out=zz[:], in_=pzz[:, 0:256], func=AF.Silu), writes=[rpzz, rzz])
            P.op("dve", lambda e: e.tensor_scalar(out=ssq[:, 2:4], in0=ssq[:, 0:2], scalar1=1.0 / 128, scalar2=EPS,
                                                  op0=ALU.mult, op1=ALU.add), writes=[rssq])
            yield
            P.op("pool", lambda e: e.tensor_tensor(out=ssq[:, 4:6], in0=ssq[:, 2:4], in1=k.m05[:, 0:1].to_broadcast([128, 2]),
                                                   op=ALU.pow), reads=[rc], writes=[rssq])
            yield
            t1, rt1 = RB["t1"].next()
            for hh in range(2):
                P.op("dve", lambda e, hh=hh: e.scalar_tensor_tensor(
                    out=t1[:, hh, :], in0=o32[:, hh, :], scalar=ssq[:, 4 + hh:5 + hh], in1=gon[:],
                    op0=ALU.mult, op1=ALU.mult), reads=[ro32, rssq, r_lc], writes=[rt1])
            yield
            ob, rob = RB["ob"].next()
            P.op("pool", lambda e: e.tensor_tensor(out=ob[:], in0=t1[:], in1=v3(zz[:]), op=ALU.mult),
                 reads=[rt1, rzz], writes=[rob])
            yield
            po, rpo = k.ps[6][:, 256:512], k.rps[6]
            pob = po.bitcast(BF16)
            for hh in range(2):
                P.op("pe", lambda e, hh=hh: e.transpose(out=pob[:, hh * 128:(hh + 1) * 128], in_=ob[:, hh, :],
                                                        identity=k.ident[:]), reads=[rob, rc], writes=[rpo])
            yield
            q4 = i % 4
            P.op("act", lambda e: e.copy(out=oT4[:, :, q4 * 128:(q4 + 1) * 128], in_=v3(pob[:, 0:256])),
                 writes=[rpo, roT4])
            if q4 == 3:
                P.dma(oTd[:, h0:h0 + 2, tb * 512:(tb + 1) * 512], oT4[:], reads=[roT4], writes=[r_oTd[tb]])
            yield

        def delayed(g, d):
            for _ in range(d):
                yield
            yield from g

        def run_lanes(gens):
            active = list(gens)
            while active:
                for g in list(active):
                    try:
                        next(g)
                    except StopIteration:
                        active.remove(g)

        def group(kh):
            wf, rwf = wfr.next()
            wz, rwz = wzr.next()
            for c in range(0, KC, 4):
                P.dma(wf[:, c:c + 4, 0:128], w_in[:, c:c + 4, kh * 128:(kh + 1) * 128], writes=[rwf], eng="pool")
                P.dma(wf[:, c:c + 4, 128:256], w_in[:, c:c + 4, 1024 + kh * 128:1024 + (kh + 1) * 128], writes=[rwf], eng="pool")
                P.dma(wf[:, c:c + 4, 256:512], w_in[:, c:c + 4, 2048 + kh * 256:2048 + (kh + 1) * 256], writes=[rwf], eng="pool")
                P.dma(wz[:, c:c + 4, :], w_in[:, c:c + 4, 4096 + kh * 256:4096 + (kh + 1) * 256], writes=[rwz], eng="pool")
            with contextlib.ExitStack() as stb:
                B = dict(pre=[Ring(P, "g_pre", [128, 515], F32, 2, stb) for _ in range(4)],
                         halo=[P.sbuf(f"g_halo{ch}", [128, 4], F32, stb) for ch in range(4)],
                         r_halo=[Res() for _ in range(4)],
                         cv=[Ring(P, "g_cv", [128, 512], F32, 2, stb) for _ in range(4)],
                         sq=[Ring(P, "g_sq", [128, 512], BF16, 2, stb) for _ in range(2)],
                         ln=[Ring(P, "g_ln", [128, 512], F32, 2, stb) for _ in range(2)])
                for tb in range(NB):
                    run_lanes([chunk_gen(kh, tb, ch, wf, rwf, B) for ch in range(4)])
            P.barrier()
            with contextlib.ExitStack() as stt:
                T3 = lambda nm, dt: (P.sbuf(nm, [128, 2, 128], dt, stt), Res())
                lanes = []
                NSLOT = NL + 2
                for ln_ in range(NL):
                    dgt = T3("g_dg", F32)
                    At = T3("g_A", BF16)
                    lanes.append(dict(bank=(0, 1, 2, 3, 7)[ln_], dg=dgt, dm=T3("g_dm", F32), dmt=dgt,
                                      A=At, M=T3("g_M", BF16),
                                      Mo=[T3("g_Mo", BF16), T3("g_Mo", BF16)],
                                      UV=[(P.sbuf("g_UV", [128, 2, 2, 128], BF16, stt), Res()) for _ in range(2)],
                                      Y=At))
                oslots = [dict((nm, T3("g_" + nm, BF16)) for nm in ("kbg", "kdec", "bv", "qkm", "TT", "nwT"))
                          for _ in range(NSLOT)]
                R3 = lambda nm, dt, n: Ring(P, nm, [128, 2, 128], dt, n, stt)
                RB = dict(vn=R3("g_vn", BF16, 2), zs=R3("g_zs", F32, 1), o32=R3("g_o32", F32, 4),
                          junk=R3("g_junk", F32, 1), ssq=Ring(P, "g_ssq", [128, 8], F32, 2, stt),
                          zz=Ring(P, "g_zz", [128, 256], F32, 2, stt), t1=R3("g_t1", F32, 1), ob=R3("g_ob", BF16, 2))
                oT4r = Ring(P, "g_oT4", [128, 2, 512], BF16, 2, stt)
                P.op("pool", lambda e: e.memset(S32[:], 0.0), writes=[r_S32])
                P.op("pool", lambda e: e.memset(Sb[:], 0.0), writes=[r_Sb])
                st4 = {}
                done = {}
                ptdone = set()
                rfin = set()
                ofin = set()

                def pt_worker(ln_):
                    for _ in range(ln_ * GDN_STAGGER):
                        yield
                    for i in range(ln_, NT, NL):
                        while i >= NSLOT and (i - NSLOT) not in rfin:
                            yield
                        yield from pt_stage(kh, i, lanes[ln_], oslots[i % NSLOT])
                        ptdone.add(i)

                def r_worker():
                    for i in range(NT):
                        while i not in ptdone or (i >= 3 and (i - 3) not in ofin):
                            yield
                        yield from r_stage(kh, i, oslots[i % NSLOT], RB, done)
                        rfin.add(i)

                def o_worker():
                    for i in range(NT):
                        while i not in done:
                            yield
                        if i % 4 == 0:
                            st4["o"] = oT4r.next()
                        oT4, roT4 = st4["o"]
                        o32, ro32 = done[i]
                        yield from o_stage(kh, i, o32, ro32, wz, rwz, RB, oT4, roT4)
                        ofin.add(i)

                run_lanes([r_worker(), o_worker()] + [pt_worker(ln_) for ln_ in range(NL)])
            P.barrier()

        for kh in range(8):
            group(kh)
    P.barrier()
    with contextlib.ExitStack() as st:
        otr = Ring(P, "g_oTt", [128, 16, 128], BF16, 3, st)
        cur = {}

        def pre_tile(i):
            t, r = otr.next()
            P.dma(t[:], oTd[:, :, i * 128:(i + 1) * 128], reads=[r_oTd[i // 4]], writes=[r])
            cur["t"], cur["r"] = t, r
            return t, r

        out_proj(k, k.gdn_w_out[j], 16, None, None, x_src, pre_tile=pre_tile, nn=nn)


def mla_layer(k, j, x_src):
    P = k.P
    rc = k.r_const
    NB = S // 512
    scale = 192.0 ** -0.5
    with contextlib.ExitStack() as st:
        rope_table(k, st)
        cqn = P.sbuf("cqn", [128, 3, S], BF16, st)
        ckvn = P.sbuf("ckvn", [128, 2, S], BF16, st)
        k2 = P.sbuf("k2", [128, S], BF16, st)
        r_cq = [Res() for _ in range(NB)]
        r_ckv = [Res() for _ in range(NB)]
        r_k2 = [Res() for _ in range(NB)]
        qg = k.gainT[:, 64 + 3 * j:64 + 3 * j + 3]
        kvg = k.gainT[:, 70 + 2 * j:70 + 2 * j + 2]
        with contextlib.ExitStack() as st1:
            win = P.sbuf("m_win", [128, KC, 768], BF16, st1)
            r_win = Res()
            w_in = k.mla_w_in[j].rearrange("(c p) n -> p c n", p=128)
            for c in range(0, KC, 2):
                P.dma(win[:, c:c + 2, 0:704], w_in[:, c:c + 2, :], writes=[r_win], eng="pool")
            vs = win[:, :, 640:704].rearrange("p c (i two) -> p c i two", two=2)
            vd = win[:, :, 704:768].rearrange("p c (i two) -> p c i two", two=2)
            P.op("act", lambda e: e.mul(out=vd[:, :, :, 0], in_=vs[:, :, :, 1], mul=-1.0), writes=[r_win])
            P.op("act", lambda e: e.copy(out=vd[:, :, :, 1], in_=vs[:, :, :, 0]), writes=[r_win])
            sqr = Ring(P, "m_sq", [128, 512], BF16, 4, st1)
            lnr = Ring(P, "m_ln", [128, 512], F32, 2, st1)
            rsr = Ring(P, "m_rs", [128, 512], F32, 2, st1)
            prr = Ring(P, "m_pr", [128, 512], BF16, 2, st1)
            allb = [0, 1, 2, 3, 4, 5, 6, 7]

            def latent(tb, nch, col0, dst, rdst, gcol, width):
                t0 = tb * 512
                rb = k.r_big[tb]
                pqs = []
                sqs = []
                for c3 in range(nch):
                    pq, rpq = bank(k, "m1", allb)
                    for c in range(KC):
                        P.op("pe", lambda e, pq=pq, c=c, c3=c3: e.matmul(
                            pq[:, :], lhsT=win[:, c, col0 + c3 * 128:col0 + (c3 + 1) * 128],
                            rhs=k.bigA[:, c, t0:t0 + 512], start=(c == 0), stop=(c == KC - 1)),
                            reads=[r_win, rb], writes=[rpq])
                    sq, rsq = sqr.next()
                    P.op("act", lambda e, sq=sq, pq=pq: e.activation(out=sq[:], in_=pq[:, :], func=AF.Square),
                         writes=[rpq, rsq])
                    pqs.append((pq, rpq))
                    sqs.append((sq, rsq))
                pss, rpss = bank(k, "m1", allb)
                for c3 in range(nch):
                    P.op("pe", lambda e, c3=c3, sq=sqs[c3][0]: e.matmul(
                        pss[:, :], lhsT=k.onesb[:], rhs=sq[:], start=(c3 == 0), stop=(c3 == nch - 1)),
                        reads=[sqs[c3][1], rc], writes=[rpss])
                ln, rln = lnr.next()
                rs, rrs = rsr.next()
                P.op("act", lambda e: e.activation(out=ln[:], in_=pss[:, :], func=AF.Ln, scale=1.0 / width,
                                                   bias=k.epsc[:, 0:1]), reads=[rc], writes=[rpss, rln])
                P.op("act", lambda e: e.activation(out=rs[:], in_=ln[:], func=AF.Exp, scale=-0.5),
                     reads=[rln], writes=[rrs])
                for c3 in range(nch):
                    pq, rpq = pqs[c3]
                    P.op("dve", lambda e, pq=pq, c3=c3: e.scalar_tensor_tensor(
                        out=dst[:, c3, t0:t0 + 512], in0=pq[:, :], scalar=gcol[:, c3:c3 + 1], in1=rs[:],
                        op0=ALU.mult, op1=ALU.mult), reads=[rrs, rc], writes=[rpq, rdst[tb]])

            def rope_key(tb):
                t0 = tb * 512
                rb = k.r_big[tb]
                pk, rpk = bank(k, "m1", allb)
                for c in range(KC):
                    P.op("pe", lambda e, c=c: e.matmul(
                        pk[:, :], lhsT=win[:, c, 640:768], rhs=k.bigA[:, c, t0:t0 + 512],
                        start=(c == 0), stop=(c == KC - 1)), reads=[r_win, rb], writes=[rpk])
                pr, rpr = prr.next()
                P.op("dve", lambda e: e.tensor_tensor(out=pr[:], in0=pk[:, :], in1=k.cs[:, t0:t0 + 512],
                                                      op=ALU.mult), reads=[rc], writes=[rpk, rpr])
                pf, rpf = bank(k, "m1", allb)
                P.op("pe", lambda e: e.matmul(pf[:, :], lhsT=k.fold[:], rhs=pr[:], start=True, stop=True),
                     reads=[rpr, rc], writes=[rpf])
                P.op("act", lambda e: e.copy(out=k2[:, t0:t0 + 512], in_=pf[:, :]), writes=[rpf, r_k2[tb]])

            for tb in range(NB):
                latent(tb, 3, 0, cqn, r_cq, qg, 384.0)
                latent(tb, 2, 384, ckvn, r_ckv, kvg, 256.0)
                rope_key(tb)
        dump(k, "cs", k.cs[:, 0:512], [rc], F32)
        dump(k, "cqn", cqn[:, :, 0:512], [r_cq[0]])
        dump(k, "ckvn", ckvn[:, :, 0:512], [r_ckv[0]])
        dump(k, "k2", k2[:, 0:512], [r_k2[0]])
        P.barrier()
        with contextlib.ExitStack() as st2:
            wqr = Ring(P, "m_wq", [128, 3, 256], BF16, 2, st2)
            wkvr = Ring(P, "m_wkv", [128, 2, 256], BF16, 2, st2)
            qn = P.sbuf("m_qn", [128, S], BF16, st2)
            q2 = P.sbuf("m_q2", [128, S], BF16, st2)
            kn = P.sbuf("m_kn", [128, S], BF16, st2)
            vv = P.sbuf("m_v", [128, NT, 128], BF16, st2)
            r_qn = [Res() for _ in range(NB)]
            r_q2 = [Res() for _ in range(NB)]
            r_kn = [Res() for _ in range(NB)]
            r_v = [Res() for _ in range(NB)]
            ptr = Ring(P, "m_pT", [128, 512], BF16, 5, st2)
            rir = Ring(P, "m_ri", [128, 512], F32, 2, st2)
            accr = Ring(P, "m_acc", [128, 512], F32, 4, st2)
            wqu = k.mla_w_q_up[j].rearrange("(c p) n -> p c n", p=128)
            wkvu = k.mla_w_kv_up[j].rearrange("(c p) n -> p c n", p=128)
            pb = [4, 5, 6, 7]

            def head_proj(h, tb, wq, rwq, wkv, rwkv):
                t0 = tb * 512
                p1, rp1 = bank(k, "mp", pb)
                for c3 in range(3):
                    P.op("pe", lambda e, c3=c3: e.matmul(
                        p1[:, :], lhsT=wq[:, c3, 0:128], rhs=cqn[:, c3, t0:t0 + 512], start=(c3 == 0), stop=(c3 == 2)),
                        reads=[rwq, r_cq[tb]], writes=[rp1])
                P.op("act", lambda e: e.copy(out=qn[:, t0:t0 + 512], in_=p1[:, :]), writes=[rp1, r_qn[tb]])
                p2, rp2 = bank(k, "mp", pb)
                for c3 in range(3):
                    P.op("pe", lambda e, c3=c3: e.matmul(
                        p2[:, :], lhsT=wq[:, c3, 128:256], rhs=cqn[:, c3, t0:t0 + 512], start=(c3 == 0), stop=(c3 == 2)),
                        reads=[rwq, r_cq[tb]], writes=[rp2])
                P.op("dve", lambda e: e.tensor_tensor(out=q2[:, t0:t0 + 512], in0=p2[:, :],
                                                      in1=k.cs[:, t0:t0 + 512], op=ALU.mult),
                     reads=[rc], writes=[rp2, r_q2[tb]])
                p3, rp3 = bank(k, "mp", pb)
                for c2 in range(2):
                    P.op("pe", lambda e, c2=c2: e.matmul(
                        p3[:, :], lhsT=wkv[:, c2, 0:128], rhs=ckvn[:, c2, t0:t0 + 512], start=(c2 == 0), stop=(c2 == 1)),
                        reads=[rwkv, r_ckv[tb]], writes=[rp3])
                P.op("act", lambda e: e.copy(out=kn[:, t0:t0 + 512], in_=p3[:, :]), writes=[rp3, r_kn[tb]])
                p4, rp4 = bank(k, "mp", pb)
                for tt in range(4):
                    for c2 in range(2):
                        P.op("pe", lambda e, c2=c2, tt=tt: e.matmul(
                            p4[:, tt * 128:(tt + 1) * 128], lhsT=ckvn[:, c2, t0 + tt * 128:t0 + (tt + 1) * 128],
                            rhs=wkv[:, c2, 128:256], start=(c2 == 0), stop=(c2 == 1)),
                            reads=[rwkv, r_ckv[tb]], writes=[rp4])
                P.op("dve", lambda e: e.tensor_copy(
                    out=vv[:, tb * 4:(tb + 1) * 4, :].rearrange("p a b -> p (a b)"), in_=p4[:, :]),
                    writes=[rp4, r_v[tb]])

            def attn_qk(h, qb, kt, nkt, accs):
                c0 = max(0, kt * 128 - qb * 512)
                w = 512 - c0
                q0 = qb * 512 + c0
                pS, rpS = bank(k, "mp", pb)
                kb = kt // 4
                P.op("pe", lambda e: e.matmul(
                    pS[:, c0:512], lhsT=kn[:, kt * 128:(kt + 1) * 128], rhs=qn[:, q0:q0 + w], start=True, stop=False),
                    reads=[r_kn[kb], r_qn[qb]], writes=[rpS])
                P.op("pe", lambda e: e.matmul(
                    pS[:, c0:512], lhsT=k2[:, kt * 128:(kt + 1) * 128], rhs=q2[:, q0:q0 + w], start=False, stop=True),
                    reads=[r_k2[kb], r_q2[qb]], writes=[rpS])
                pT, rpT = ptr.next()
                P.op("act", lambda e: e.activation(out=pT[:, c0:512], in_=pS[:, c0:512], func=AF.Exp, scale=scale),
                     writes=[rpS, rpT])
                if kt >= 4 * qb:
                    P.op("pool", lambda e: e.memset(pT[64:128, c0:c0 + 64], 0.0), writes=[rpT])
                acc, racc = accs[kt % 2]
                eng = ("pool", "dve")[kt % 2]
                if kt < 2:
                    if c0 > 0:
                        P.op(eng, lambda e: e.memset(acc[:, 0:c0], 0.0), writes=[racc])
                    P.op(eng, lambda e: e.tensor_copy(out=acc[:, c0:512], in_=pT[:, c0:512]), reads=[rpT], writes=[racc])
                else:
                    P.op(eng, lambda e: e.tensor_tensor(out=acc[:, c0:512], in0=acc[:, c0:512], in1=pT[:, c0:512], op=ALU.add),
                         reads=[rpT], writes=[racc])
                return pT, rpT, c0

            def attn_pv(kt, nkt, po, rpo, pT, rpT, c0):
                kb = kt // 4
                P.op("pe", lambda e: e.matmul(
                    po[:, c0:512], lhsT=vv[:, kt, :], rhs=pT[:, c0:512], start=(kt == 0), stop=(kt == nkt - 1)),
                    reads=[r_v[kb], rpT], writes=[rpo])

            def attn_block(h, qb):
                po, rpo = bank(k, "ao", [0, 1])
                prs, rprs = bank(k, "ar", [2, 3])
                nkt = 4 * qb + 4
                accs = [accr.next(), accr.next()]
                LA = 2
                pend = {}
                for idx in range(nkt + LA):
                    if idx < nkt:
                        pend[idx] = attn_qk(h, qb, idx, nkt, accs)
                    if idx >= LA:
                        attn_pv(idx - LA, nkt, po, rpo, *pend.pop(idx - LA))
                P.op("dve", lambda e: e.tensor_tensor(out=accs[0][0][:], in0=accs[0][0][:], in1=accs[1][0][:], op=ALU.add),
                     reads=[accs[1][1]], writes=[accs[0][1]])
                P.op("pe", lambda e: e.matmul(prs[:, :], lhsT=k.onesf[:], rhs=accs[0][0][:], start=True, stop=True),
                     reads=[rc, accs[0][1]], writes=[rprs])
                ri, rri = rir.next()
                P.op("dve", lambda e: e.reciprocal(out=ri[:], in_=prs[:, :]), writes=[rprs, rri])
                P.op("dve", lambda e: e.tensor_tensor(
                    out=k.bigA[:, h, qb * 512:(qb + 1) * 512], in0=po[:, :], in1=ri[:], op=ALU.mult),
                    reads=[rri], writes=[rpo, k.r_big[qb]])

            def head(h):
                wq, rwq = wqr.next()
                wkv, rwkv = wkvr.next()
                P.dma(wq[:, :, 0:192], wqu[:, :, h * 192:(h + 1) * 192], writes=[rwq], eng="pool")
                P.dma(wkv[:, :, :], wkvu[:, :, h * 256:(h + 1) * 256], writes=[rwkv], eng="pool")
                vs = wq[:, :, 128:192].rearrange("p c (i two) -> p c i two", two=2)
                vd = wq[:, :, 192:256].rearrange("p c (i two) -> p c i two", two=2)
                P.op("act", lambda e: e.mul(out=vd[:, :, :, 0], in_=vs[:, :, :, 1], mul=-1.0), writes=[rwq])
                P.op("act", lambda e: e.copy(out=vd[:, :, :, 1], in_=vs[:, :, :, 0]), writes=[rwq])
                for tb in range(NB):
                    head_proj(h, tb, wq, rwq, wkv, rwkv)
                if h == 0:
                    dump(k, "qn", qn[:, 0:512], [r_qn[0]])
                    dump(k, "q2", q2[:, 0:512], [r_q2[0]])
                    dump(k, "kn", kn[:, 0:512], [r_kn[0]])
                    dump(k, "vv", vv[:, 0:4, :], [r_v[0]])
                for qb in range(NB):
                    attn_block(h, qb)

            for h in range(8):
                head(h)
        dump(k, "oT", k.bigA[:, :, 0:1024], [k.r_big[0], k.r_big[1]])
        P.barrier()
        out_proj(k, k.mla_w_out[j], 8, lambda h, i: k.bigA[:, h, i * 128:(i + 1) * 128],
                 lambda i: [k.r_big[i // 4]], x_src)


def out_proj(k, w_dram, nch, lhs_fn, lhs_res_fn, x_src, pre_tile=None, nn=None):
    P = k.P
    with contextlib.ExitStack() as st:
        wo = P.sbuf("wo", [128, nch, D], BF16, st)
        r_wo = Res()
        wv = w_dram.rearrange("(c p) n -> p c n", p=128)
        for c in range(0, nch, 2):
            P.dma(wo[:, c:c + 2, :], wv[:, c:c + 2, :], writes=[r_wo], eng="pool")
        xr = Ring(P, "ox", [128, D], F32, 3, st)
        NR = setup_next_norm(k, nn, st)

        def tile(i):
            if pre_tile is not None:
                lt, lr = pre_tile(i)
                lhs = lambda c: lt[:, c, :]
                lres = [lr]
            else:
                lhs = lambda c: lhs_fn(c, i)
                lres = lhs_res_fn(i)
            xt, rx = xr.next()
            P.dma(xt[:], x_src[i * 128:(i + 1) * 128, :], reads=k.xr(x_src, i), writes=[rx])
            for dh in range(2):
                po, rpo = bank(k, "op", [0, 1, 2, 3])
                for c in range(nch):
                    P.op("pe", lambda e, po=po, c=c, dh=dh: e.matmul(
                        po[:, :], lhsT=lhs(c), rhs=wo[:, c, dh * 512:(dh + 1) * 512],
                        start=(c == 0), stop=(c == nch - 1)), reads=[r_wo] + lres, writes=[rpo])
                P.op("dve", lambda e, po=po, dh=dh: e.tensor_tensor(
                    out=xt[:, dh * 512:(dh + 1) * 512], in0=po[:, :], in1=xt[:, dh * 512:(dh + 1) * 512], op=ALU.add),
                    writes=[rpo, rx])
            P.dma(k.xres[i * 128:(i + 1) * 128, :], xt[:], reads=[rx], writes=[k.r_xres[i]], eng="act")
            if nn is not None:
                norm_tile(k, xt, rx, i, nn, NR, [4, 5])

        for i in range(NT):
            tile(i)


def _invf_table():
    inv_freq = (10000.0 ** (-np.arange(0, 64, 2, dtype=np.float32) / 64.0)).astype(np.float32)
    t = np.zeros((128, 2), np.float32)
    for p in range(128):
        t[p, 0] = inv_freq[(p % 64) // 2] / (2.0 * math.pi)
        t[p, 1] = 0.25 if p < 64 else 0.0
    return t


def _gdn_masks():
    i = np.arange(128)[:, None]
    j = np.arange(128)[None, :]
    m = np.zeros((128, 8, 128), np.float32)
    m0 = ((i // 2 == j // 2) & (i > j)).astype(np.float32)
    m[:, 0, :] = m0
    m[:, 1, :] = m0.T
    for lv, b in enumerate((2, 4, 8, 16, 32, 64)):
        ma = ((i // (2 * b) == j // (2 * b)) & ((i // b) % 2 == 1) & ((j // b) % 2 == 0)).astype(np.float32)
        m[:, 2 + lv, :] = ma.T
    return m


_NC_CACHE = {}


def kernel(**inputs):
    key = "full"
    if key not in _NC_CACHE:
        _NC_CACHE[key] = build()
    nc = _NC_CACHE[key]
    n = 8
    shared = {}
    for name, v in inputs.items():
        if name in ("x", "positions"):
            continue
        a = np.ascontiguousarray(np.asarray(v))
        if name == "final_norm":
            a = a.reshape(1, D)
        shared[name] = a
    shared["invf"] = _invf_table()
    shared["gmask"] = _gdn_masks()
    x = np.asarray(inputs["x"])
    pos = np.asarray(inputs["positions"])
    in_maps = []
    for i in range(n):
        m = dict(shared)
        m["x"] = np.ascontiguousarray(x[i])
        m["positions"] = np.ascontiguousarray(pos[i].reshape(1, S).astype(np.int32))
        in_maps.append(m)
    res = run_bass_kernel_spmd(nc, in_maps, core_ids=list(range(n)))
    _NC_CACHE["last"] = res
    return np.stack([np.asarray(res.results[i]["out"]) for i in range(n)], axis=0).astype(np.float32)
```
